# Optimizing a Trainium2 kernel written in Bass

```python
import functools
import jax, jax.numpy as jnp
from jax import lax
import numpy as np

D_MODEL = 2048
BATCH = 8
SEQ = 4096
DEPTH = 1
DEC_BATCH = 8
DEC_SEQ = 64
PAST_LEN = 4096

CHUNK = 64
D_HEAD = 128
H_FOX = 8
H_BAND = 8
FOX_WIDTH = H_FOX * D_HEAD
BAND_WIDTH = H_BAND * D_HEAD
MIX_WIDTH = FOX_WIDTH + BAND_WIDTH
N_BAND_PAST = 8
BAND_ROWS = N_BAND_PAST * CHUNK
MAX_REL = 128
D_FF = 4 * D_MODEL
Q_BLOCK = 128
ALPHA = (2.0 * DEPTH) ** 0.25
BETA = (8.0 * DEPTH) ** -0.25
LN_EPS = 1e-5
NEG = -1e30
IN_COLS = 3 * FOX_WIDTH + H_FOX + 3 * BAND_WIDTH
SPLIT_AT = np.cumsum([FOX_WIDTH, FOX_WIDTH, FOX_WIDTH, H_FOX, BAND_WIDTH, BAND_WIDTH]).tolist()

kernel_name = "fox_chunkband_hybrid_stream_step"


def _norm(x):
    xf = x.astype(jnp.float32)
    mu = jnp.mean(xf, axis=-1, keepdims=True)
    var = jnp.mean(jnp.square(xf - mu), axis=-1, keepdims=True)
    return (xf - mu) * lax.rsqrt(var + LN_EPS)


def layer_norm(x, g, b):
    y = _norm(x) * g.astype(jnp.float32) + b.astype(jnp.float32)
    return y.astype(x.dtype)


def split_proj(u, w_in, b_forget):
    B_, S_ = u.shape[:2]
    proj = jnp.einsum('bsd,de->bse', u, w_in)
    q_a, k_a, v_a, f_a, q_b, k_b, v_b = jnp.split(proj, SPLIT_AT, axis=-1)
    heads = lambda t, h: t.reshape(B_, S_, h, D_HEAD)
    logf = jax.nn.log_sigmoid((f_a + b_forget).astype(jnp.float32))
    return (heads(q_a, H_FOX), heads(k_a, H_FOX), heads(v_a, H_FOX), logf,
            heads(q_b, H_BAND), heads(k_b, H_BAND), heads(v_b, H_BAND))


def fox_attend(q, k, v, f_q, f_k, q_pos, k_pos):
    s = jnp.einsum('bqhd,bkhd->bhqk', q, k).astype(jnp.float32) * (D_HEAD ** -0.5)
    s = s + jnp.swapaxes(f_q, 1, 2)[..., :, None] - jnp.swapaxes(f_k, 1, 2)[..., None, :]
    s = jnp.where(k_pos[None, :] <= q_pos[:, None], s, NEG)
    p = jax.nn.softmax(s, axis=-1).astype(v.dtype)
    return jnp.einsum('bhqk,bkhd->bqhd', p, v)


def fox_prompt(q, k, v, logf):
    B_, S_ = q.shape[:2]
    F = jnp.cumsum(logf, axis=1)
    pos = jnp.arange(S_)
    nqb = S_ // Q_BLOCK
    qb = jnp.swapaxes(q.reshape(B_, nqb, Q_BLOCK, H_FOX, D_HEAD), 0, 1)
    fb = jnp.swapaxes(F.reshape(B_, nqb, Q_BLOCK, H_FOX), 0, 1)
    pb = pos.reshape(nqb, Q_BLOCK)
    out = lax.map(lambda a: fox_attend(a[0], k, v, a[1], F, a[2], pos), (qb, fb, pb))
    return jnp.swapaxes(out, 0, 1).reshape(B_, S_, FOX_WIDTH)


def fox_sample(q, k, v, logf, k_c, v_c, logf_c):
    B_, T = q.shape[:2]
    P = k_c.shape[1]
    F = jnp.cumsum(jnp.concatenate([logf_c.astype(jnp.float32), logf], axis=1), axis=1)
    k_all = jnp.concatenate([k_c.astype(k.dtype), k], axis=1)
    v_all = jnp.concatenate([v_c.astype(v.dtype), v], axis=1)
    pos = jnp.arange(P + T)
    out = fox_attend(q, k_all, v_all, F[:, P:], F, pos[P:], pos)
    return out.reshape(B_, T, FOX_WIDTH)


def rel_bias_matrix(table, rel):
    return table[:, jnp.clip(rel, -MAX_REL, MAX_REL) + MAX_REL].astype(jnp.float32)


def band_prompt(q, k, v, table):
    B_, S_ = q.shape[:2]
    L = CHUNK
    NC = S_ // L
    K_LEN = (N_BAND_PAST + 1) * L
    qc = q.reshape(B_, NC, L, H_BAND, D_HEAD)
    pad = ((0, 0), (N_BAND_PAST, 0), (0, 0), (0, 0), (0, 0))
    kp = jnp.pad(k.reshape(B_, NC, L, H_BAND, D_HEAD), pad)
    vp = jnp.pad(v.reshape(B_, NC, L, H_BAND, D_HEAD), pad)
    offs = list(range(N_BAND_PAST, -1, -1))
    s = jnp.concatenate(
        [jnp.einsum('bnqhd,bnkhd->bnhqk', qc, kp[:, N_BAND_PAST - o:N_BAND_PAST - o + NC]) for o in offs],
        axis=-1).astype(jnp.float32) * (D_HEAD ** -0.5)
    kk = jnp.arange(K_LEN)
    rel = kk[None, :] - N_BAND_PAST * L - jnp.arange(L)[:, None]
    s = s + rel_bias_matrix(table, rel)
    valid = (jnp.arange(NC)[:, None] * L + kk[None, :] - N_BAND_PAST * L) >= 0
    s = jnp.where(valid[:, None, None, :], s, NEG)
    p = jax.nn.softmax(s, axis=-1).astype(v.dtype)
    p_parts = jnp.split(p, N_BAND_PAST + 1, axis=-1)
    out = sum(jnp.einsum('bnhqk,bnkhd->bnqhd', pp, vp[:, N_BAND_PAST - o:N_BAND_PAST - o + NC])
              for pp, o in zip(p_parts, offs))
    return out.reshape(B_, S_, BAND_WIDTH)


def band_sample(q, k, v, k_c, v_c, table):
    B_, T = q.shape[:2]
    W = k_c.shape[1]
    k_all = jnp.concatenate([k_c.astype(k.dtype), k], axis=1)
    v_all = jnp.concatenate([v_c.astype(v.dtype), v], axis=1)
    s = jnp.einsum('bqhd,bkhd->bhqk', q, k_all).astype(jnp.float32) * (D_HEAD ** -0.5)
    rel = jnp.arange(W + T)[None, :] - W - jnp.arange(T)[:, None]
    s = s + rel_bias_matrix(table, rel)
    p = jax.nn.softmax(s, axis=-1).astype(v.dtype)
    return jnp.einsum('bhqk,bkhd->bqhd', p, v_all).reshape(B_, T, BAND_WIDTH)


def prompt_mixer(table, q_a, k_a, v_a, logf, q_b, k_b, v_b):
    W = min(BAND_ROWS, q_b.shape[1])
    mix = jnp.concatenate([fox_prompt(q_a, k_a, v_a, logf), band_prompt(q_b, k_b, v_b, table)], axis=-1)
    return mix, (k_a, v_a, logf, k_b[:, -W:], v_b[:, -W:])


def sample_mixer(table, fk_c, fv_c, flogf_c, bk_c, bv_c, q_a, k_a, v_a, logf, q_b, k_b, v_b):
    mix = jnp.concatenate([fox_sample(q_a, k_a, v_a, logf, fk_c, fv_c, flogf_c),
                           band_sample(q_b, k_b, v_b, bk_c, bv_c, table)], axis=-1)
    return mix, (k_a, v_a, logf, k_b, v_b)


def trunk_layer(x, c, mixer, w_ada, b_ada, w_in, b_forget, w_out, ln_mix_g, ln_mix_b,
                w_up, b_up, w_down, ln_mlp_g, ln_mlp_b):
    ada = jnp.einsum('bd,de->be', jax.nn.silu(c), w_ada) + b_ada
    sh_a, sc_a, g_a, sh_m, sc_m, g_m = [t[:, None, :] for t in jnp.split(ada, 6, axis=-1)]
    u = (_norm(x) * (1.0 + sc_a) + sh_a).astype(x.dtype)
    mix, states = mixer(*split_proj(u, w_in, b_forget))
    x = layer_norm(ALPHA * x + g_a * jnp.einsum('bse,ed->bsd', mix, w_out), ln_mix_g, ln_mix_b)
    u = (_norm(x) * (1.0 + sc_m) + sh_m).astype(x.dtype)
    h = jnp.square(jax.nn.relu(jnp.einsum('bsd,df->bsf', u, w_up) + b_up))
    x = layer_norm(ALPHA * x + g_m * jnp.einsum('bsf,fd->bsd', h, w_down), ln_mlp_g, ln_mlp_b)
    return x, states


def setup_inputs(seed: int = 0) -> dict:
    key = jax.random.key(seed)
    ks = jax.random.split(key, 24)
    f32 = jnp.float32
    nrm = lambda k, shape, s=1.0: jax.random.normal(k, shape, f32) * s
    W = min(BAND_ROWS, PAST_LEN)
    return {
        "x_prompt": nrm(ks[0], (BATCH, SEQ, D_MODEL)),
        "x_sample": nrm(ks[1], (DEC_BATCH, DEC_SEQ, D_MODEL)),
        "cache_fox_k": nrm(ks[2], (DEPTH, DEC_BATCH, PAST_LEN, H_FOX, D_HEAD)),
        "cache_fox_v": nrm(ks[3], (DEPTH, DEC_BATCH, PAST_LEN, H_FOX, D_HEAD)),
        "cache_fox_logf": jax.nn.log_sigmoid(nrm(ks[4], (DEPTH, DEC_BATCH, PAST_LEN, H_FOX)) + 3.0),
        "cache_band_k": nrm(ks[5], (DEPTH, DEC_BATCH, W, H_BAND, D_HEAD)),
        "cache_band_v": nrm(ks[6], (DEPTH, DEC_BATCH, W, H_BAND, D_HEAD)),
        "c_prompt": nrm(ks[7], (BATCH, D_MODEL)),
        "c_sample": nrm(ks[8], (DEC_BATCH, D_MODEL)),
        "w_ada": nrm(ks[9], (DEPTH, D_MODEL, 6 * D_MODEL), 0.5 * D_MODEL ** -0.5),
        "b_ada": nrm(ks[10], (DEPTH, 6 * D_MODEL), 0.01),
        "w_in": nrm(ks[11], (DEPTH, D_MODEL, IN_COLS), D_MODEL ** -0.5),
        "b_forget": jnp.linspace(1.0, 5.0, H_FOX, dtype=f32)[None, :] + nrm(ks[12], (DEPTH, H_FOX), 0.1),
        "rel_bias": nrm(ks[13], (DEPTH, H_BAND, 2 * MAX_REL + 1), 0.2),
        "w_out": nrm(ks[14], (DEPTH, MIX_WIDTH, D_MODEL), BETA * MIX_WIDTH ** -0.5),
        "ln_mix_g": 1.0 + nrm(ks[15], (DEPTH, D_MODEL), 0.02),
        "ln_mix_b": nrm(ks[16], (DEPTH, D_MODEL), 0.02),
        "w_up": nrm(ks[17], (DEPTH, D_MODEL, D_FF), D_MODEL ** -0.5),
        "b_up": nrm(ks[18], (DEPTH, D_FF), 0.01),
        "w_down": nrm(ks[19], (DEPTH, D_FF, D_MODEL), BETA * D_FF ** -0.5),
        "ln_mlp_g": 1.0 + nrm(ks[20], (DEPTH, D_MODEL), 0.02),
        "ln_mlp_b": nrm(ks[21], (DEPTH, D_MODEL), 0.02),
    }


def reference(x_prompt, x_sample, cache_fox_k, cache_fox_v, cache_fox_logf, cache_band_k, cache_band_v,
              c_prompt, c_sample, w_ada, b_ada, w_in, b_forget, rel_bias, w_out, ln_mix_g, ln_mix_b,
              w_up, b_up, w_down, ln_mlp_g, ln_mlp_b):
    xp, xs = x_prompt, x_sample
    p_states, s_states = [], []
    for l in range(DEPTH):
        wl = (w_ada[l], b_ada[l], w_in[l], b_forget[l], w_out[l], ln_mix_g[l], ln_mix_b[l],
              w_up[l], b_up[l], w_down[l], ln_mlp_g[l], ln_mlp_b[l])
        xp, sp = trunk_layer(xp, c_prompt, functools.partial(prompt_mixer, rel_bias[l]), *wl)
        xs, ss = trunk_layer(xs, c_sample,
                             functools.partial(sample_mixer, rel_bias[l], cache_fox_k[l], cache_fox_v[l],
                                               cache_fox_logf[l], cache_band_k[l], cache_band_v[l]), *wl)
        p_states.append(sp)
        s_states.append(ss)
    stk = lambda lst, i: jnp.stack([st[i] for st in lst], axis=0)
    return (xp, xs,
            stk(p_states, 0), stk(p_states, 1), stk(p_states, 2), stk(p_states, 3), stk(p_states, 4),
            stk(s_states, 0), stk(s_states, 1), stk(s_states, 2), stk(s_states, 3), stk(s_states, 4))
```

```python
from contextlib import ExitStack

import numpy as np
import concourse.bass as bass
import concourse.mybir as mybir
from concourse.bass_utils import run_bass_kernel_spmd

F32 = mybir.dt.float32
BF16 = mybir.dt.bfloat16
AF = mybir.ActivationFunctionType
ALU = mybir.AluOpType
AX = mybir.AxisListType

D = 2048
DH = 128
HF = 8
HB = 8
NH = HF + HB
DFF = 8192
IN_COLS = 6152
KC = D // 128
ALPHA = 2.0 ** 0.25
LN_EPS = 1e-5
SCALE = DH ** -0.5
ENGS = ("pe", "act", "dve", "pool", "sp")
BLK = {"pe": "tensor", "act": "scalar", "dve": "vector", "pool": "gpsimd", "sp": "sync"}


class Op:
    __slots__ = ("eng", "fn", "chan", "deps", "sig", "ticket", "cval", "gv")

    def __init__(self, eng, fn, chan):
        self.eng = eng
        self.fn = fn
        self.chan = chan
        self.deps = ()
        self.sig = False
        self.ticket = 0
        self.cval = 0
        self.gv = None


class Prog:
    def __init__(self, nc):
        self.nc = nc
        self.ops = {e: [] for e in ENGS}
        self.wr = {}
        self.rd = {}
        self.chan_last = {}
        self.chan_cnt = {}
        self.bar = []
        self.last = {}

    def add(self, eng, fn, reads=(), writes=(), chan=None, group=False):
        op = Op(eng, fn, chan)
        deps = {}
        for k in reads:
            w = self.wr.get(k)
            if w is not None:
                deps[id(w)] = w
        for k in writes:
            w = self.wr.get(k)
            if w is not None:
                deps[id(w)] = w
            for r in self.rd.get(k, {}).values():
                deps[id(r)] = r
        for b in self.bar:
            deps[id(b)] = b
        if chan is not None:
            prev = self.chan_last.get(chan)
            if prev is not None and not group:
                deps[id(prev)] = prev
            c = self.chan_cnt.get(chan, 0) + 1
            self.chan_cnt[chan] = c
            op.cval = 16 * c
            if group and prev is not None:
                op.gv = prev.gv
                op.gv[0] = op.cval
            else:
                op.gv = [op.cval]
            self.chan_last[chan] = op
        dl = []
        for d in deps.values():
            if d is op:
                continue
            if d.chan is None and chan is None and d.eng == "pe" and eng == "pe":
                continue
            d.sig = True
            dl.append(d)
        op.deps = dl
        who = chan if chan is not None else eng
        for k in reads:
            self.rd.setdefault(k, {})[who] = op
        for k in writes:
            self.wr[k] = op
            self.rd[k] = {}
        self.ops[eng].append(op)
        self.last[who] = op
        return op

    def barrier(self):
        self.bar = list(self.last.values())
        self.wr = {}
        self.rd = {}

    def emit(self):
        nc = self.nc
        for e in ENGS:
            t = 0
            for op in self.ops[e]:
                if op.chan is None and op.sig:
                    t += 1
                    op.ticket = t
        with ExitStack() as es:
            esem = {e: es.enter_context(nc.semaphore("s_" + e)) for e in ENGS}
            csem = {c: es.enter_context(nc.semaphore("c_%d" % i))
                    for i, c in enumerate(self.chan_cnt)}
            block = es.enter_context(nc.Block())
            for e in ENGS:
                ops = self.ops[e]

                def body(eng, ops=ops, e=e):
                    waited = {}
                    for op in ops:
                        for d in op.deps:
                            if d.chan is not None:
                                sem, val = csem[d.chan], d.gv[0]
                            else:
                                sem, val = esem[d.eng], d.ticket
                            if waited.get(sem.num, 0) < val:
                                eng.wait_ge(sem, val)
                                waited[sem.num] = val
                        ins = op.fn(eng)
                        if ins is None:
                            continue
                        if op.chan is not None:
                            ins.then_inc(csem[op.chan], 16)
                        elif op.sig:
                            ins.then_inc(esem[e], 1)

                getattr(block, BLK[e])(body)


def _split_cols(n, step):
    return [(c, min(step, n - c)) for c in range(0, n, step)]


def build(S, PAST, with_sample=True, debug=False, phases="ABCD"):
    assert S % 512 == 0
    nc = bass.Bass("TRN2", target_bir_lowering=False)
    P = Prog(nc)
    NT = S // 128

    def din(name, shape, dt=F32):
        return nc.dram_tensor(name, list(shape), dt, kind="ExternalInput").ap()

    def dout(name, shape, dt=F32):
        return nc.dram_tensor(name, list(shape), dt, kind="ExternalOutput").ap()

    def dscr(name, shape, dt):
        return nc.dram_tensor(name, list(shape), dt, kind="ExternalOutput" if debug else "Internal").ap()

    x_d = din("x_p", [S, D])
    c2_d = din("c2", [2, D])
    w_ada_d = din("w_ada", [D, 6 * D])
    b_ada_d = din("b_ada", [1, 6 * D])
    w_in_d = din("w_in", [D, IN_COLS])
    b_f_d = din("b_forget", [1, HF])
    ident_d = din("ident", [128, 128])
    tri_d = din("tri", [128, 128])
    sel_d = din("sel64", [128, 128])
    jx_d = din("jx", [128, 128])
    mka_d = din("mka", [128, 128])
    mkb_d = din("mkb", [128, 128])
    g_d = din("gtab", [HB, 768])
    w_out_d = din("w_out", [D, D])
    w_up_d = din("w_up", [D, DFF])
    b_up_d = din("b_up", [1, DFF])
    w_dn_d = din("w_down", [DFF, D])
    lnv_d = din("lnv", [4, D])
    y_o = dout("y_p", [S, D])
    NP = PAST // 128
    NS = 64
    xs_d = din("x_s", [NS, D])
    cfk_d = din("cfk", [PAST, HF * DH])
    cfv_d = din("cfv", [PAST, HF * DH])
    cfl_d = din("cfl", [PAST, HF])
    cbk_d = din("cbk", [512, HB * DH])
    cbv_d = din("cbv", [512, HB * DH])
    ys_o = dout("y_s", [NS, D])
    fks_o = dout("fox_k_s", [NS, HF * DH])
    fvs_o = dout("fox_v_s", [NS, HF * DH])
    fls_o = dout("fox_logf_s", [NS, HF])
    bks_o = dout("band_k_s", [NS, HB * DH])
    bvs_o = dout("band_v_s", [NS, HB * DH])

    fk_o = dout("fox_k_p", [S, HF * DH])
    fv_o = dout("fox_v_p", [S, HF * DH])
    fl_o = dout("fox_logf_p", [S, HF])
    WB = min(512, S)
    bk_o = dout("band_k_p", [WB, HB * DH])
    bv_o = dout("band_v_p", [WB, HB * DH])

    qT_s = dscr("qT_s", [NH, 128, S], BF16)
    kT_s = dscr("kT_s", [NH, 128, S], BF16)
    v_s = dscr("v_s", [NH, 128, NT, DH], BF16)
    ada_s = dscr("ada_s", [2, 6 * D], F32)
    mixT_s = dscr("mixT_s", [D, S], BF16)
    u2T_s = dscr("u2T_s", [D, S], BF16)
    x1_s = dscr("x1_s", [S, D], F32)
    wup_s = dscr("wup_s", [D, DFF], BF16)
    wdn_s = dscr("wdn_s", [DFF, D], BF16)
    qTs_s = dscr("qTs_s", [NH, 128, 128], BF16)
    kTfs_s = dscr("kTfs_s", [HF, 128, (NP + 1) * 128], BF16)
    kTbs_s = dscr("kTbs_s", [HB, 128, 5 * 128], BF16)
    vfs_s = dscr("vfs_s", [HF, 128, NP + 1, DH], BF16)
    vbs_s = dscr("vbs_s", [HB, 128, 5, DH], BF16)
    mixTs_s = dscr("mixTs_s", [D, 128], BF16)
    u2Ts_s = dscr("u2Ts_s", [D, 128], BF16)
    x1s_s = dscr("x1s_s", [128, D], F32)

    es = ExitStack()
    with es:
        def sb(name, shape, dt):
            return es.enter_context(nc.sbuf_tensor(name, list(shape), dt))

        ident = sb("ident_sb", [128, 128], F32)
        identb = sb("identb", [128, 128], BF16)
        tri = sb("tri_sb", [128, 128], F32)
        onesf = sb("onesf", [128, 128], F32)
        sel64 = sb("sel64_sb", [128, 128], F32)
        adaT = sb("adaT", [128, 96, 2], F32)
        scp1 = sb("scp1", [128, 2, KC, 2], F32)
        bfor = sb("bfor", [128, HF], F32)
        Fall = sb("Fall", [128, NT, HF], F32)
        Rall = sb("Rall", [128, NT, HF], F32)
        carry = sb("carry", [128, HF], F32)
        carryS = sb("carryS", [128, HF], F32)
        FallS = sb("FallS", [128, NP + 1, HF], F32)
        RallS = sb("RallS", [128, NP + 1, HF], F32)

        P.add("sp", lambda e: e.dma_start(out=ident[:, :], in_=ident_d[:, :]), writes=["c0"], chan="const")
        P.add("sp", lambda e: e.dma_start(out=tri[:, :], in_=tri_d[:, :]), writes=["c1"], chan="const", group=True)
        P.add("sp", lambda e: e.dma_start(out=sel64[:, :], in_=sel_d[:, :]), writes=["c2"], chan="const", group=True)
        P.add("sp", lambda e: e.dma_start(out=bfor[:, :], in_=b_f_d[0:1, :].to_broadcast([128, HF])),
              writes=["c3"], chan="const", group=True)
        P.add("dve", lambda e: e.tensor_copy(out=identb[:, :], in_=ident[:, :]), reads=["c0"], writes=["c4"])
        P.add("dve", lambda e: e.memset(onesf[:, :], 1.0), writes=["c5"])
        P.add("dve", lambda e: e.memset(carry[:, :], 0.0), writes=["carry"])
        trib = sb("trib", [128, 128], BF16)
        jx = sb("jx_sb", [128, 128], F32)
        mka = sb("mka_sb", [128, 128], F32)
        mkb = sb("mkb_sb", [128, 128], F32)
        P.add("sp", lambda e: e.dma_start(out=jx[:, :], in_=jx_d[:, :]), writes=["c6"], chan="const", group=True)
        P.add("sp", lambda e: e.dma_start(out=mka[:, :], in_=mka_d[:, :]), writes=["c7"], chan="const", group=True)
        P.add("sp", lambda e: e.dma_start(out=mkb[:, :], in_=mkb_d[:, :]), writes=["c8"], chan="const", group=True)
        P.add("dve", lambda e: e.tensor_copy(out=trib[:, :], in_=tri[:, :]), reads=["c1"], writes=["c9"])
        onesb = sb("onesb", [128, 128], BF16)
        P.add("dve", lambda e: e.memset(onesb[:, :], 1.0), writes=["c10"])

        def s0_logf(esS):
            clf = esS.enter_context(nc.sbuf_tensor("clf", [128, NP, HF], F32))
            ps_f = esS.enter_context(nc.psum_tensor("psS_f", [128, 512], F32))
            P.add("dve", lambda e: e.memset(carryS[:, :], 0.0), writes=["carryS"])
            P.add("sp", lambda e: e.dma_start(out=clf[:, :, :], in_=cfl_d[:, :].rearrange("(t p) h -> p t h", p=128)),
                  writes=["clf"], chan="clf")
            for t in range(NP):
                P.add("pe", lambda e, t=t: e.matmul(ps_f[:, 64:72], lhsT=tri[:, :], rhs=clf[:, t, :], start=True, stop=True),
                      reads=["clf"], writes=["psS_f"])
                P.add("pe", lambda e, t=t: e.matmul(ps_f[:, 80:88], lhsT=onesf[:, :], rhs=clf[:, t, :], start=True, stop=True),
                      reads=["clf"], writes=["psS_f"])
                P.add("dve", lambda e, t=t: e.tensor_add(out=FallS[:, t, :], in0=ps_f[:, 64:72], in1=carryS[:, :]),
                      reads=["psS_f", "carryS"], writes=["psS_f", ("FallS", t)])
                P.add("dve", lambda e: e.tensor_add(out=carryS[:, :], in0=ps_f[:, 80:88], in1=carryS[:, :]),
                      reads=["psS_f", "carryS"], writes=["psS_f", "carryS"])

        def s0_kv_jobs(esS, ps_k):
            kc = [esS.enter_context(nc.sbuf_tensor("kc%d" % i, [128, HF * DH], BF16)) for i in range(2)]
            kst = [esS.enter_context(nc.sbuf_tensor("kst%d" % i, [128, 8, 128], BF16)) for i in range(2)]
            jobs = []

            def vjob(dstv, srcv, nt, h):
                def f():
                    for t4 in range(0, nt, 4):
                        P.add("pool", lambda e, t4=t4: e.dma_start(
                            out=dstv[h, :, t4:t4 + 4, :],
                            in_=srcv[t4 * 128:(t4 + 4) * 128, h * DH:(h + 1) * DH].rearrange("(t p) d -> p t d", p=128)),
                            chan="vcache", group=True)
                return f

            def kjob(n, src, dstT, t):
                def f():
                    b = n % 2
                    P.add("pool", lambda e: e.dma_start(out=kc[b][:, :], in_=src[t * 128:(t + 1) * 128, :]),
                          writes=[("kc", b)], chan=("kc", b))
                    for hh in range(8):
                        P.add("pe", lambda e, hh=hh: e.transpose(
                            out=ps_k[:, hh, :], in_=kc[b][:, hh * 128:(hh + 1) * 128], identity=identb[:, :]),
                            reads=[("kc", b)], writes=["ps_k"])
                    P.add("dve", lambda e: e.tensor_copy(out=kst[b][:, :, :], in_=ps_k[:, :, :]),
                          reads=["ps_k"], writes=["ps_k", ("kst", b)])
                    P.add("sp", lambda e: e.dma_start(
                        out=dstT[:, :, t * 128:(t + 1) * 128].rearrange("h p t -> p h t"), in_=kst[b][:, :, :]),
                        reads=[("kst", b)], chan=("kst", b))
                return f
            kj = [(cfk_d, kTfs_s, t) for t in range(NP)] + [(cbk_d, kTbs_s, t) for t in range(4)]
            vj = [vjob(vfs_s, cfv_d, NP, h) for h in range(HF)] + [vjob(vbs_s, cbv_d, 4, h) for h in range(HB)]
            for n, (src, dstT, t) in enumerate(kj):
                jobs.append(kjob(n, src, dstT, t))
                if n % 2 == 1 and vj:
                    jobs.append(vj.pop(0))
            jobs.extend(vj)
            return jobs

        def phase_ada(blocks, es0):
            def sb0(name, shape, dt):
                return es0.enter_context(nc.sbuf_tensor(name, list(shape), dt))
            csb = sb0("csb", [2, D], F32)
            ctmp = sb0("ctmp", [2, D], F32)
            sT = sb0("sT", [128, KC, 2], BF16)
            wab = [sb0("wab%d" % i, [128, KC, 512], BF16) for i in range(2)]
            bab = [sb0("bab%d" % i, [2, 512], F32) for i in range(2)]
            blk = [sb0("ablk%d" % i, [2, 512], F32) for i in range(2)]
            ps_t = es0.enter_context(nc.psum_tensor("ps_sT", [128, 512], F32))
            ps_a = [es0.enter_context(nc.psum_tensor("ps_a%d" % i, [128, 512], F32)) for i in range(2)]
            ps_at = es0.enter_context(nc.psum_tensor("ps_at", [128, 512], F32))

            P.add("sp", lambda e: e.dma_start(out=csb[:, :], in_=c2_d[:, :]), writes=["csb"], chan="cld")
            P.add("act", lambda e: e.activation(out=ctmp[:, :], in_=csb[:, :], func=AF.Exp, scale=-1.0),
                  reads=["csb"], writes=["ctmp"])
            P.add("dve", lambda e: e.tensor_scalar_add(out=ctmp[:, :], in0=ctmp[:, :], scalar1=1.0),
                  reads=["ctmp"], writes=["ctmp"])
            P.add("dve", lambda e: e.reciprocal(out=ctmp[:, :], in_=ctmp[:, :]), reads=["ctmp"], writes=["ctmp"])
            P.add("dve", lambda e: e.tensor_mul(out=ctmp[:, :], in0=ctmp[:, :], in1=csb[:, :]),
                  reads=["ctmp", "csb"], writes=["ctmp"])
            for k in range(KC):
                P.add("pe", lambda e, k=k: e.transpose(out=ps_t[:, 2 * k:2 * k + 2],
                                                       in_=ctmp[:, k * 128:(k + 1) * 128],
                                                       identity=ident[0:2, 0:2]),
                      reads=["ctmp"], writes=["ps_t"])
            P.add("dve", lambda e: e.tensor_copy(out=sT[:, :, :],
                                                 in_=ps_t[:, 0:2 * KC].rearrange("p (k r) -> p k r", r=2)),
                  reads=["ps_t"], writes=["sT"])
            for i, cb in enumerate(blocks):
                s = i % 2
                c0 = cb * 512
                P.add("pool", lambda e, s=s, c0=c0: e.dma_start(
                    out=wab[s][:, :, :], in_=w_ada_d[:, c0:c0 + 512].rearrange("(k p) c -> p k c", p=128)),
                    writes=[("wab", s)], chan=("wab", s))
                P.add("sp", lambda e, s=s, c0=c0: e.dma_start(
                    out=bab[s][:, :], in_=b_ada_d[0:1, c0:c0 + 512].to_broadcast([2, 512])),
                    writes=[("bab", s)], chan=("bab", s))
                for k in range(KC):
                    P.add("pe", lambda e, s=s, k=k: e.matmul(ps_a[s][0:2, :], lhsT=sT[:, k, :], rhs=wab[s][:, k, :],
                                                            start=(k == 0), stop=(k == KC - 1)),
                          reads=[("wab", s), "sT"], writes=[("ps_a", s)])
                P.add("dve", lambda e, s=s: e.tensor_add(out=blk[s][:, :], in0=ps_a[s][0:2, :], in1=bab[s][:, :]),
                      reads=[("ps_a", s), ("bab", s)], writes=[("ps_a", s), ("blk", s)])
                P.add("sp", lambda e, s=s, c0=c0: e.dma_start(out=ada_s[:, c0:c0 + 512], in_=blk[s][:, :]),
                      reads=[("blk", s)], writes=[("ada_s", cb)], chan=("ablk", s))
                for j in range(4):
                    jj = cb * 4 + j
                    P.add("pe", lambda e, s=s, j=j, jj=jj: e.transpose(
                        out=ps_at[:, 2 * (jj % 96):2 * (jj % 96) + 2], in_=blk[s][:, j * 128:(j + 1) * 128],
                        identity=ident[0:2, 0:2]),
                        reads=[("blk", s)], writes=["ps_at"])
                P.add("dve", lambda e, cb=cb: e.tensor_copy(
                    out=adaT[:, cb * 4:cb * 4 + 4, :],
                    in_=ps_at[:, 8 * cb:8 * cb + 8].rearrange("p (j r) -> p j r", r=2)),
                    reads=["ps_at"], writes=["ps_at", ("adaT", cb)])

        with ExitStack() as es0:
            phase_ada(list(range(24)), es0)
            P.add("dve", lambda e: e.tensor_scalar_add(out=scp1[:, 0, :, :], in0=adaT[:, 16:32, :], scalar1=1.0),
                  reads=[("adaT", cb) for cb in range(4, 8)], writes=["scp1a"])
            P.add("dve", lambda e: e.tensor_scalar_add(out=scp1[:, 1, :, :], in0=adaT[:, 64:80, :], scalar1=1.0),
                  reads=[("adaT", cb) for cb in range(16, 20)], writes=["scp1m"])
            if with_sample:
                s0_logf(es0)
            P.barrier()

        def phase_a(esA, sfx, xsrc, ntok, nvalid, row, outs, dst, Fd, Rd, gs_off, carry):
            def sbA(name, shape, dt):
                return esA.enter_context(nc.sbuf_tensor(name + sfx, list(shape), dt))
            TA = min(1024, ntok)
            QW = min(512, TA)
            nsubT = TA // 128
            xt = [sbA("xt%d" % i, [128, D], F32) for i in range(2)]
            xn = [sbA("xn%d" % i, [128, D], BF16) for i in range(2)]
            st = [sbA("st%d" % i, [128, 8], F32) for i in range(2)]
            bst = [sbA("bst%d" % i, [128, 24], F32) for i in range(2)]
            fs = sbA("fs", [128, 16], F32)
            uT = [sbA("uT%d" % i, [128, KC, TA], BF16) for i in range(2)]
            wb = [sbA("wb%d" % i, [128, KC, 512], BF16) for i in range(2)]
            wf = sbA("wf", [128, KC, HF], BF16)
            qst = [sbA("qst%d" % i, [128, 512], BF16) for i in range(2)]
            kvf = [sbA("kvf%d" % i, [128, 512], F32) for i in range(2)]
            kvb = [sbA("kvb%d" % i, [128, 512], BF16) for i in range(2)]
            kTst = [sbA("kTst%d" % i, [128, 4, 128], BF16) for i in range(2)]
            lf = [sbA("lf%d" % i, [128, 4, HF], F32) for i in range(2)]
            ps_tr = [esA.enter_context(nc.psum_tensor("psA_tr%d" % i + sfx, [128, 8, 128], BF16)) for i in range(4)]
            ps_mm = [esA.enter_context(nc.psum_tensor("psA_mm%d" % i + sfx, [128, 512], F32)) for i in range(2)]
            ps_f = esA.enter_context(nc.psum_tensor("psA_f" + sfx, [128, 512], F32))
            ps_kt = esA.enter_context(nc.psum_tensor("psA_kt" + sfx, [128, 8, 128], BF16))

            P.add("pool", lambda e: e.dma_start(
                out=wf[:, :, :], in_=w_in_d[:, 3072:3080].rearrange("(k p) c -> p k c", p=128)),
                writes=["wf"], chan="wf")

            cblocks = [(0, "q"), (512, "q"), (1024, "k"), (1536, "k"), (2048, "v"), (2560, "v"),
                       (3080, "q"), (3592, "q"), (4104, "k"), (4616, "k"), (5128, "v"), (5640, "v")]
            wcnt = [0]
            mmcnt = [0]
            stc = [0]
            sub_global = [0]
            kpend = []

            def ln_sub(t0, s):
                ub = (t0 // TA) % 2
                U = uT[ub]
                g = sub_global[0]
                sub_global[0] += 1
                b = g % 2
                r0 = t0 + s * 128
                if nvalid < 128:
                    P.add("pool", lambda e, b=b: e.memset(xt[b][:, :], 0.0), writes=[("xt", b)])
                    P.add("sp", lambda e, b=b: e.dma_start(out=xt[b][0:nvalid, :], in_=xsrc[0:nvalid, :]),
                          writes=[("xt", b)], chan=("xt", b))
                else:
                    P.add("sp", lambda e, b=b, r0=r0: e.dma_start(out=xt[b][:, :], in_=xsrc[r0:r0 + 128, :]),
                          writes=[("xt", b)], chan=("xt", b))
                for q4 in range(4):
                    P.add("dve", lambda e, b=b, q4=q4: e.bn_stats(out=bst[b][:, q4 * 6:(q4 + 1) * 6],
                                                                 in_=xt[b][:, q4 * 512:(q4 + 1) * 512]),
                          reads=[("xt", b)], writes=[("bst", b, q4)])
                P.add("dve", lambda e, b=b: e.bn_aggr(out=st[b][:, 0:2], in_=bst[b][:, :]),
                      reads=[("bst", b, q4) for q4 in range(4)], writes=[("st0", b)])
                P.add("dve", lambda e, b=b: e.tensor_scalar_add(out=st[b][:, 4:5], in0=st[b][:, 1:2], scalar1=LN_EPS),
                      reads=[("st0", b)], writes=[("st4", b)])
                P.add("act", lambda e, b=b: e.activation(out=st[b][:, 5:6], in_=st[b][:, 4:5], func=AF.Ln),
                      reads=[("st4", b)], writes=[("st5", b)])
                P.add("act", lambda e, b=b: e.activation(out=st[b][:, 5:6], in_=st[b][:, 5:6], func=AF.Exp, scale=-0.5),
                      reads=[("st5", b)], writes=[("st5", b)])
                P.add("dve", lambda e, b=b: e.scalar_tensor_tensor(
                    out=st[b][:, 6:7], in0=st[b][:, 0:1], scalar=-1.0, in1=st[b][:, 5:6],
                    op0=ALU.mult, op1=ALU.mult),
                    reads=[("st0", b), ("st5", b)], writes=[("st6", b)])
                P.add("act", lambda e, b=b: e.activation(out=xn[b][:, :], in_=xt[b][:, :], func=AF.Identity,
                                                         scale=st[b][:, 5:6], bias=st[b][:, 6:7]),
                      reads=[("xt", b), ("st5", b), ("st6", b)], writes=[("xn", b)])
                def back():
                    for half in range(2):
                        pb = 2 * (g % 2) + half
                        eng = "act" if half == 0 else "dve"
                        for kk in range(8):
                            k = half * 8 + kk
                            P.add("pe", lambda e, b=b, pb=pb, k=k, kk=kk: e.transpose(
                                out=ps_tr[pb][:, kk, :], in_=xn[b][:, k * 128:(k + 1) * 128], identity=identb[:, :]),
                                reads=[("xn", b)], writes=[("ps_tr", pb)])
                        for kk in range(8):
                            k = half * 8 + kk
                            if eng == "act":
                                fn = lambda e, pb=pb, k=k, kk=kk, s=s, U=U: e.activation(
                                    out=U[:, k, s * 128:(s + 1) * 128], in_=ps_tr[pb][:, kk, :], func=AF.Identity,
                                    scale=scp1[:, 0, k, row:row + 1], bias=adaT[:, k, row:row + 1])
                            else:
                                fn = lambda e, pb=pb, k=k, kk=kk, s=s, U=U: e.tensor_scalar(
                                    out=U[:, k, s * 128:(s + 1) * 128], in0=ps_tr[pb][:, kk, :],
                                    scalar1=scp1[:, 0, k, row:row + 1], scalar2=adaT[:, k, row:row + 1],
                                    op0=ALU.mult, op1=ALU.add)
                            P.add(eng, fn, reads=[("ps_tr", pb)], writes=[("uT", ub, s, k)])
                return back

            def forget_sub(t0, s):
                ub = (t0 // TA) % 2
                U = uT[ub]
                if True:
                    gs = (t0 // 128) + s
                    lb = (gs // 4) % 2
                    for k in range(KC):
                        P.add("pe", lambda e, k=k, s=s, U=U: e.matmul(
                            ps_f[:, 0:HF], lhsT=U[:, k, s * 128:(s + 1) * 128], rhs=wf[:, k, :],
                            start=(k == 0), stop=(k == KC - 1)),
                            reads=[("uT", ub, s, k), "wf"], writes=["ps_f"])
                    L = lf[lb][:, gs % 4, :]
                    P.add("dve", lambda e: e.tensor_add(out=fs[:, 0:8], in0=ps_f[:, 0:HF], in1=bfor[:, :]),
                          reads=["ps_f"], writes=["ps_f", "fs0"])
                    P.add("dve", lambda e: e.tensor_scalar_mul(out=fs[:, 8:16], in0=fs[:, 0:8], scalar1=-1.0),
                          reads=["fs0"], writes=["fs1"])
                    P.add("dve", lambda e: e.tensor_tensor(out=fs[:, 8:16], in0=fs[:, 8:16], in1=fs[:, 0:8], op=ALU.min),
                          reads=["fs0", "fs1"], writes=["fs1"])
                    P.add("act", lambda e: e.activation(out=fs[:, 8:16], in_=fs[:, 8:16], func=AF.Exp),
                          reads=["fs1"], writes=["fs1"])
                    P.add("act", lambda e: e.activation(out=fs[:, 8:16], in_=fs[:, 8:16], func=AF.Ln, bias=1.0),
                          reads=["fs1"], writes=["fs1"])
                    P.add("dve", lambda e: e.tensor_scalar_min(out=fs[:, 0:8], in0=fs[:, 0:8], scalar1=0.0),
                          reads=["fs0"], writes=["fs0"])
                    P.add("dve", lambda e, L=L: e.tensor_sub(out=L, in0=fs[:, 0:8], in1=fs[:, 8:16]),
                          reads=["fs0", "fs1"], writes=[("lf", lb, gs % 4)])
                    if nvalid < 128:
                        P.add("sp", lambda e, lb=lb, gs=gs: e.dma_start(
                            out=outs["logf"][0:nvalid, :], in_=lf[lb][0:nvalid, gs % 4, :]),
                            reads=[("lf", lb, gs % 4)], chan=("lfo", lb))
                    elif gs % 4 == 3:
                        P.add("sp", lambda e, lb=lb, gs=gs: e.dma_start(
                            out=outs["logf"][(gs - 3) * 128:(gs + 1) * 128, :].rearrange("(s p) h -> p s h", p=128),
                            in_=lf[lb][:, :, :]),
                            reads=[("lf", lb, i) for i in range(4)], chan=("lfo", lb))
                    P.add("pe", lambda e, L=L: e.matmul(ps_f[:, 64:72], lhsT=tri[:, :], rhs=L, start=True, stop=True),
                          reads=[("lf", lb, gs % 4)], writes=["ps_f"])
                    P.add("pe", lambda e, L=L: e.matmul(ps_f[:, 80:88], lhsT=onesf[:, :], rhs=L, start=True, stop=True),
                          reads=[("lf", lb, gs % 4)], writes=["ps_f"])
                    P.add("dve", lambda e, gs=gs: e.tensor_add(out=Fd[:, gs_off + gs, :], in0=ps_f[:, 64:72], in1=carry[:, :]),
                          reads=["ps_f", "carry"], writes=["ps_f", ("Fall", gs)])
                    P.add("dve", lambda e: e.tensor_add(out=carry[:, :], in0=ps_f[:, 80:88], in1=carry[:, :]),
                          reads=["ps_f", "carry"], writes=["ps_f", "carry"])
                    P.add("pe", lambda e, gs=gs: e.matmul(ps_f[:, 96:104], lhsT=sel64[:, :], rhs=Fd[:, gs_off + gs, :],
                                                          start=True, stop=True),
                          reads=[("Fall", gs)], writes=["ps_f"])
                    P.add("dve", lambda e, gs=gs: e.tensor_copy(out=Rd[:, gs_off + gs, :], in_=ps_f[:, 96:104]),
                          reads=["ps_f"], writes=["ps_f", ("Rall", gs)])

            def colblock(t0, c0, kind):
                ub = (t0 // TA) % 2
                U = uT[ub]
                wi = wcnt[0] % 2
                wcnt[0] += 1
                P.add("pool", lambda e, wi=wi, c0=c0: e.dma_start(
                    out=wb[wi][:, :, :], in_=w_in_d[:, c0:c0 + 512].rearrange("(k p) c -> p k c", p=128)),
                    writes=[("wb", wi)], chan=("wb", wi))
                h0 = (c0 // 512) * 4 if c0 < 3072 else HF + ((c0 - 3080) // 512) * 4
                h0 = h0 % 8 + (8 if c0 >= 3080 else 0)
                if kind == "q":
                    for hh in range(4):
                        h = h0 + hh
                        for tb in range(TA // QW):
                            pm = mmcnt[0] % 2
                            mmcnt[0] += 1
                            for k in range(KC):
                                P.add("pe", lambda e, pm=pm, wi=wi, k=k, hh=hh, tb=tb, U=U: e.matmul(
                                    ps_mm[pm][:, 0:QW], lhsT=wb[wi][:, k, hh * 128:(hh + 1) * 128],
                                    rhs=U[:, k, tb * QW:(tb + 1) * QW], start=(k == 0), stop=(k == KC - 1)),
                                    reads=[("wb", wi)] + [("uT", ub, tb * (QW // 128) + i, k) for i in range(QW // 128)],
                                    writes=[("ps_mm", pm)])
                            qi = stc[0] % 2
                            stc[0] += 1
                            P.add("act", lambda e, pm=pm, qi=qi: e.activation(
                                out=qst[qi][:, 0:QW], in_=ps_mm[pm][:, 0:QW], func=AF.Copy),
                                reads=[("ps_mm", pm)], writes=[("ps_mm", pm), ("qst", qi)])
                            qdst = dst["q"](h, t0 + tb * QW, QW)
                            P.add("sp", lambda e, qi=qi, qdst=qdst: e.dma_start(out=qdst, in_=qst[qi][:, 0:QW]),
                                  reads=[("qst", qi)], writes=[("qT_s", h)], chan=("qst", qi))
                else:
                    for s in range(nsubT):
                        gs = (t0 // 128) + s
                        pm = mmcnt[0] % 2
                        mmcnt[0] += 1
                        for k in range(KC):
                            P.add("pe", lambda e, pm=pm, wi=wi, k=k, s=s, U=U: e.matmul(
                                ps_mm[pm][:, :], lhsT=U[:, k, s * 128:(s + 1) * 128], rhs=wb[wi][:, k, :],
                                start=(k == 0), stop=(k == KC - 1)),
                                reads=[("wb", wi), ("uT", ub, s, k)], writes=[("ps_mm", pm)])
                        while kpend:
                            kpend.pop(0)()
                        qi = stc[0] % 2
                        stc[0] += 1
                        P.add("act", lambda e, pm=pm, qi=qi: e.activation(
                            out=kvf[qi][:, :], in_=ps_mm[pm][:, :], func=AF.Copy),
                            reads=[("ps_mm", pm)], writes=[("kvf", qi)])
                        P.add("dve", lambda e, pm=pm, qi=qi: e.tensor_copy(out=kvb[qi][:, :], in_=ps_mm[pm][:, :]),
                              reads=[("ps_mm", pm)], writes=[("ps_mm", pm), ("kvb", qi)])
                        fox = c0 < 3072
                        cc = (c0 - (1024 if kind == "k" else 2048)) if fox else (c0 - (4104 if kind == "k" else 5128))
                        nr = min(128, nvalid)
                        if fox:
                            odst = outs["fk" if kind == "k" else "fv"][gs * 128:gs * 128 + nr, cc:cc + 512]
                        elif gs * 128 >= ntok - outs["wb"]:
                            rr = gs * 128 - (ntok - outs["wb"])
                            odst = outs["bk" if kind == "k" else "bv"][rr:rr + nr, cc:cc + 512]
                        else:
                            odst = None
                        if odst is not None:
                            P.add("sp", lambda e, qi=qi, odst=odst, nr=nr: e.dma_start(out=odst, in_=kvf[qi][0:nr, :]),
                                  reads=[("kvf", qi)], chan=("kvf", qi))
                        if kind == "v":
                            vdst = dst["v"](h0, gs)
                            P.add("sp", lambda e, qi=qi, vdst=vdst: e.dma_start(
                                out=vdst, in_=kvb[qi][:, :].rearrange("p (h d) -> p h d", d=DH)),
                                reads=[("kvb", qi)], writes=[("v_s", h0)], chan=("kvb", qi))
                        else:
                            ki = stc[0] % 2

                            def ktail(qi=qi, ki=ki, h0=h0, gs=gs):
                                for hh in range(4):
                                    P.add("pe", lambda e, qi=qi, hh=hh: e.transpose(
                                        out=ps_kt[:, hh, :], in_=kvb[qi][:, hh * 128:(hh + 1) * 128], identity=identb[:, :]),
                                        reads=[("kvb", qi)], writes=["ps_kt"])
                                P.add("dve", lambda e, ki=ki: e.tensor_copy(out=kTst[ki][:, :, :], in_=ps_kt[:, 0:4, :]),
                                      reads=["ps_kt"], writes=["ps_kt", ("kTst", ki)])
                                kdst = dst["kt"](h0, gs)
                                P.add("sp", lambda e, ki=ki, kdst=kdst: e.dma_start(out=kdst, in_=kTst[ki][:, :, :]),
                                      reads=[("kTst", ki)], writes=[("kT_s", h0)], chan=("kTst", ki))
                            kpend.append(ktail)
                while kpend:
                    kpend.pop(0)()

            tiles = list(range(0, ntok, TA))
            for s in range(nsubT):
                ln_sub(tiles[0], s)()
            for ti, t0 in enumerate(tiles):
                pend_back = None
                for ci, (c0, kind) in enumerate(cblocks):
                    colblock(t0, c0, kind)
                    if ci < nsubT:
                        forget_sub(t0, ci)
                    if pend_back is not None:
                        pend_back()
                        pend_back = None
                    if ti + 1 < len(tiles) and ci < nsubT:
                        pend_back = ln_sub(tiles[ti + 1], ci)
                if pend_back is not None:
                    pend_back()

        dstP = dict(
            q=lambda h, a, w: qT_s[h, :, a:a + w],
            kt=lambda h0, gs: kT_s[h0:h0 + 4, :, gs * 128:(gs + 1) * 128].rearrange("h p t -> p h t"),
            v=lambda h0, gs: v_s[h0:h0 + 4, :, gs, :].rearrange("h p d -> p h d"))
        with ExitStack() as esA:
            phase_a(esA, "", x_d, S, S, 0, dict(fk=fk_o, fv=fv_o, logf=fl_o, bk=bk_o, bv=bv_o, wb=WB),
                    dstP, Fall, Rall, 0, carry)
            P.barrier()

        def q_s(h, a, w):
            return qTs_s[h, :, 0:w]

        def kt_s(h0, gs):
            if h0 < HF:
                return kTfs_s[h0:h0 + 4, :, NP * 128:(NP + 1) * 128].rearrange("h p t -> p h t")
            return kTbs_s[h0 - HF:h0 - HF + 4, :, 4 * 128:5 * 128].rearrange("h p t -> p h t")

        def v_ss(h0, gs):
            if h0 < HF:
                return vfs_s[h0:h0 + 4, :, NP, :].rearrange("h p d -> p h d")
            return vbs_s[h0 - HF:h0 - HF + 4, :, 4, :].rearrange("h p d -> p h d")

        if with_sample:
            with ExitStack() as esA:
                phase_a(esA, "s", xs_d, 128, NS, 1, dict(fk=fks_o, fv=fvs_o, logf=fls_o, bk=bks_o, bv=bvs_o, wb=128),
                        dict(q=q_s, kt=kt_s, v=v_ss), FallS, RallS, NP, carryS)
                P.barrier()


        def phase_b(esB, sfx, heads, jobs_fn=None):
            def sbB(name, shape, dt):
                return esB.enter_context(nc.sbuf_tensor(name + sfx, list(shape), dt))
            NKM = max(cfg(h)["NK"] for (h, cfg, _m, _p) in heads)
            NQM = max(len(cfg(h)["qtiles"]) for (h, cfg, _m, _p) in heads)
            qT = [sbB("qT%d" % i, [128, NQM * 128], BF16) for i in range(2)]
            kT = [sbB("kT%d" % i, [128, NKM * 128], BF16) for i in range(2)]
            vA = [sbB("vA%d" % i, [128, NKM, DH], BF16) for i in range(2)]
            biasT = sbB("biasT", [128, NQM, NKM], F32)
            Eb = sbB("Eb", [128, HB, 5, 128], F32)
            xh = [sbB("xh%d" % i, [128, 128], F32) for i in range(2)]
            NPB = 5
            Pb = [sbB("Pb%d" % i, [128, 512], BF16) for i in range(NPB)]
            recb = [sbB("recb%d" % i, [128, 512], F32) for i in range(2)]
            dcp = [sbB("dcp%d" % i, [128, 512], F32) for i in range(2)]
            rcp = [sbB("rcp%d" % i, [128, 512], F32) for i in range(2)]
            acc = [sbB("acc%d" % i, [128, 512], F32) for i in range(2)]
            NGM = (NQM + 3) // 4
            biasF = sbB("biasF", [128, NGM, NKM], F32)
            cjt = sbB("cjt", [128, NGM, 4], F32)
            mst = [sbB("mst%d" % i, [128, 512], BF16) for i in range(2)]
            ps_s = [esB.enter_context(nc.psum_tensor("psB_s%d" % i + sfx, [128, 512], F32)) for i in range(3)]
            ps_k = esB.enter_context(nc.psum_tensor("psB_k" + sfx, [128, 8, 128], BF16))
            jobs = jobs_fn(esB, ps_k) if jobs_fn is not None else []
            ps_o = [esB.enter_context(nc.psum_tensor("psB_o%d" % i + sfx, [128, 512], F32)) for i in range(2)]
            ps_r = [esB.enter_context(nc.psum_tensor("psB_r%d" % i + sfx, [128, 512], F32)) for i in range(2)]

            if any(h >= HF for (h, _c, _m, _p) in heads):
                cnt = 0
                for hb in range(HB):
                    for jj in range(5):
                        xi = cnt % 2
                        si = cnt % 3
                        cnt += 1
                        src = bass.AP(tensor=g_d.tensor, offset=hb * 768 + 128 * jj, ap=[[1, 128], [1, 128]])
                        P.add("sp", lambda e, xi=xi, src=src: e.dma_start(out=xh[xi][:, :], in_=src),
                              writes=[("xh", xi)], chan=("xh", xi))
                        P.add("pe", lambda e, xi=xi, si=si: e.matmul(ps_s[si][:, 0:128], lhsT=jx[:, :], rhs=xh[xi][:, :],
                                                                    start=True, stop=True),
                              reads=[("xh", xi)], writes=[("ps_s", si)])
                        P.add("act", lambda e, si=si, hb=hb, jj=jj: e.activation(
                            out=Eb[:, hb, jj, :], in_=ps_s[si][:, 0:128], func=AF.Exp),
                            reads=[("ps_s", si)], writes=[("ps_s", si), ("Eb", hb)])
                        if jj in (0, 4):
                            mk = mka if jj == 0 else mkb
                            P.add("pool", lambda e, hb=hb, jj=jj, mk=mk: e.tensor_mul(
                                out=Eb[:, hb, jj, :], in0=Eb[:, hb, jj, :], in1=mk[:, :]),
                                reads=[("Eb", hb)], writes=[("Eb", hb)])

            gcount = [0]
            ocount = [0]
            prev_part = None
            for hi, (h, cfg, mixdst, part) in enumerate(heads):
                if prev_part is not None and part != prev_part:
                    while jobs:
                        jobs.pop(0)()
                    P.barrier()
                prev_part = part
                hb2 = hi % 2
                fox = h < HF
                cf = cfg(h)
                NK = cf["NK"]
                qtiles = cf["qtiles"]
                qoff = qtiles[0]
                Fd, Rd = cf["Fd"], cf["Rd"]
                nq = len(qtiles)
                P.add("sp", lambda e, cf=cf, hb2=hb2, nq=nq: e.dma_start(out=qT[hb2][:, 0:nq * 128], in_=cf["qsrc"]),
                      writes=[("qT", hb2)], chan=("qkv", hb2))
                P.add("sp", lambda e, cf=cf, hb2=hb2, NK=NK: e.dma_start(out=kT[hb2][:, 0:NK * 128], in_=cf["ksrc"]),
                      writes=[("kT", hb2)], chan=("qkv", hb2), group=True)
                P.add("sp", lambda e, cf=cf, hb2=hb2, NK=NK: e.dma_start(out=vA[hb2][:, 0:NK, :], in_=cf["vsrc"]),
                      writes=[("vA", hb2)], chan=("qkv", hb2), group=True)
                if fox:
                    for j in qtiles:
                        P.add("dve", lambda e, j=j, h=h, qoff=qoff, Fd=Fd, Rd=Rd: e.tensor_scalar(
                            out=biasT[:, j - qoff, 0:j + 1], in0=Fd[:, 0:j + 1, h], scalar1=-1.0, scalar2=Rd[:, j, h:h + 1],
                            op0=ALU.mult, op1=ALU.add),
                            writes=[("biasT", j - qoff)])
                steps = []
                for g0 in range(0, nq, 4):
                    G = qtiles[g0:g0 + 4]
                    j0, j1 = G[0], G[-1]
                    gi = g0 // 4
                    if fox and j0 > 0:
                        P.add("dve", lambda e, gi=gi, j0=j0, h=h, Fd=Fd, Rd=Rd: e.tensor_scalar(
                            out=biasF[:, gi, 0:j0], in0=Fd[:, 0:j0, h], scalar1=-1.0, scalar2=Rd[:, j0, h:h + 1],
                            op0=ALU.mult, op1=ALU.add),
                            writes=[("biasF", gi)])
                        P.add("dve", lambda e, gi=gi, j0=j0, ng=len(G), h=h, Rd=Rd: e.tensor_scalar(
                            out=cjt[:, gi, 0:ng], in0=Rd[:, j0:j0 + ng, h], scalar1=Rd[:, j0, h:h + 1], scalar2=None,
                            op0=ALU.subtract),
                            writes=[("cj", gi)])
                        P.add("act", lambda e, gi=gi, ng=len(G): e.activation(
                            out=cjt[:, gi, 0:ng], in_=cjt[:, gi, 0:ng], func=AF.Exp),
                            reads=[("cj", gi)], writes=[("cj", gi)])
                        for kb in range(j0, j1 + 1):
                            steps.append((j0, len(G), kb, kb, j1, kb == j0, kb == j1, "d", gi))
                        for kb in range(0, j0):
                            steps.append((j0, len(G), kb, j0, j1, kb == 0, kb == j0 - 1, "f", gi))
                        continue
                    kbs = list(range(0, j1 + 1)) if fox else list(range(max(0, j0 - 4), j1 + 1))
                    for kb in kbs:
                        ja = max(j0, kb)
                        jb = j1 if fox else min(j1, kb + 4)
                        steps.append((j0, len(G), kb, ja, jb, kb == kbs[0], kb == kbs[-1], "n", gi))
                pend = []

                def emit_pv(item, hb2=hb2, h=h, qoff=qoff, mixdst=mixdst):
                    (j0, ng, kb, ja, jb, first, last, pslot, ob, typ, gi) = item
                    n = jb - ja + 1
                    c0 = (ja - j0) * 128
                    P.add("pe", lambda e: e.matmul(
                        ps_o[ob][:, c0:c0 + n * 128], lhsT=vA[hb2][:, kb, :], rhs=Pb[pslot][:, 0:n * 128],
                        start=first, stop=last, skip_group_check=True),
                        reads=[("Pb", pslot, i) for i in range(n)] + [("vA", hb2)], writes=[("ps_o", ob)])
                    P.add("pe", lambda e: e.matmul(
                        ps_r[ob][:, c0:c0 + n * 128], lhsT=onesb[:, :], rhs=Pb[pslot][:, 0:n * 128],
                        start=first, stop=last, skip_group_check=True),
                        reads=[("Pb", pslot, i) for i in range(n)], writes=[("ps_r", ob)])
                    if last and typ == "d":
                        w = ng * 128
                        g2 = gi % 2
                        P.add("act", lambda e: e.activation(out=dcp[g2][:, 0:w], in_=ps_o[ob][:, 0:w], func=AF.Copy),
                              reads=[("ps_o", ob)], writes=[("ps_o", ob), ("dcp", g2)])
                        P.add("act", lambda e: e.activation(out=rcp[g2][:, 0:w], in_=ps_r[ob][:, 0:w], func=AF.Copy),
                              reads=[("ps_r", ob)], writes=[("ps_r", ob), ("rcp", g2)])
                    elif last and typ == "f":
                        w = ng * 128
                        g2 = gi % 2
                        for i in range(ng):
                            cs = slice(i * 128, (i + 1) * 128)
                            P.add("dve", lambda e, i=i, cs=cs: e.scalar_tensor_tensor(
                                out=acc[g2][:, cs], in0=ps_o[ob][:, cs], scalar=cjt[:, gi, i:i + 1], in1=dcp[g2][:, cs],
                                op0=ALU.mult, op1=ALU.add),
                                reads=[("ps_o", ob), ("cj", gi), ("dcp", g2)], writes=[("ps_o", ob), ("acc", g2, i)])
                            P.add("dve", lambda e, i=i, cs=cs: e.scalar_tensor_tensor(
                                out=recb[g2][:, cs], in0=ps_r[ob][:, cs], scalar=cjt[:, gi, i:i + 1], in1=rcp[g2][:, cs],
                                op0=ALU.mult, op1=ALU.add),
                                reads=[("ps_r", ob), ("cj", gi), ("rcp", g2)], writes=[("ps_r", ob), ("recq", g2, i)])
                        P.add("dve", lambda e: e.reciprocal(out=recb[g2][:, 0:w], in_=recb[g2][:, 0:w]),
                              reads=[("recq", g2, i) for i in range(ng)], writes=[("recb", g2)] + [("recq", g2, i) for i in range(ng)])
                        P.add("dve", lambda e: e.tensor_mul(out=mst[g2][:, 0:w], in0=acc[g2][:, 0:w], in1=recb[g2][:, 0:w]),
                              reads=[("acc", g2, i) for i in range(ng)] + [("recb", g2)], writes=[("mst", g2)])
                        md = mixdst(h, j0 - qoff, ng)
                        P.add("sp", lambda e: e.dma_start(out=md, in_=mst[g2][:, 0:w]),
                              reads=[("mst", g2)], writes=[("mixT_s", h)], chan=("mst", g2))
                        if jobs:
                            jobs.pop(0)()
                    elif last:
                        w = ng * 128
                        if h < HF:
                            P.add("dve", lambda e: e.reciprocal(out=recb[ob][:, 0:w], in_=ps_r[ob][:, 0:w]),
                                  reads=[("ps_r", ob)], writes=[("ps_r", ob), ("recb", ob)])
                        else:
                            P.add("act", lambda e: e.activation(out=recb[ob][:, 0:w], in_=ps_r[ob][:, 0:w], func=AF.Ln),
                                  reads=[("ps_r", ob)], writes=[("ps_r", ob), ("recb", ob)])
                            P.add("act", lambda e: e.activation(out=recb[ob][:, 0:w], in_=recb[ob][:, 0:w], func=AF.Exp,
                                                                scale=-1.0),
                                  reads=[("recb", ob)], writes=[("recb", ob)])
                        P.add("dve", lambda e: e.tensor_mul(out=mst[ob][:, 0:w], in0=ps_o[ob][:, 0:w], in1=recb[ob][:, 0:w]),
                              reads=[("ps_o", ob), ("recb", ob)], writes=[("ps_o", ob), ("mst", ob)])
                        md = mixdst(h, j0 - qoff, ng)
                        P.add("sp", lambda e: e.dma_start(out=md, in_=mst[ob][:, 0:w]),
                              reads=[("mst", ob)], writes=[("mixT_s", h)], chan=("mst", ob))
                        if jobs:
                            jobs.pop(0)()

                for (j0, ng, kb, ja, jb, first, last, typ, gi) in steps:
                    si = gcount[0] % 3
                    pslot = gcount[0] % NPB
                    gcount[0] += 1
                    if typ == "n":
                        if first:
                            ocount[0] += 1
                        ob = ocount[0] % 2
                    else:
                        ob = 0 if typ == "d" else 1
                    n = jb - ja + 1
                    P.add("pe", lambda e, si=si, kb=kb, ja=ja, jb=jb, n=n, hb2=hb2, qoff=qoff: e.matmul(
                        ps_s[si][:, 0:n * 128], lhsT=kT[hb2][:, kb * 128:(kb + 1) * 128],
                        rhs=qT[hb2][:, (ja - qoff) * 128:(jb + 1 - qoff) * 128], start=True, stop=True),
                        reads=[("kT", hb2), ("qT", hb2)], writes=[("ps_s", si)])
                    if fox and typ == "f":
                        P.add("act", lambda e, si=si, n=n, kb=kb, gi=gi, pslot=pslot: e.activation(
                            out=Pb[pslot][:, 0:n * 128], in_=ps_s[si][:, 0:n * 128], func=AF.Exp, scale=SCALE,
                            bias=biasF[:, gi, kb:kb + 1]),
                            reads=[("ps_s", si), ("biasF", gi)], writes=[("Pb", pslot, i) for i in range(n)])
                    elif fox:
                        for j in range(ja, jb + 1):
                            i = j - ja
                            P.add("act", lambda e, si=si, i=i, kb=kb, j=j, pslot=pslot, qoff=qoff: e.activation(
                                out=Pb[pslot][:, i * 128:(i + 1) * 128], in_=ps_s[si][:, i * 128:(i + 1) * 128],
                                func=AF.Exp, scale=SCALE, bias=biasT[:, j - qoff, kb:kb + 1]),
                                reads=[("ps_s", si), ("biasT", j - qoff)], writes=[("Pb", pslot, i)])
                        if ja == kb:
                            P.add("pool", lambda e, pslot=pslot: e.tensor_mul(
                                out=Pb[pslot][:, 0:128], in0=Pb[pslot][:, 0:128], in1=trib[:, :]),
                                reads=[("Pb", pslot, 0)], writes=[("Pb", pslot, 0)])
                    else:
                        P.add("act", lambda e, si=si, n=n, pslot=pslot: e.activation(
                            out=Pb[pslot][:, 0:n * 128], in_=ps_s[si][:, 0:n * 128], func=AF.Exp, scale=SCALE),
                            reads=[("ps_s", si)], writes=[("Pb", pslot, i) for i in range(n)])
                        P.add("dve" if gcount[0] % 2 == 0 else "pool", lambda e, pslot=pslot, n=n, kb=kb, ja=ja, jb=jb, h=h: e.tensor_mul(
                            out=Pb[pslot][:, 0:n * 128], in0=Pb[pslot][:, 0:n * 128],
                            in1=Eb[:, h - HF, ja - kb:jb - kb + 1, :].rearrange("p a b -> p (a b)")),
                            reads=[("Pb", pslot, i) for i in range(n)] + [("Eb", h - HF)],
                            writes=[("Pb", pslot, i) for i in range(n)])
                    pend.append((j0, ng, kb, ja, jb, first, last, pslot, ob, typ, gi))
                    if len(pend) > 2:
                        emit_pv(pend.pop(0))
                while pend:
                    emit_pv(pend.pop(0))

        def bcast_load(dst, src_row, chan):
            P.add("sp", lambda e: e.dma_start(out=dst[:, :], in_=src_row.to_broadcast([128, D])),
                  writes=[chan], chan=chan)

        def load_wo(es_, sfx, row, defer=None):
            wo = es_.enter_context(nc.sbuf_tensor("wo" + sfx, [128, KC, D], BF16))

            def emit():
                for c4 in range(4):
                    P.add("pool", lambda e, c4=c4: e.dma_start(
                        out=wo[:, :, c4 * 512:(c4 + 1) * 512],
                        in_=w_out_d[:, c4 * 512:(c4 + 1) * 512].rearrange("(k p) c -> p k c", p=128)),
                        writes=[("wo", c4)], chan=("wo", c4))
            if defer is not None:
                defer.append(emit)
            else:
                emit()
            return wo

        def cfgP(h):
            return dict(NK=NT, qtiles=list(range(NT)), qsrc=qT_s[h, :, :], ksrc=kT_s[h, :, :], vsrc=v_s[h, :, :, :],
                        Fd=Fall, Rd=Rall)

        def cfgS(h):
            if h < HF:
                return dict(NK=NP + 1, qtiles=[NP], qsrc=qTs_s[h, :, :], ksrc=kTfs_s[h, :, :], vsrc=vfs_s[h, :, :, :],
                            Fd=FallS, Rd=RallS)
            return dict(NK=5, qtiles=[4], qsrc=qTs_s[h, :, :], ksrc=kTbs_s[h - HF, :, :], vsrc=vbs_s[h - HF, :, :, :],
                        Fd=FallS, Rd=RallS)

        wcast_jobs = []
        if "D" in phases:
            def wc_up(r):
                return lambda: P.add("pool", lambda e: e.dma_start(out=wup_s[r * 128:(r + 1) * 128, :],
                                                                   in_=w_up_d[r * 128:(r + 1) * 128, :]),
                                     chan="wcast", group=True)

            def wc_dn(r):
                return lambda: P.add("pool", lambda e: e.dma_start(
                    out=wdn_s[r * 512:(r + 1) * 512, :].rearrange("(a p) c -> p a c", p=128),
                    in_=w_dn_d[r * 512:(r + 1) * 512, :].rearrange("(a p) c -> p a c", p=128)),
                    chan="wcast", group=True)
            for r in range(D // 128):
                wcast_jobs.append(wc_up(r))
                wcast_jobs.append(wc_dn(r))
        if "B" in phases:
            mdP = lambda h, j0, n: mixT_s[h * 128:(h + 1) * 128, j0 * 128:(j0 + n) * 128]
            mdS = lambda h, j0, n: mixTs_s[h * 128:(h + 1) * 128, j0 * 128:(j0 + n) * 128]
            hl = [(h, cfgP, mdP, 0) for h in range(NH)]
            if with_sample:
                hl += [(h, cfgS, mdS, 1) for h in range(NH)]
            esW = ExitStack()
            pre_jobs = []
            wo_p = load_wo(esW, "", 0, defer=pre_jobs) if "C" in phases else None

            def all_jobs(esS, ps_k):
                sj = s0_kv_jobs(esS, ps_k) if with_sample else []
                out = list(pre_jobs)
                wj = list(wcast_jobs)
                while sj or wj:
                    if wj:
                        out.append(wj.pop(0))
                    if sj:
                        out.append(sj.pop(0))
                return out
            with ExitStack() as esB:
                phase_b(esB, "", hl, all_jobs)
                P.barrier()

        def ln_tokmajor(pre, stt, bstt, tag, q):
            for q4 in range(4):
                P.add("dve", lambda e, q4=q4: e.bn_stats(out=bstt[:, q4 * 6:(q4 + 1) * 6], in_=pre[:, q4 * 512:(q4 + 1) * 512]),
                      reads=[(tag, q)], writes=[("bst" + tag, q, q4)])
            P.add("dve", lambda e: e.bn_aggr(out=stt[:, 0:2], in_=bstt[:, :]),
                  reads=[("bst" + tag, q, q4) for q4 in range(4)], writes=[("st0" + tag, q)])
            P.add("dve", lambda e: e.tensor_scalar_add(out=stt[:, 4:5], in0=stt[:, 1:2], scalar1=LN_EPS),
                  reads=[("st0" + tag, q)], writes=[("st4" + tag, q)])
            P.add("act", lambda e: e.activation(out=stt[:, 5:6], in_=stt[:, 4:5], func=AF.Ln),
                  reads=[("st4" + tag, q)], writes=[("st5" + tag, q)])
            P.add("act", lambda e: e.activation(out=stt[:, 5:6], in_=stt[:, 5:6], func=AF.Exp, scale=-0.5),
                  reads=[("st5" + tag, q)], writes=[("st5" + tag, q)])
            P.add("dve", lambda e: e.scalar_tensor_tensor(
                out=stt[:, 6:7], in0=stt[:, 0:1], scalar=-1.0, in1=stt[:, 5:6], op0=ALU.mult, op1=ALU.mult),
                reads=[("st0" + tag, q), ("st5" + tag, q)], writes=[("st6" + tag, q)])

        def phase_c1(esC, sfx, ntok, nvalid, row, xsrc, mixsrc, x1dst, u2dst, wo=None):
            def sbC(name, shape, dt):
                return esC.enter_context(nc.sbuf_tensor(name + sfx, list(shape), dt))
            TB = min(512, ntok)
            NS4 = TB // 128
            if wo is None:
                wo = load_wo(esC, sfx, row)
            gbc = sbC("gbc", [128, D], F32)
            lgbc = sbC("lgbc", [128, D], F32)
            lbbc = sbC("lbbc", [128, D], F32)
            mT = [sbC("mT%d" % i, [128, KC, TB], BF16) for i in range(2)]
            xt = [sbC("xtc%d" % i, [128, D], F32) for i in range(2)]
            pre = [sbC("pre%d" % i, [128, D], F32) for i in range(2)]
            tmp = [sbC("tmpc%d" % i, [128, 512], F32) for i in range(2)]
            xn2 = [sbC("xn2%d" % i, [128, D], BF16) for i in range(2)]
            u2st = [sbC("u2st%d" % i, [128, KC, 128], BF16) for i in range(2)]
            st = [sbC("stc%d" % i, [128, 8], F32) for i in range(4)]
            bst = [sbC("bstc%d" % i, [128, 24], F32) for i in range(4)]
            ps_mm = [esC.enter_context(nc.psum_tensor("psC_mm%d" % i + sfx, [128, 512], F32)) for i in range(4)]
            ps_tr = [esC.enter_context(nc.psum_tensor("psC_tr%d" % i + sfx, [128, 8, 128], BF16)) for i in range(4)]

            bcast_load(gbc, ada_s[row:row + 1, 2 * D:3 * D], "gbc" + sfx)
            bcast_load(lgbc, lnv_d[0:1, :], "lgbc" + sfx)
            bcast_load(lbbc, lnv_d[1:2, :], "lbbc" + sfx)
            mm = [0]
            pending = []
            pending2 = []
            for tb in range(ntok // TB):
                mb = tb % 2
                P.add("sp", lambda e, mb=mb, tb=tb: e.dma_start(
                    out=mT[mb][:, :, :], in_=mixsrc[:, tb * TB:(tb + 1) * TB].rearrange("(k p) t -> p k t", p=128)),
                    writes=[("mT", mb)], chan=("mT", mb))
                for s4 in range(NS4):
                    g = tb * NS4 + s4
                    b = g % 2
                    r0 = g * 128
                    if nvalid < 128:
                        P.add("pool", lambda e, b=b: e.memset(xt[b][:, :], 0.0), writes=[("xtc", b)])
                        P.add("sp", lambda e, b=b: e.dma_start(out=xt[b][0:nvalid, :], in_=xsrc[0:nvalid, :]),
                              writes=[("xtc", b)], chan=("xtc", b))
                    else:
                        P.add("sp", lambda e, b=b, r0=r0: e.dma_start(out=xt[b][:, :], in_=xsrc[r0:r0 + 128, :]),
                              writes=[("xtc", b)], chan=("xtc", b))
                    for c4 in range(4):
                        pm = mm[0] % 4
                        mm[0] += 1
                        for k in range(KC):
                            P.add("pe", lambda e, pm=pm, mb=mb, k=k, s4=s4, c4=c4: e.matmul(
                                ps_mm[pm][:, :], lhsT=mT[mb][:, k, s4 * 128:(s4 + 1) * 128],
                                rhs=wo[:, k, c4 * 512:(c4 + 1) * 512], start=(k == 0), stop=(k == KC - 1)),
                                reads=[("mT", mb), ("wo", c4)], writes=[("ps_mm", pm)])
                        ti = mm[0] % 2
                        cs = slice(c4 * 512, (c4 + 1) * 512)
                        P.add("dve", lambda e, pm=pm, ti=ti, cs=cs: e.tensor_mul(
                            out=tmp[ti][:, :], in0=ps_mm[pm][:, :], in1=gbc[:, cs]),
                            reads=[("ps_mm", pm), "gbc" + sfx], writes=[("ps_mm", pm), ("tmpc", ti)])
                        P.add("dve", lambda e, b=b, ti=ti, cs=cs: e.scalar_tensor_tensor(
                            out=pre[b][:, cs], in0=xt[b][:, cs], scalar=ALPHA, in1=tmp[ti][:, :],
                            op0=ALU.mult, op1=ALU.add),
                            reads=[("xtc", b), ("tmpc", ti)], writes=[("pre", b, c4)])
                    while pending2:
                        pending2.pop(0)()
                    while pending:
                        pending.pop(0)()
                    for q4 in range(4):
                        P.add("dve", lambda e, b=b, q4=q4: e.bn_stats(out=bst[b][:, q4 * 6:(q4 + 1) * 6],
                                                                     in_=pre[b][:, q4 * 512:(q4 + 1) * 512]),
                              reads=[("pre", b, q4)], writes=[("bstc", b, q4)])
                    P.add("dve", lambda e, b=b: e.bn_aggr(out=st[b][:, 0:2], in_=bst[b][:, :]),
                          reads=[("bstc", b, q4) for q4 in range(4)], writes=[("st0c", b)])
                    P.add("dve", lambda e, b=b: e.tensor_scalar_add(out=st[b][:, 4:5], in0=st[b][:, 1:2], scalar1=LN_EPS),
                          reads=[("st0c", b)], writes=[("st4c", b)])
                    P.add("act", lambda e, b=b: e.activation(out=st[b][:, 5:6], in_=st[b][:, 4:5], func=AF.Ln),
                          reads=[("st4c", b)], writes=[("st5c", b)])
                    P.add("act", lambda e, b=b: e.activation(out=st[b][:, 5:6], in_=st[b][:, 5:6], func=AF.Exp, scale=-0.5),
                          reads=[("st5c", b)], writes=[("st5c", b)])
                    P.add("dve", lambda e, b=b: e.scalar_tensor_tensor(
                        out=st[b][:, 6:7], in0=st[b][:, 0:1], scalar=-1.0, in1=st[b][:, 5:6], op0=ALU.mult, op1=ALU.mult),
                        reads=[("st0c", b), ("st5c", b)], writes=[("st6c", b)])
                    P.add("act", lambda e, b=b: e.activation(out=pre[b][:, :], in_=pre[b][:, :], func=AF.Identity,
                                                             scale=st[b][:, 5:6], bias=st[b][:, 6:7]),
                          reads=[("pre", b, q4) for q4 in range(4)] + [("st5c", b), ("st6c", b)],
                          writes=[("pre", b, q4) for q4 in range(4)])
                    P.add("pool", lambda e, b=b: e.tensor_mul(out=pre[b][:, :], in0=pre[b][:, :], in1=lgbc[:, :]),
                          reads=[("pre", b, q4) for q4 in range(4)] + ["lgbc" + sfx], writes=[("pre", b, q4) for q4 in range(4)])
                    P.add("pool", lambda e, b=b: e.tensor_add(out=pre[b][:, :], in0=pre[b][:, :], in1=lbbc[:, :]),
                          reads=[("pre", b, q4) for q4 in range(4)] + ["lbbc" + sfx], writes=[("pre", b, q4) for q4 in range(4)])
                    P.add("pool", lambda e, b=b, r0=r0: e.dma_start(out=x1dst[r0:r0 + 128, :], in_=pre[b][:, :]),
                          reads=[("pre", b, q4) for q4 in range(4)], writes=[("x1_s", g)], chan=("x1o", b))
                    def do_tail(g=g, b=b, r0=r0):
                        b2 = 2 + b
                        for q4 in range(4):
                            P.add("dve", lambda e, b=b, b2=b2, q4=q4: e.bn_stats(out=bst[b2][:, q4 * 6:(q4 + 1) * 6],
                                                                                in_=pre[b][:, q4 * 512:(q4 + 1) * 512]),
                                  reads=[("pre", b, q4)], writes=[("bstc", b2, q4)])
                        P.add("dve", lambda e, b2=b2: e.bn_aggr(out=st[b2][:, 0:2], in_=bst[b2][:, :]),
                              reads=[("bstc", b2, q4) for q4 in range(4)], writes=[("st0c", b2)])
                        P.add("dve", lambda e, b2=b2: e.tensor_scalar_add(out=st[b2][:, 4:5], in0=st[b2][:, 1:2], scalar1=LN_EPS),
                              reads=[("st0c", b2)], writes=[("st4c", b2)])
                        P.add("act", lambda e, b2=b2: e.activation(out=st[b2][:, 5:6], in_=st[b2][:, 4:5], func=AF.Ln),
                              reads=[("st4c", b2)], writes=[("st5c", b2)])
                        P.add("act", lambda e, b2=b2: e.activation(out=st[b2][:, 5:6], in_=st[b2][:, 5:6], func=AF.Exp, scale=-0.5),
                              reads=[("st5c", b2)], writes=[("st5c", b2)])
                        P.add("dve", lambda e, b2=b2: e.scalar_tensor_tensor(
                            out=st[b2][:, 6:7], in0=st[b2][:, 0:1], scalar=-1.0, in1=st[b2][:, 5:6], op0=ALU.mult, op1=ALU.mult),
                            reads=[("st0c", b2), ("st5c", b2)], writes=[("st6c", b2)])
                        P.add("act", lambda e, b=b, b2=b2: e.activation(out=xn2[b][:, :], in_=pre[b][:, :], func=AF.Identity,
                                                                       scale=st[b2][:, 5:6], bias=st[b2][:, 6:7]),
                              reads=[("pre", b, q4) for q4 in range(4)] + [("st5c", b2), ("st6c", b2)], writes=[("xn2", b)])
                        pending2.append(lambda: do_tail_b(g, b, r0))

                    def do_tail_b(g, b, r0):
                        for half in range(2):
                            pb = 2 * (g % 2) + half
                            eng = "act" if half == 0 else "dve"
                            for kk in range(8):
                                k = half * 8 + kk
                                P.add("pe", lambda e, b=b, pb=pb, k=k, kk=kk: e.transpose(
                                    out=ps_tr[pb][:, kk, :], in_=xn2[b][:, k * 128:(k + 1) * 128], identity=identb[:, :]),
                                    reads=[("xn2", b)], writes=[("ps_trc", pb)])
                            for kk in range(8):
                                k = half * 8 + kk
                                if eng == "act":
                                    fn = lambda e, pb=pb, k=k, kk=kk, b=b: e.activation(
                                        out=u2st[b][:, k, :], in_=ps_tr[pb][:, kk, :], func=AF.Identity,
                                        scale=scp1[:, 1, k, row:row + 1], bias=adaT[:, 48 + k, row:row + 1])
                                else:
                                    fn = lambda e, pb=pb, k=k, kk=kk, b=b: e.tensor_scalar(
                                        out=u2st[b][:, k, :], in0=ps_tr[pb][:, kk, :],
                                        scalar1=scp1[:, 1, k, row:row + 1], scalar2=adaT[:, 48 + k, row:row + 1],
                                        op0=ALU.mult, op1=ALU.add)
                                P.add(eng, fn, reads=[("ps_trc", pb)], writes=[("u2st", b, k)])
                        P.add("pool", lambda e, b=b, r0=r0: e.dma_start(
                            out=u2dst[:, r0:r0 + 128].rearrange("(k p) t -> p k t", p=128), in_=u2st[b][:, :, :]),
                            reads=[("u2st", b, k) for k in range(KC)], writes=[("u2T_s", g)], chan=("u2o", b))
                    pending.append(do_tail)
            while pending:
                pending.pop(0)()
            while pending2:
                pending2.pop(0)()

        if "C" in phases:
            with ExitStack() as esC:
                phase_c1(esC, "", S, S, 0, x_d, mixT_s, x1_s, u2T_s, wo=wo_p)
                P.barrier()
            if with_sample:
                with ExitStack() as esC:
                    phase_c1(esC, "s", 128, NS, 1, xs_d, mixTs_s, x1s_s, u2Ts_s, wo=wo_p)
                    P.barrier()
            if "B" in phases:
                esW.close()

        def phase_c2(esD, sfx, ntok, nvalid, row, u2src, x1src, ydst):
            def sbD(name, shape, dt):
                return esD.enter_context(nc.sbuf_tensor(name + sfx, list(shape), dt))
            TT = min(512, ntok)
            nsub = TT // 128
            u2 = sbD("u2", [128, KC, TT], BF16)
            hT = sbD("hT", [128, DFF // 128, TT], BF16)
            wu = [sbD("wu%d" % i, [128, KC, 512], BF16) for i in range(2)]
            wd = [sbD("wd%d" % i, [128, 8, 512], BF16) for i in range(2)]
            x1 = sbD("x1t", [128, nsub, D], F32)
            gbc = sbD("gmbc", [128, D], F32)
            lgbc = sbD("lgbc2", [128, D], F32)
            lbbc = sbD("lbbc2", [128, D], F32)
            bupT = sbD("bupT", [128, DFF // 128], F32)
            zt = [sbD("zt%d" % i, [128, TT], F32) for i in range(2)]
            tmp = [sbD("tmpd%d" % i, [128, 512], F32) for i in range(2)]
            st = [sbD("std%d" % i, [128, 8], F32) for i in range(2)]
            bst = [sbD("bstd%d" % i, [128, 24], F32) for i in range(2)]
            ps_up = [esD.enter_context(nc.psum_tensor("psD_up%d" % i + sfx, [128, 512], F32)) for i in range(3)]
            ps_dn = [esD.enter_context(nc.psum_tensor("psD_dn%d" % i + sfx, [128, 512], F32)) for i in range(4)]
            ps_b = esD.enter_context(nc.psum_tensor("psD_b" + sfx, [128, 512], F32))

            bcast_load(gbc, ada_s[row:row + 1, 5 * D:6 * D], "gmbc" + sfx)
            bcast_load(lgbc, lnv_d[2:3, :], "lgbc2" + sfx)
            bcast_load(lbbc, lnv_d[3:4, :], "lbbc2" + sfx)
            bur = sbD("bur", [DFF // 128, 128], F32)
            P.add("sp", lambda e: e.dma_start(out=bur[:, :], in_=b_up_d[0, :].rearrange("(f p) -> f p", p=128)),
                  writes=["bur"], chan="bur" + sfx)
            P.add("pe", lambda e: e.transpose(out=ps_b[:, 0:DFF // 128], in_=bur[:, :],
                                              identity=ident[0:DFF // 128, 0:DFF // 128]),
                  reads=["bur"], writes=["ps_b"])
            P.add("dve", lambda e: e.tensor_copy(out=bupT[:, :], in_=ps_b[:, 0:DFF // 128]),
                  reads=["ps_b"], writes=["ps_b", "bupT"])
            wuc = [0]
            wdc = [0]
            upc = [0]
            wu_ready = {}

            def issue_wu(f4):
                wi = wuc[0] % 2
                wuc[0] += 1
                P.add("pool", lambda e, wi=wi, f4=f4: e.dma_start(
                    out=wu[wi][:, :, :], in_=wup_s[:, f4 * 512:(f4 + 1) * 512].rearrange("(k p) c -> p k c", p=128)),
                    writes=[("wu", wi)], chan=("wu", wi))
                return wi

            def load_u2(t0):
                P.add("sp", lambda e, t0=t0: e.dma_start(
                    out=u2[:, :, :], in_=u2src[:, t0:t0 + TT].rearrange("(k p) t -> p k t", p=128)),
                    writes=["u2"], chan="u2" + sfx)

            load_u2(0)
            for t0 in range(0, ntok, TT):
                P.add("sp", lambda e, t0=t0: e.dma_start(
                    out=x1[:, :, :], in_=x1src[t0:t0 + TT, :].rearrange("(s p) d -> p s d", p=128)),
                    writes=[("x1t", s4, c4) for s4 in range(nsub) for c4 in range(4)], chan="x1t" + sfx)
                for f4 in range(DFF // 512):
                    if (t0, f4) in wu_ready:
                        wi = wu_ready[(t0, f4)]
                    else:
                        wi = issue_wu(f4)
                    for ff in range(4):
                        fc = f4 * 4 + ff
                        pu = upc[0] % 3
                        zi = upc[0] % 2
                        upc[0] += 1
                        for k in range(KC):
                            P.add("pe", lambda e, pu=pu, wi=wi, k=k, ff=ff: e.matmul(
                                ps_up[pu][:, 0:TT], lhsT=wu[wi][:, k, ff * 128:(ff + 1) * 128], rhs=u2[:, k, :],
                                start=(k == 0), stop=(k == KC - 1)),
                                reads=[("wu", wi), "u2"], writes=[("ps_up", pu)])
                        P.add("act", lambda e, pu=pu, zi=zi, fc=fc: e.activation(
                            out=zt[zi][:, :], in_=ps_up[pu][:, 0:TT], func=AF.Relu, bias=bupT[:, fc:fc + 1]),
                            reads=[("ps_up", pu), "bupT"], writes=[("ps_up", pu), ("zt", zi)])
                        P.add("dve", lambda e, zi=zi, fc=fc: e.tensor_mul(out=hT[:, fc, :], in0=zt[zi][:, :], in1=zt[zi][:, :]),
                              reads=[("zt", zi)], writes=[("hT", fc)])
                if t0 + TT < ntok:
                    load_u2(t0 + TT)
                for c4 in range(4):
                    for fg in range(DFF // 1024):
                        wi = wdc[0] % 2
                        wdc[0] += 1
                        P.add("pool", lambda e, wi=wi, fg=fg, c4=c4: e.dma_start(
                            out=wd[wi][:, :, :],
                            in_=wdn_s[fg * 1024:(fg + 1) * 1024, c4 * 512:(c4 + 1) * 512].rearrange("(k p) c -> p k c", p=128)),
                            writes=[("wd", wi)], chan=("wd", wi))
                        for kk in range(8):
                            fc = fg * 8 + kk
                            for s4 in range(nsub):
                                P.add("pe", lambda e, wi=wi, kk=kk, fc=fc, s4=s4: e.matmul(
                                    ps_dn[s4][:, :], lhsT=hT[:, fc, s4 * 128:(s4 + 1) * 128], rhs=wd[wi][:, kk, :],
                                    start=(fc == 0), stop=(fc == DFF // 128 - 1)),
                                    reads=[("wd", wi), ("hT", fc)], writes=[("ps_dn", s4)])
                    cs = slice(c4 * 512, (c4 + 1) * 512)
                    for s4 in range(nsub):
                        ti = (c4 * nsub + s4) % 2
                        P.add("dve", lambda e, s4=s4, ti=ti, cs=cs: e.tensor_mul(
                            out=tmp[ti][:, :], in0=ps_dn[s4][:, :], in1=gbc[:, cs]),
                            reads=[("ps_dn", s4), "gmbc" + sfx], writes=[("ps_dn", s4), ("tmpd", ti)])
                        P.add("dve", lambda e, s4=s4, ti=ti, cs=cs: e.scalar_tensor_tensor(
                            out=x1[:, s4, cs], in0=x1[:, s4, cs], scalar=ALPHA, in1=tmp[ti][:, :],
                            op0=ALU.mult, op1=ALU.add),
                            reads=[("x1t", s4, c4), ("tmpd", ti)], writes=[("x1t", s4, c4)])
                if t0 + TT < ntok:
                    for f4 in (0, 1):
                        wu_ready[(t0 + TT, f4)] = issue_wu(f4)
                for s4 in range(nsub):
                    b = s4 % 2
                    allk = [("x1t", s4, c4) for c4 in range(4)]
                    for q4 in range(4):
                        P.add("dve", lambda e, b=b, q4=q4, s4=s4: e.bn_stats(out=bst[b][:, q4 * 6:(q4 + 1) * 6],
                                                                            in_=x1[:, s4, q4 * 512:(q4 + 1) * 512]),
                              reads=[("x1t", s4, q4)], writes=[("bstd", b, q4)])
                    P.add("dve", lambda e, b=b: e.bn_aggr(out=st[b][:, 0:2], in_=bst[b][:, :]),
                          reads=[("bstd", b, q4) for q4 in range(4)], writes=[("st0d", b)])
                    P.add("dve", lambda e, b=b: e.tensor_scalar_add(out=st[b][:, 4:5], in0=st[b][:, 1:2], scalar1=LN_EPS),
                          reads=[("st0d", b)], writes=[("st4d", b)])
                    P.add("act", lambda e, b=b: e.activation(out=st[b][:, 5:6], in_=st[b][:, 4:5], func=AF.Ln),
                          reads=[("st4d", b)], writes=[("st5d", b)])
                    P.add("act", lambda e, b=b: e.activation(out=st[b][:, 5:6], in_=st[b][:, 5:6], func=AF.Exp, scale=-0.5),
                          reads=[("st5d", b)], writes=[("st5d", b)])
                    P.add("dve", lambda e, b=b: e.scalar_tensor_tensor(
                        out=st[b][:, 6:7], in0=st[b][:, 0:1], scalar=-1.0, in1=st[b][:, 5:6], op0=ALU.mult, op1=ALU.mult),
                        reads=[("st0d", b), ("st5d", b)], writes=[("st6d", b)])
                    P.add("act", lambda e, b=b, s4=s4: e.activation(out=x1[:, s4, :], in_=x1[:, s4, :], func=AF.Identity,
                                                                   scale=st[b][:, 5:6], bias=st[b][:, 6:7]),
                          reads=allk + [("st5d", b), ("st6d", b)], writes=allk)
                    P.add("pool", lambda e, s4=s4: e.tensor_mul(out=x1[:, s4, :], in0=x1[:, s4, :], in1=lgbc[:, :]),
                          reads=allk + ["lgbc2" + sfx], writes=allk)
                    P.add("pool", lambda e, s4=s4: e.tensor_add(out=x1[:, s4, :], in0=x1[:, s4, :], in1=lbbc[:, :]),
                          reads=allk + ["lbbc2" + sfx], writes=allk)
                    r0 = t0 + s4 * 128
                    nr = min(128, nvalid)
                    P.add("sp", lambda e, s4=s4, r0=r0, nr=nr: e.dma_start(out=ydst[r0:r0 + nr, :], in_=x1[0:nr, s4, :]),
                          reads=allk, chan=("yo", s4))

        if "D" in phases:
            with ExitStack() as esD:
                phase_c2(esD, "", S, S, 0, u2T_s, x1_s, y_o)
                P.barrier()
            if with_sample:
                with ExitStack() as esD:
                    phase_c2(esD, "s", 128, NS, 1, u2Ts_s, x1s_s, ys_o)
                    P.barrier()

        P.barrier()
        P.add("sp", lambda e: None)
        P.emit()
    return nc


def _consts():
    ident = np.eye(128, dtype=np.float32)
    tri = np.triu(np.ones((128, 128), dtype=np.float32))
    sel = np.zeros((128, 128), dtype=np.float32)
    sel[64, :] = 1.0
    jx = np.ascontiguousarray(np.eye(128, dtype=np.float32)[::-1])
    k = np.arange(128)[:, None]
    i = np.arange(128)[None, :]
    mka = 1.0 - ((k >= 64) & (i < 64)).astype(np.float32)
    mkb = 1.0 - ((k < 64) & (i >= 64)).astype(np.float32)
    return dict(ident=ident, tri=tri, sel64=sel, jx=jx, mka=mka, mkb=mkb)


def _gtab(rel_bias):
    m = np.arange(768)
    idx = np.clip(127 - m, -128, 128) + 128
    return np.ascontiguousarray(rel_bias[:, idx], dtype=np.float32)


S_FULL = 4096
PAST_FULL = 4096
_NC_CACHE = {}


def kernel(x_prompt, x_sample, cache_fox_k, cache_fox_v, cache_fox_logf, cache_band_k, cache_band_v,
           c_prompt, c_sample, w_ada, b_ada, w_in, b_forget, rel_bias, w_out, ln_mix_g, ln_mix_b,
           w_up, b_up, w_down, ln_mlp_g, ln_mlp_b):
    f = lambda a: np.ascontiguousarray(np.asarray(a), dtype=np.float32)
    B, S, _ = x_prompt.shape
    PAST = cache_fox_k.shape[2]
    key = (S, PAST)
    if key not in _NC_CACHE:
        _NC_CACHE[key] = build(S, PAST)
    nc = _NC_CACHE[key]
    shared = dict(w_ada=f(w_ada[0]), b_ada=f(b_ada), w_in=f(w_in[0]), b_forget=f(b_forget),
                  gtab=_gtab(np.asarray(rel_bias[0])), w_out=f(w_out[0]), w_up=f(w_up[0]), b_up=f(b_up),
                  w_down=f(w_down[0]),
                  lnv=f(np.stack([ln_mix_g[0], ln_mix_b[0], ln_mlp_g[0], ln_mlp_b[0]])))
    shared.update(_consts())
    in_maps = []
    for b in range(B):
        m = dict(shared)
        m.update(x_p=f(x_prompt[b]), x_s=f(x_sample[b]), c2=f(np.stack([c_prompt[b], c_sample[b]])),
                 cfk=f(cache_fox_k[0, b]).reshape(PAST, HF * DH), cfv=f(cache_fox_v[0, b]).reshape(PAST, HF * DH),
                 cfl=f(cache_fox_logf[0, b]), cbk=f(cache_band_k[0, b]).reshape(-1, HB * DH),
                 cbv=f(cache_band_v[0, b]).reshape(-1, HB * DH))
        in_maps.append(m)
    res = run_bass_kernel_spmd(nc, in_maps, core_ids=list(range(B)))
    R = res.results
    WB = min(512, S)
    T = x_sample.shape[1]
    g = lambda name, shape: np.stack([np.asarray(R[b][name], dtype=np.float32).reshape(shape) for b in range(B)])
    return (g("y_p", (S, D)), g("y_s", (T, D)),
            g("fox_k_p", (S, HF, DH))[None], g("fox_v_p", (S, HF, DH))[None], g("fox_logf_p", (S, HF))[None],
            g("band_k_p", (WB, HB, DH))[None], g("band_v_p", (WB, HB, DH))[None],
            g("fox_k_s", (T, HF, DH))[None], g("fox_v_s", (T, HF, DH))[None], g("fox_logf_s", (T, HF))[None],
            g("band_k_s", (T, HB, DH))[None], g("band_v_s", (T, HB, DH))[None])
```

```python
from contextlib import ExitStack

import numpy as np
import concourse.bass as bass
import concourse.mybir as mybir
from concourse.bass_utils import run_bass_kernel_spmd

F32 = mybir.dt.float32
BF16 = mybir.dt.bfloat16
AF = mybir.ActivationFunctionType
ALU = mybir.AluOpType
AX = mybir.AxisListType

D = 2048
DH = 128
HF = 8
HB = 8
NH = HF + HB
DFF = 8192
IN_COLS = 6152
KC = D // 128
ALPHA = 2.0 ** 0.25
LN_EPS = 1e-5
SCALE = DH ** -0.5
ENGS = ("pe", "act", "dve", "pool", "sp")
BLK = {"pe": "tensor", "act": "scalar", "dve": "vector", "pool": "gpsimd", "sp": "sync"}


class Op:
    __slots__ = ("eng", "fn", "chan", "deps", "sig", "ticket", "cval", "gv")

    def __init__(self, eng, fn, chan):
        self.eng = eng
        self.fn = fn
        self.chan = chan
        self.deps = ()
        self.sig = False
        self.ticket = 0
        self.cval = 0
        self.gv = None


class Prog:
    def __init__(self, nc):
        self.nc = nc
        self.ops = {e: [] for e in ENGS}
        self.wr = {}
        self.rd = {}
        self.chan_last = {}
        self.chan_cnt = {}
        self.bar = []
        self.last = {}

    def add(self, eng, fn, reads=(), writes=(), chan=None, group=False):
        op = Op(eng, fn, chan)
        deps = {}
        for k in reads:
            w = self.wr.get(k)
            if w is not None:
                deps[id(w)] = w
        for k in writes:
            w = self.wr.get(k)
            if w is not None:
                deps[id(w)] = w
            for r in self.rd.get(k, {}).values():
                deps[id(r)] = r
        for b in self.bar:
            deps[id(b)] = b
        if chan is not None:
            prev = self.chan_last.get(chan)
            if prev is not None and not group:
                deps[id(prev)] = prev
            c = self.chan_cnt.get(chan, 0) + 1
            self.chan_cnt[chan] = c
            op.cval = 16 * c
            if group and prev is not None:
                op.gv = prev.gv
                op.gv[0] = op.cval
            else:
                op.gv = [op.cval]
            self.chan_last[chan] = op
        dl = []
        for d in deps.values():
            if d is op:
                continue
            if d.chan is None and chan is None and d.eng == "pe" and eng == "pe":
                continue
            d.sig = True
            dl.append(d)
        op.deps = dl
        who = chan if chan is not None else eng
        for k in reads:
            self.rd.setdefault(k, {})[who] = op
        for k in writes:
            self.wr[k] = op
            self.rd[k] = {}
        self.ops[eng].append(op)
        self.last[who] = op
        return op

    def barrier(self):
        self.bar = list(self.last.values())
        self.wr = {}
        self.rd = {}

    def emit(self):
        nc = self.nc
        for e in ENGS:
            t = 0
            for op in self.ops[e]:
                if op.chan is None and op.sig:
                    t += 1
                    op.ticket = t
        with ExitStack() as es:
            esem = {e: es.enter_context(nc.semaphore("s_" + e)) for e in ENGS}
            csem = {c: es.enter_context(nc.semaphore("c_%d" % i))
                    for i, c in enumerate(self.chan_cnt)}
            block = es.enter_context(nc.Block())
            for e in ENGS:
                ops = self.ops[e]

                def body(eng, ops=ops, e=e):
                    waited = {}
                    for op in ops:
                        for d in op.deps:
                            if d.chan is not None:
                                sem, val = csem[d.chan], d.gv[0]
                            else:
                                sem, val = esem[d.eng], d.ticket
                            if waited.get(sem.num, 0) < val:
                                eng.wait_ge(sem, val)
                                waited[sem.num] = val
                        ins = op.fn(eng)
                        if ins is None:
                            continue
                        if op.chan is not None:
                            ins.then_inc(csem[op.chan], 16)
                        elif op.sig:
                            ins.then_inc(esem[e], 1)

                getattr(block, BLK[e])(body)


def _split_cols(n, step):
    return [(c, min(step, n - c)) for c in range(0, n, step)]


def build(S, PAST, with_sample=True, debug=False, phases="ABCD"):
    assert S % 512 == 0
    nc = bass.Bass("TRN2", target_bir_lowering=False)
    P = Prog(nc)
    NT = S // 128

    def din(name, shape, dt=F32):
        return nc.dram_tensor(name, list(shape), dt, kind="ExternalInput").ap()

    def dout(name, shape, dt=F32):
        return nc.dram_tensor(name, list(shape), dt, kind="ExternalOutput").ap()

    def dscr(name, shape, dt):
        return nc.dram_tensor(name, list(shape), dt, kind="ExternalOutput" if debug else "Internal").ap()

    x_d = din("x_p", [S, D])
    c2_d = din("c2", [2, D])
    w_ada_d = din("w_ada", [D, 6 * D])
    b_ada_d = din("b_ada", [1, 6 * D])
    w_in_d = din("w_in", [D, IN_COLS])
    b_f_d = din("b_forget", [1, HF])
    ident_d = din("ident", [128, 128])
    tri_d = din("tri", [128, 128])
    sel_d = din("sel64", [128, 128])
    jx_d = din("jx", [128, 128])
    mka_d = din("mka", [128, 128])
    mkb_d = din("mkb", [128, 128])
    g_d = din("gtab", [HB, 768])
    w_out_d = din("w_out", [D, D])
    w_up_d = din("w_up", [D, DFF])
    b_up_d = din("b_up", [1, DFF])
    w_dn_d = din("w_down", [DFF, D])
    lnv_d = din("lnv", [4, D])
    y_o = dout("y_p", [S, D])
    NP = PAST // 128
    NS = 64
    xs_d = din("x_s", [NS, D])
    cfk_d = din("cfk", [PAST, HF * DH])
    cfv_d = din("cfv", [PAST, HF * DH])
    cfl_d = din("cfl", [PAST, HF])
    cbk_d = din("cbk", [512, HB * DH])
    cbv_d = din("cbv", [512, HB * DH])
    ys_o = dout("y_s", [NS, D])
    fks_o = dout("fox_k_s", [NS, HF * DH])
    fvs_o = dout("fox_v_s", [NS, HF * DH])
    fls_o = dout("fox_logf_s", [NS, HF])
    bks_o = dout("band_k_s", [NS, HB * DH])
    bvs_o = dout("band_v_s", [NS, HB * DH])

    fk_o = dout("fox_k_p", [S, HF * DH])
    fv_o = dout("fox_v_p", [S, HF * DH])
    fl_o = dout("fox_logf_p", [S, HF])
    WB = min(512, S)
    bk_o = dout("band_k_p", [WB, HB * DH])
    bv_o = dout("band_v_p", [WB, HB * DH])

    qT_s = dscr("qT_s", [NH, 128, S], BF16)
    kT_s = dscr("kT_s", [NH, 128, S], BF16)
    v_s = dscr("v_s", [NH, 128, NT, DH], BF16)
    ada_s = dscr("ada_s", [2, 6 * D], F32)
    mixT_s = dscr("mixT_s", [D, S], BF16)
    u2T_s = dscr("u2T_s", [D, S], BF16)
    x1_s = dscr("x1_s", [S, D], F32)
    wup_s = dscr("wup_s", [D, DFF], BF16)
    wdn_s = dscr("wdn_s", [DFF, D], BF16)
    qTs_s = dscr("qTs_s", [NH, 128, 128], BF16)
    kTfs_s = dscr("kTfs_s", [HF, 128, (NP + 1) * 128], BF16)
    kTbs_s = dscr("kTbs_s", [HB, 128, 5 * 128], BF16)
    vfs_s = dscr("vfs_s", [HF, 128, NP + 1, DH], BF16)
    vbs_s = dscr("vbs_s", [HB, 128, 5, DH], BF16)
    mixTs_s = dscr("mixTs_s", [D, 128], BF16)
    u2Ts_s = dscr("u2Ts_s", [D, 128], BF16)
    x1s_s = dscr("x1s_s", [128, D], F32)

    es = ExitStack()
    with es:
        def sb(name, shape, dt):
            return es.enter_context(nc.sbuf_tensor(name, list(shape), dt))

        ident = sb("ident_sb", [128, 128], F32)
        identb = sb("identb", [128, 128], BF16)
        tri = sb("tri_sb", [128, 128], F32)
        onesf = sb("onesf", [128, 128], F32)
        sel64 = sb("sel64_sb", [128, 128], F32)
        adaT = sb("adaT", [128, 96, 2], F32)
        scp1 = sb("scp1", [128, 2, KC, 2], F32)
        bfor = sb("bfor", [128, HF], F32)
        Fall = sb("Fall", [128, NT, HF], F32)
        Rall = sb("Rall", [128, NT, HF], F32)
        carry = sb("carry", [128, HF], F32)
        carryS = sb("carryS", [128, HF], F32)
        FallS = sb("FallS", [128, NP + 1, HF], F32)
        RallS = sb("RallS", [128, NP + 1, HF], F32)

        P.add("sp", lambda e: e.dma_start(out=ident[:, :], in_=ident_d[:, :]), writes=["c0"], chan="const")
        P.add("sp", lambda e: e.dma_start(out=tri[:, :], in_=tri_d[:, :]), writes=["c1"], chan="const", group=True)
        P.add("sp", lambda e: e.dma_start(out=sel64[:, :], in_=sel_d[:, :]), writes=["c2"], chan="const", group=True)
        P.add("sp", lambda e: e.dma_start(out=bfor[:, :], in_=b_f_d[0:1, :].to_broadcast([128, HF])),
              writes=["c3"], chan="const", group=True)
        P.add("dve", lambda e: e.tensor_copy(out=identb[:, :], in_=ident[:, :]), reads=["c0"], writes=["c4"])
        P.add("dve", lambda e: e.memset(onesf[:, :], 1.0), writes=["c5"])
        P.add("dve", lambda e: e.memset(carry[:, :], 0.0), writes=["carry"])
        trib = sb("trib", [128, 128], BF16)
        jx = sb("jx_sb", [128, 128], F32)
        mka = sb("mka_sb", [128, 128], F32)
        mkb = sb("mkb_sb", [128, 128], F32)
        P.add("sp", lambda e: e.dma_start(out=jx[:, :], in_=jx_d[:, :]), writes=["c6"], chan="const", group=True)
        P.add("sp", lambda e: e.dma_start(out=mka[:, :], in_=mka_d[:, :]), writes=["c7"], chan="const", group=True)
        P.add("sp", lambda e: e.dma_start(out=mkb[:, :], in_=mkb_d[:, :]), writes=["c8"], chan="const", group=True)
        P.add("dve", lambda e: e.tensor_copy(out=trib[:, :], in_=tri[:, :]), reads=["c1"], writes=["c9"])
        onesb = sb("onesb", [128, 128], BF16)
        P.add("dve", lambda e: e.memset(onesb[:, :], 1.0), writes=["c10"])

        def s0_logf(esS):
            clf = esS.enter_context(nc.sbuf_tensor("clf", [128, NP, HF], F32))
            ps_f = esS.enter_context(nc.psum_tensor("psS_f", [128, 512], F32))
            P.add("dve", lambda e: e.memset(carryS[:, :], 0.0), writes=["carryS"])
            P.add("sp", lambda e: e.dma_start(out=clf[:, :, :], in_=cfl_d[:, :].rearrange("(t p) h -> p t h", p=128)),
                  writes=["clf"], chan="clf")
            for t in range(NP):
                P.add("pe", lambda e, t=t: e.matmul(ps_f[:, 64:72], lhsT=tri[:, :], rhs=clf[:, t, :], start=True, stop=True),
                      reads=["clf"], writes=["psS_f"])
                P.add("pe", lambda e, t=t: e.matmul(ps_f[:, 80:88], lhsT=onesf[:, :], rhs=clf[:, t, :], start=True, stop=True),
                      reads=["clf"], writes=["psS_f"])
                P.add("dve", lambda e, t=t: e.tensor_add(out=FallS[:, t, :], in0=ps_f[:, 64:72], in1=carryS[:, :]),
                      reads=["psS_f", "carryS"], writes=["psS_f", ("FallS", t)])
                P.add("dve", lambda e: e.tensor_add(out=carryS[:, :], in0=ps_f[:, 80:88], in1=carryS[:, :]),
                      reads=["psS_f", "carryS"], writes=["psS_f", "carryS"])

        def s0_kv_jobs(esS, ps_k):
            kc = [esS.enter_context(nc.sbuf_tensor("kc%d" % i, [128, HF * DH], BF16)) for i in range(2)]
            kst = [esS.enter_context(nc.sbuf_tensor("kst%d" % i, [128, 8, 128], BF16)) for i in range(2)]
            jobs = []

            def vjob(dstv, srcv, nt, h):
                def f():
                    for t4 in range(0, nt, 4):
                        P.add("pool", lambda e, t4=t4: e.dma_start(
                            out=dstv[h, :, t4:t4 + 4, :],
                            in_=srcv[t4 * 128:(t4 + 4) * 128, h * DH:(h + 1) * DH].rearrange("(t p) d -> p t d", p=128)),
                            chan="vcache", group=True)
                return f

            def kload(n):
                (src, dstT, t) = kj[n]
                b = n % 2
                P.add("pool", lambda e: e.dma_start(out=kc[b][:, :], in_=src[t * 128:(t + 1) * 128, :]),
                      writes=[("kc", b)], chan=("kc", b))

            def kjob(n, src, dstT, t):
                def f():
                    b = n % 2
                    if n == 0:
                        kload(0)
                    for hh in range(8):
                        P.add("pe", lambda e, hh=hh: e.transpose(
                            out=ps_k[:, hh, :], in_=kc[b][:, hh * 128:(hh + 1) * 128], identity=identb[:, :]),
                            reads=[("kc", b)], writes=["ps_k"])
                    P.add("dve", lambda e: e.tensor_copy(out=kst[b][:, :, :], in_=ps_k[:, :, :]),
                          reads=["ps_k"], writes=["ps_k", ("kst", b)])
                    P.add("sp", lambda e: e.dma_start(
                        out=dstT[:, :, t * 128:(t + 1) * 128].rearrange("h p t -> p h t"), in_=kst[b][:, :, :]),
                        reads=[("kst", b)], chan=("kst", b))
                    if n + 1 < len(kj):
                        kload(n + 1)
                return f
            kj = [(cfk_d, kTfs_s, t) for t in range(NP)] + [(cbk_d, kTbs_s, t) for t in range(4)]
            vj = [vjob(vfs_s, cfv_d, NP, h) for h in range(HF)] + [vjob(vbs_s, cbv_d, 4, h) for h in range(HB)]
            for n, (src, dstT, t) in enumerate(kj):
                jobs.append(kjob(n, src, dstT, t))
                if n % 2 == 1 and vj:
                    jobs.append(vj.pop(0))
            jobs.extend(vj)
            return jobs

        def phase_ada(blocks, es0):
            def sb0(name, shape, dt):
                return es0.enter_context(nc.sbuf_tensor(name, list(shape), dt))
            csb = sb0("csb", [2, D], F32)
            ctmp = sb0("ctmp", [2, D], F32)
            sT = sb0("sT", [128, KC, 2], BF16)
            wab = [sb0("wab%d" % i, [128, KC, 512], BF16) for i in range(2)]
            bab = [sb0("bab%d" % i, [2, 512], F32) for i in range(2)]
            blk = [sb0("ablk%d" % i, [2, 512], F32) for i in range(2)]
            ps_t = es0.enter_context(nc.psum_tensor("ps_sT", [128, 512], F32))
            ps_a = [es0.enter_context(nc.psum_tensor("ps_a%d" % i, [128, 512], F32)) for i in range(2)]
            ps_at = es0.enter_context(nc.psum_tensor("ps_at", [128, 512], F32))

            P.add("sp", lambda e: e.dma_start(out=csb[:, :], in_=c2_d[:, :]), writes=["csb"], chan="cld")
            P.add("act", lambda e: e.activation(out=ctmp[:, :], in_=csb[:, :], func=AF.Exp, scale=-1.0),
                  reads=["csb"], writes=["ctmp"])
            P.add("dve", lambda e: e.tensor_scalar_add(out=ctmp[:, :], in0=ctmp[:, :], scalar1=1.0),
                  reads=["ctmp"], writes=["ctmp"])
            P.add("dve", lambda e: e.reciprocal(out=ctmp[:, :], in_=ctmp[:, :]), reads=["ctmp"], writes=["ctmp"])
            P.add("dve", lambda e: e.tensor_mul(out=ctmp[:, :], in0=ctmp[:, :], in1=csb[:, :]),
                  reads=["ctmp", "csb"], writes=["ctmp"])
            for k in range(KC):
                P.add("pe", lambda e, k=k: e.transpose(out=ps_t[:, 2 * k:2 * k + 2],
                                                       in_=ctmp[:, k * 128:(k + 1) * 128],
                                                       identity=ident[0:2, 0:2]),
                      reads=["ctmp"], writes=["ps_t"])
            P.add("dve", lambda e: e.tensor_copy(out=sT[:, :, :],
                                                 in_=ps_t[:, 0:2 * KC].rearrange("p (k r) -> p k r", r=2)),
                  reads=["ps_t"], writes=["sT"])
            for i, cb in enumerate(blocks):
                s = i % 2
                c0 = cb * 512
                P.add("pool", lambda e, s=s, c0=c0: e.dma_start(
                    out=wab[s][:, :, :], in_=w_ada_d[:, c0:c0 + 512].rearrange("(k p) c -> p k c", p=128)),
                    writes=[("wab", s)], chan=("wab", s))
                P.add("sp", lambda e, s=s, c0=c0: e.dma_start(
                    out=bab[s][:, :], in_=b_ada_d[0:1, c0:c0 + 512].to_broadcast([2, 512])),
                    writes=[("bab", s)], chan=("bab", s))
                for k in range(KC):
                    P.add("pe", lambda e, s=s, k=k: e.matmul(ps_a[s][0:2, :], lhsT=sT[:, k, :], rhs=wab[s][:, k, :],
                                                            start=(k == 0), stop=(k == KC - 1)),
                          reads=[("wab", s), "sT"], writes=[("ps_a", s)])
                P.add("dve", lambda e, s=s: e.tensor_add(out=blk[s][:, :], in0=ps_a[s][0:2, :], in1=bab[s][:, :]),
                      reads=[("ps_a", s), ("bab", s)], writes=[("ps_a", s), ("blk", s)])
                P.add("sp", lambda e, s=s, c0=c0: e.dma_start(out=ada_s[:, c0:c0 + 512], in_=blk[s][:, :]),
                      reads=[("blk", s)], writes=[("ada_s", cb)], chan=("ablk", s))
                for j in range(4):
                    jj = cb * 4 + j
                    P.add("pe", lambda e, s=s, j=j, jj=jj: e.transpose(
                        out=ps_at[:, 2 * (jj % 96):2 * (jj % 96) + 2], in_=blk[s][:, j * 128:(j + 1) * 128],
                        identity=ident[0:2, 0:2]),
                        reads=[("blk", s)], writes=["ps_at"])
                P.add("dve", lambda e, cb=cb: e.tensor_copy(
                    out=adaT[:, cb * 4:cb * 4 + 4, :],
                    in_=ps_at[:, 8 * cb:8 * cb + 8].rearrange("p (j r) -> p j r", r=2)),
                    reads=["ps_at"], writes=["ps_at", ("adaT", cb)])

        with ExitStack() as es0:
            phase_ada(list(range(24)), es0)
            P.add("dve", lambda e: e.tensor_scalar_add(out=scp1[:, 0, :, :], in0=adaT[:, 16:32, :], scalar1=1.0),
                  reads=[("adaT", cb) for cb in range(4, 8)], writes=["scp1a"])
            P.add("dve", lambda e: e.tensor_scalar_add(out=scp1[:, 1, :, :], in0=adaT[:, 64:80, :], scalar1=1.0),
                  reads=[("adaT", cb) for cb in range(16, 20)], writes=["scp1m"])
            if with_sample:
                s0_logf(es0)
            P.barrier()

        def phase_a(esA, sfx, xsrc, ntok, nvalid, row, outs, dst, Fd, Rd, gs_off, carry):
            def sbA(name, shape, dt):
                return esA.enter_context(nc.sbuf_tensor(name + sfx, list(shape), dt))
            TA = min(1024, ntok)
            QW = min(512, TA)
            nsubT = TA // 128
            xt = [sbA("xt%d" % i, [128, D], F32) for i in range(2)]
            xn = [sbA("xn%d" % i, [128, D], BF16) for i in range(2)]
            st = [sbA("st%d" % i, [128, 8], F32) for i in range(2)]
            bst = [sbA("bst%d" % i, [128, 24], F32) for i in range(2)]
            fs = sbA("fs", [128, 16], F32)
            uT = [sbA("uT%d" % i, [128, KC, TA], BF16) for i in range(2)]
            wb = [sbA("wb%d" % i, [128, KC, 512], BF16) for i in range(2)]
            wf = sbA("wf", [128, KC, HF], BF16)
            qst = [sbA("qst%d" % i, [128, 512], BF16) for i in range(2)]
            kvf = [sbA("kvf%d" % i, [128, 512], F32) for i in range(2)]
            kvb = [sbA("kvb%d" % i, [128, 512], BF16) for i in range(2)]
            kTst = [sbA("kTst%d" % i, [128, 4, 128], BF16) for i in range(2)]
            lf = [sbA("lf%d" % i, [128, 4, HF], F32) for i in range(2)]
            ps_tr = [esA.enter_context(nc.psum_tensor("psA_tr%d" % i + sfx, [128, 8, 128], BF16)) for i in range(4)]
            ps_mm = [esA.enter_context(nc.psum_tensor("psA_mm%d" % i + sfx, [128, 512], F32)) for i in range(2)]
            ps_f = esA.enter_context(nc.psum_tensor("psA_f" + sfx, [128, 512], F32))
            ps_kt = esA.enter_context(nc.psum_tensor("psA_kt" + sfx, [128, 8, 128], BF16))

            P.add("pool", lambda e: e.dma_start(
                out=wf[:, :, :], in_=w_in_d[:, 3072:3080].rearrange("(k p) c -> p k c", p=128)),
                writes=["wf"], chan="wf")

            cblocks = [(0, "q"), (512, "q"), (1024, "k"), (1536, "k"), (2048, "v"), (2560, "v"),
                       (3080, "q"), (3592, "q"), (4104, "k"), (4616, "k"), (5128, "v"), (5640, "v")]
            wcnt = [0]
            mmcnt = [0]
            stc = [0]
            sub_global = [0]
            kpend = []

            def ln_sub(t0, s):
                ub = (t0 // TA) % 2
                U = uT[ub]
                g = sub_global[0]
                sub_global[0] += 1
                b = g % 2
                r0 = t0 + s * 128
                if nvalid < 128:
                    P.add("pool", lambda e, b=b: e.memset(xt[b][:, :], 0.0), writes=[("xt", b)])
                    P.add("sp", lambda e, b=b: e.dma_start(out=xt[b][0:nvalid, :], in_=xsrc[0:nvalid, :]),
                          writes=[("xt", b)], chan=("xt", b))
                else:
                    P.add("sp", lambda e, b=b, r0=r0: e.dma_start(out=xt[b][:, :], in_=xsrc[r0:r0 + 128, :]),
                          writes=[("xt", b)], chan=("xt", b))
                for q4 in range(4):
                    P.add("dve", lambda e, b=b, q4=q4: e.bn_stats(out=bst[b][:, q4 * 6:(q4 + 1) * 6],
                                                                 in_=xt[b][:, q4 * 512:(q4 + 1) * 512]),
                          reads=[("xt", b)], writes=[("bst", b, q4)])
                P.add("dve", lambda e, b=b: e.bn_aggr(out=st[b][:, 0:2], in_=bst[b][:, :]),
                      reads=[("bst", b, q4) for q4 in range(4)], writes=[("st0", b)])
                P.add("dve", lambda e, b=b: e.tensor_scalar_add(out=st[b][:, 4:5], in0=st[b][:, 1:2], scalar1=LN_EPS),
                      reads=[("st0", b)], writes=[("st4", b)])
                P.add("act", lambda e, b=b: e.activation(out=st[b][:, 5:6], in_=st[b][:, 4:5], func=AF.Ln),
                      reads=[("st4", b)], writes=[("st5", b)])
                P.add("act", lambda e, b=b: e.activation(out=st[b][:, 5:6], in_=st[b][:, 5:6], func=AF.Exp, scale=-0.5),
                      reads=[("st5", b)], writes=[("st5", b)])
                P.add("dve", lambda e, b=b: e.scalar_tensor_tensor(
                    out=st[b][:, 6:7], in0=st[b][:, 0:1], scalar=-1.0, in1=st[b][:, 5:6],
                    op0=ALU.mult, op1=ALU.mult),
                    reads=[("st0", b), ("st5", b)], writes=[("st6", b)])
                P.add("act", lambda e, b=b: e.activation(out=xn[b][:, :], in_=xt[b][:, :], func=AF.Identity,
                                                         scale=st[b][:, 5:6], bias=st[b][:, 6:7]),
                      reads=[("xt", b), ("st5", b), ("st6", b)], writes=[("xn", b)])
                def back():
                    for half in range(2):
                        pb = 2 * (g % 2) + half
                        eng = "act" if half == 0 else "dve"
                        for kk in range(8):
                            k = half * 8 + kk
                            P.add("pe", lambda e, b=b, pb=pb, k=k, kk=kk: e.transpose(
                                out=ps_tr[pb][:, kk, :], in_=xn[b][:, k * 128:(k + 1) * 128], identity=identb[:, :]),
                                reads=[("xn", b)], writes=[("ps_tr", pb)])
                        for kk in range(8):
                            k = half * 8 + kk
                            if eng == "act":
                                fn = lambda e, pb=pb, k=k, kk=kk, s=s, U=U: e.activation(
                                    out=U[:, k, s * 128:(s + 1) * 128], in_=ps_tr[pb][:, kk, :], func=AF.Identity,
                                    scale=scp1[:, 0, k, row:row + 1], bias=adaT[:, k, row:row + 1])
                            else:
                                fn = lambda e, pb=pb, k=k, kk=kk, s=s, U=U: e.tensor_scalar(
                                    out=U[:, k, s * 128:(s + 1) * 128], in0=ps_tr[pb][:, kk, :],
                                    scalar1=scp1[:, 0, k, row:row + 1], scalar2=adaT[:, k, row:row + 1],
                                    op0=ALU.mult, op1=ALU.add)
                            P.add(eng, fn, reads=[("ps_tr", pb)], writes=[("uT", ub, s, k)])
                return back

            def forget_sub(t0, s):
                ub = (t0 // TA) % 2
                U = uT[ub]
                if True:
                    gs = (t0 // 128) + s
                    lb = (gs // 4) % 2
                    for k in range(KC):
                        P.add("pe", lambda e, k=k, s=s, U=U: e.matmul(
                            ps_f[:, 0:HF], lhsT=U[:, k, s * 128:(s + 1) * 128], rhs=wf[:, k, :],
                            start=(k == 0), stop=(k == KC - 1)),
                            reads=[("uT", ub, s, k), "wf"], writes=["ps_f"])
                    L = lf[lb][:, gs % 4, :]
                    P.add("dve", lambda e: e.tensor_add(out=fs[:, 0:8], in0=ps_f[:, 0:HF], in1=bfor[:, :]),
                          reads=["ps_f"], writes=["ps_f", "fs0"])
                    P.add("dve", lambda e: e.tensor_scalar_mul(out=fs[:, 8:16], in0=fs[:, 0:8], scalar1=-1.0),
                          reads=["fs0"], writes=["fs1"])
                    P.add("dve", lambda e: e.tensor_tensor(out=fs[:, 8:16], in0=fs[:, 8:16], in1=fs[:, 0:8], op=ALU.min),
                          reads=["fs0", "fs1"], writes=["fs1"])
                    P.add("act", lambda e: e.activation(out=fs[:, 8:16], in_=fs[:, 8:16], func=AF.Exp),
                          reads=["fs1"], writes=["fs1"])
                    P.add("act", lambda e: e.activation(out=fs[:, 8:16], in_=fs[:, 8:16], func=AF.Ln, bias=1.0),
                          reads=["fs1"], writes=["fs1"])
                    P.add("dve", lambda e: e.tensor_scalar_min(out=fs[:, 0:8], in0=fs[:, 0:8], scalar1=0.0),
                          reads=["fs0"], writes=["fs0"])
                    P.add("dve", lambda e, L=L: e.tensor_sub(out=L, in0=fs[:, 0:8], in1=fs[:, 8:16]),
                          reads=["fs0", "fs1"], writes=[("lf", lb, gs % 4)])
                    if nvalid < 128:
                        P.add("sp", lambda e, lb=lb, gs=gs: e.dma_start(
                            out=outs["logf"][0:nvalid, :], in_=lf[lb][0:nvalid, gs % 4, :]),
                            reads=[("lf", lb, gs % 4)], chan=("lfo", lb))
                    elif gs % 4 == 3:
                        P.add("sp", lambda e, lb=lb, gs=gs: e.dma_start(
                            out=outs["logf"][(gs - 3) * 128:(gs + 1) * 128, :].rearrange("(s p) h -> p s h", p=128),
                            in_=lf[lb][:, :, :]),
                            reads=[("lf", lb, i) for i in range(4)], chan=("lfo", lb))
                    P.add("pe", lambda e, L=L: e.matmul(ps_f[:, 64:72], lhsT=tri[:, :], rhs=L, start=True, stop=True),
                          reads=[("lf", lb, gs % 4)], writes=["ps_f"])
                    P.add("pe", lambda e, L=L: e.matmul(ps_f[:, 80:88], lhsT=onesf[:, :], rhs=L, start=True, stop=True),
                          reads=[("lf", lb, gs % 4)], writes=["ps_f"])
                    P.add("dve", lambda e, gs=gs: e.tensor_add(out=Fd[:, gs_off + gs, :], in0=ps_f[:, 64:72], in1=carry[:, :]),
                          reads=["ps_f", "carry"], writes=["ps_f", ("Fall", gs)])
                    P.add("dve", lambda e: e.tensor_add(out=carry[:, :], in0=ps_f[:, 80:88], in1=carry[:, :]),
                          reads=["ps_f", "carry"], writes=["ps_f", "carry"])
                    P.add("pe", lambda e, gs=gs: e.matmul(ps_f[:, 96:104], lhsT=sel64[:, :], rhs=Fd[:, gs_off + gs, :],
                                                          start=True, stop=True),
                          reads=[("Fall", gs)], writes=["ps_f"])
                    P.add("dve", lambda e, gs=gs: e.tensor_copy(out=Rd[:, gs_off + gs, :], in_=ps_f[:, 96:104]),
                          reads=["ps_f"], writes=["ps_f", ("Rall", gs)])

            def colblock(t0, c0, kind):
                ub = (t0 // TA) % 2
                U = uT[ub]
                wi = wcnt[0] % 2
                wcnt[0] += 1
                P.add("pool", lambda e, wi=wi, c0=c0: e.dma_start(
                    out=wb[wi][:, :, :], in_=w_in_d[:, c0:c0 + 512].rearrange("(k p) c -> p k c", p=128)),
                    writes=[("wb", wi)], chan=("wb", wi))
                h0 = (c0 // 512) * 4 if c0 < 3072 else HF + ((c0 - 3080) // 512) * 4
                h0 = h0 % 8 + (8 if c0 >= 3080 else 0)
                if kind == "q":
                    for hh in range(4):
                        h = h0 + hh
                        for tb in range(TA // QW):
                            pm = mmcnt[0] % 2
                            mmcnt[0] += 1
                            for k in range(KC):
                                P.add("pe", lambda e, pm=pm, wi=wi, k=k, hh=hh, tb=tb, U=U: e.matmul(
                                    ps_mm[pm][:, 0:QW], lhsT=wb[wi][:, k, hh * 128:(hh + 1) * 128],
                                    rhs=U[:, k, tb * QW:(tb + 1) * QW], start=(k == 0), stop=(k == KC - 1)),
                                    reads=[("wb", wi)] + [("uT", ub, tb * (QW // 128) + i, k) for i in range(QW // 128)],
                                    writes=[("ps_mm", pm)])
                            qi = stc[0] % 2
                            stc[0] += 1
                            P.add("act", lambda e, pm=pm, qi=qi: e.activation(
                                out=qst[qi][:, 0:QW], in_=ps_mm[pm][:, 0:QW], func=AF.Copy),
                                reads=[("ps_mm", pm)], writes=[("ps_mm", pm), ("qst", qi)])
                            qdst = dst["q"](h, t0 + tb * QW, QW)
                            P.add("sp", lambda e, qi=qi, qdst=qdst: e.dma_start(out=qdst, in_=qst[qi][:, 0:QW]),
                                  reads=[("qst", qi)], writes=[("qT_s", h)], chan=("qst", qi))
                else:
                    for s in range(nsubT):
                        gs = (t0 // 128) + s
                        pm = mmcnt[0] % 2
                        mmcnt[0] += 1
                        for k in range(KC):
                            P.add("pe", lambda e, pm=pm, wi=wi, k=k, s=s, U=U: e.matmul(
                                ps_mm[pm][:, :], lhsT=U[:, k, s * 128:(s + 1) * 128], rhs=wb[wi][:, k, :],
                                start=(k == 0), stop=(k == KC - 1)),
                                reads=[("wb", wi), ("uT", ub, s, k)], writes=[("ps_mm", pm)])
                        while kpend:
                            kpend.pop(0)()
                        qi = stc[0] % 2
                        stc[0] += 1
                        P.add("act", lambda e, pm=pm, qi=qi: e.activation(
                            out=kvf[qi][:, :], in_=ps_mm[pm][:, :], func=AF.Copy),
                            reads=[("ps_mm", pm)], writes=[("kvf", qi)])
                        P.add("dve", lambda e, pm=pm, qi=qi: e.tensor_copy(out=kvb[qi][:, :], in_=ps_mm[pm][:, :]),
                              reads=[("ps_mm", pm)], writes=[("ps_mm", pm), ("kvb", qi)])
                        fox = c0 < 3072
                        cc = (c0 - (1024 if kind == "k" else 2048)) if fox else (c0 - (4104 if kind == "k" else 5128))
                        nr = min(128, nvalid)
                        if fox:
                            odst = outs["fk" if kind == "k" else "fv"][gs * 128:gs * 128 + nr, cc:cc + 512]
                        elif gs * 128 >= ntok - outs["wb"]:
                            rr = gs * 128 - (ntok - outs["wb"])
                            odst = outs["bk" if kind == "k" else "bv"][rr:rr + nr, cc:cc + 512]
                        else:
                            odst = None
                        if odst is not None:
                            P.add("sp", lambda e, qi=qi, odst=odst, nr=nr: e.dma_start(out=odst, in_=kvf[qi][0:nr, :]),
                                  reads=[("kvf", qi)], chan=("kvf", qi))
                        if kind == "v":
                            vdst = dst["v"](h0, gs)
                            P.add("sp", lambda e, qi=qi, vdst=vdst: e.dma_start(
                                out=vdst, in_=kvb[qi][:, :].rearrange("p (h d) -> p h d", d=DH)),
                                reads=[("kvb", qi)], writes=[("v_s", h0)], chan=("kvb", qi))
                        else:
                            ki = stc[0] % 2

                            def ktail(qi=qi, ki=ki, h0=h0, gs=gs):
                                for hh in range(4):
                                    P.add("pe", lambda e, qi=qi, hh=hh: e.transpose(
                                        out=ps_kt[:, hh, :], in_=kvb[qi][:, hh * 128:(hh + 1) * 128], identity=identb[:, :]),
                                        reads=[("kvb", qi)], writes=["ps_kt"])
                                P.add("dve", lambda e, ki=ki: e.tensor_copy(out=kTst[ki][:, :, :], in_=ps_kt[:, 0:4, :]),
                                      reads=["ps_kt"], writes=["ps_kt", ("kTst", ki)])
                                kdst = dst["kt"](h0, gs)
                                P.add("sp", lambda e, ki=ki, kdst=kdst: e.dma_start(out=kdst, in_=kTst[ki][:, :, :]),
                                      reads=[("kTst", ki)], writes=[("kT_s", h0)], chan=("kTst", ki))
                            kpend.append(ktail)
                while kpend:
                    kpend.pop(0)()

            tiles = list(range(0, ntok, TA))
            for s in range(nsubT):
                ln_sub(tiles[0], s)()
            for ti, t0 in enumerate(tiles):
                pend_back = None
                for ci, (c0, kind) in enumerate(cblocks):
                    colblock(t0, c0, kind)
                    if ci < nsubT:
                        forget_sub(t0, ci)
                    if pend_back is not None:
                        pend_back()
                        pend_back = None
                    if ti + 1 < len(tiles) and ci < nsubT:
                        pend_back = ln_sub(tiles[ti + 1], ci)
                if pend_back is not None:
                    pend_back()

        dstP = dict(
            q=lambda h, a, w: qT_s[h, :, a:a + w],
            kt=lambda h0, gs: kT_s[h0:h0 + 4, :, gs * 128:(gs + 1) * 128].rearrange("h p t -> p h t"),
            v=lambda h0, gs: v_s[h0:h0 + 4, :, gs, :].rearrange("h p d -> p h d"))
        with ExitStack() as esA:
            phase_a(esA, "", x_d, S, S, 0, dict(fk=fk_o, fv=fv_o, logf=fl_o, bk=bk_o, bv=bv_o, wb=WB),
                    dstP, Fall, Rall, 0, carry)
            P.barrier()

        def q_s(h, a, w):
            return qTs_s[h, :, 0:w]

        def kt_s(h0, gs):
            if h0 < HF:
                return kTfs_s[h0:h0 + 4, :, NP * 128:(NP + 1) * 128].rearrange("h p t -> p h t")
            return kTbs_s[h0 - HF:h0 - HF + 4, :, 4 * 128:5 * 128].rearrange("h p t -> p h t")

        def v_ss(h0, gs):
            if h0 < HF:
                return vfs_s[h0:h0 + 4, :, NP, :].rearrange("h p d -> p h d")
            return vbs_s[h0 - HF:h0 - HF + 4, :, 4, :].rearrange("h p d -> p h d")

        if with_sample:
            with ExitStack() as esA:
                phase_a(esA, "s", xs_d, 128, NS, 1, dict(fk=fks_o, fv=fvs_o, logf=fls_o, bk=bks_o, bv=bvs_o, wb=128),
                        dict(q=q_s, kt=kt_s, v=v_ss), FallS, RallS, NP, carryS)
                P.barrier()


        def phase_b(esB, sfx, heads, jobs_fn=None):
            def sbB(name, shape, dt):
                return esB.enter_context(nc.sbuf_tensor(name + sfx, list(shape), dt))
            NKM = max(cfg(h)["NK"] for (h, cfg, _m, _p) in heads)
            NQM = max(len(cfg(h)["qtiles"]) for (h, cfg, _m, _p) in heads)
            qT = [sbB("qT%d" % i, [128, NQM * 128], BF16) for i in range(2)]
            kT = [sbB("kT%d" % i, [128, NKM * 128], BF16) for i in range(2)]
            vA = [sbB("vA%d" % i, [128, NKM, DH], BF16) for i in range(2)]
            biasT = sbB("biasT", [128, NQM, NKM], F32)
            Eb = sbB("Eb", [128, HB, 5, 128], F32)
            xh = [sbB("xh%d" % i, [128, 128], F32) for i in range(2)]
            NPB = 5
            Pb = [sbB("Pb%d" % i, [128, 512], BF16) for i in range(NPB)]
            recb = [sbB("recb%d" % i, [128, 512], F32) for i in range(2)]
            dcp = [sbB("dcp%d" % i, [128, 512], F32) for i in range(2)]
            rcp = [sbB("rcp%d" % i, [128, 512], F32) for i in range(2)]
            acc = [sbB("acc%d" % i, [128, 512], F32) for i in range(2)]
            NGM = (NQM + 3) // 4
            biasF = sbB("biasF", [128, NGM, NKM], F32)
            cjt = sbB("cjt", [128, NGM, 4], F32)
            mst = [sbB("mst%d" % i, [128, 512], BF16) for i in range(2)]
            ps_s = [esB.enter_context(nc.psum_tensor("psB_s%d" % i + sfx, [128, 512], F32)) for i in range(3)]
            ps_k = esB.enter_context(nc.psum_tensor("psB_k" + sfx, [128, 8, 128], BF16))
            jobs = jobs_fn(esB, ps_k) if jobs_fn is not None else []
            ps_o = [esB.enter_context(nc.psum_tensor("psB_o%d" % i + sfx, [128, 512], F32)) for i in range(2)]
            ps_r = [esB.enter_context(nc.psum_tensor("psB_r%d" % i + sfx, [128, 512], F32)) for i in range(2)]

            e_left = [0]
            if any(h >= HF for (h, _c, _m, _p) in heads):
                def ejob(hb):
                    def f():
                        for jj in range(5):
                            cnt = hb * 5 + jj
                            xi = cnt % 2
                            si = cnt % 3
                            src = bass.AP(tensor=g_d.tensor, offset=hb * 768 + 128 * jj, ap=[[1, 128], [1, 128]])
                            P.add("sp", lambda e, xi=xi, src=src: e.dma_start(out=xh[xi][:, :], in_=src),
                                  writes=[("xh", xi)], chan=("xh", xi))
                            P.add("pe", lambda e, xi=xi, si=si: e.matmul(ps_s[si][:, 0:128], lhsT=jx[:, :], rhs=xh[xi][:, :],
                                                                        start=True, stop=True),
                                  reads=[("xh", xi)], writes=[("ps_s", si)])
                            P.add("act", lambda e, si=si, jj=jj: e.activation(
                                out=Eb[:, hb, jj, :], in_=ps_s[si][:, 0:128], func=AF.Exp),
                                reads=[("ps_s", si)], writes=[("ps_s", si), ("Eb", hb)])
                            if jj in (0, 4):
                                mk = mka if jj == 0 else mkb
                                P.add("pool", lambda e, jj=jj, mk=mk: e.tensor_mul(
                                    out=Eb[:, hb, jj, :], in0=Eb[:, hb, jj, :], in1=mk[:, :]),
                                    reads=[("Eb", hb)], writes=[("Eb", hb)])
                        e_left[0] -= 1
                    return f
                ej = [ejob(hb) for hb in range(HB)]
                e_left[0] = len(ej)
                jobs = ej + jobs

            gcount = [0]
            ocount = [0]
            prev_part = None
            loaded = set()

            def emit_loads(hi):
                (h_, cfg_, _m, _p) = heads[hi]
                cf = cfg_(h_)
                hb2 = hi % 2
                NK = cf["NK"]
                nq = len(cf["qtiles"])
                loaded.add(hi)
                P.add("sp", lambda e: e.dma_start(out=qT[hb2][:, 0:nq * 128], in_=cf["qsrc"]),
                      writes=[("qT", hb2)], chan=("qkv", hb2))
                P.add("sp", lambda e: e.dma_start(out=kT[hb2][:, 0:NK * 128], in_=cf["ksrc"]),
                      writes=[("kT", hb2)], chan=("qkv", hb2), group=True)
                P.add("sp", lambda e: e.dma_start(out=vA[hb2][:, 0:NK, :], in_=cf["vsrc"]),
                      writes=[("vA", hb2)], chan=("qkv", hb2), group=True)

            for hi, (h, cfg, mixdst, part) in enumerate(heads):
                if prev_part is not None and part != prev_part:
                    while jobs:
                        jobs.pop(0)()
                    P.barrier()
                prev_part = part
                hb2 = hi % 2
                fox = h < HF
                if not fox:
                    while e_left[0] > 0:
                        jobs.pop(0)()
                cf = cfg(h)
                NK = cf["NK"]
                qtiles = cf["qtiles"]
                qoff = qtiles[0]
                Fd, Rd = cf["Fd"], cf["Rd"]
                nq = len(qtiles)
                if hi not in loaded:
                    emit_loads(hi)
                if fox:
                    for j in qtiles:
                        P.add("dve", lambda e, j=j, h=h, qoff=qoff, Fd=Fd, Rd=Rd: e.tensor_scalar(
                            out=biasT[:, j - qoff, 0:j + 1], in0=Fd[:, 0:j + 1, h], scalar1=-1.0, scalar2=Rd[:, j, h:h + 1],
                            op0=ALU.mult, op1=ALU.add),
                            writes=[("biasT", j - qoff)])
                steps = []
                for g0 in range(0, nq, 4):
                    G = qtiles[g0:g0 + 4]
                    j0, j1 = G[0], G[-1]
                    gi = g0 // 4
                    if fox and j0 > 0:
                        P.add("dve", lambda e, gi=gi, j0=j0, h=h, Fd=Fd, Rd=Rd: e.tensor_scalar(
                            out=biasF[:, gi, 0:j0], in0=Fd[:, 0:j0, h], scalar1=-1.0, scalar2=Rd[:, j0, h:h + 1],
                            op0=ALU.mult, op1=ALU.add),
                            writes=[("biasF", gi)])
                        P.add("dve", lambda e, gi=gi, j0=j0, ng=len(G), h=h, Rd=Rd: e.tensor_scalar(
                            out=cjt[:, gi, 0:ng], in0=Rd[:, j0:j0 + ng, h], scalar1=Rd[:, j0, h:h + 1], scalar2=None,
                            op0=ALU.subtract),
                            writes=[("cj", gi)])
                        P.add("act", lambda e, gi=gi, ng=len(G): e.activation(
                            out=cjt[:, gi, 0:ng], in_=cjt[:, gi, 0:ng], func=AF.Exp),
                            reads=[("cj", gi)], writes=[("cj", gi)])
                        for kb in range(j0, j1 + 1):
                            steps.append((j0, len(G), kb, kb, j1, kb == j0, kb == j1, "d", gi))
                        for kb in range(0, j0):
                            steps.append((j0, len(G), kb, j0, j1, kb == 0, kb == j0 - 1, "f", gi))
                        continue
                    kbs = list(range(0, j1 + 1)) if fox else list(range(max(0, j0 - 4), j1 + 1))
                    for kb in kbs:
                        ja = max(j0, kb)
                        jb = j1 if fox else min(j1, kb + 4)
                        steps.append((j0, len(G), kb, ja, jb, kb == kbs[0], kb == kbs[-1], "n", gi))
                pend = []

                def emit_pv(item, hb2=hb2, h=h, qoff=qoff, mixdst=mixdst):
                    (j0, ng, kb, ja, jb, first, last, pslot, ob, typ, gi) = item
                    n = jb - ja + 1
                    c0 = (ja - j0) * 128
                    P.add("pe", lambda e: e.matmul(
                        ps_o[ob][:, c0:c0 + n * 128], lhsT=vA[hb2][:, kb, :], rhs=Pb[pslot][:, 0:n * 128],
                        start=first, stop=last, skip_group_check=True),
                        reads=[("Pb", pslot, i) for i in range(n)] + [("vA", hb2)], writes=[("ps_o", ob)])
                    P.add("pe", lambda e: e.matmul(
                        ps_r[ob][:, c0:c0 + n * 128], lhsT=onesb[:, :], rhs=Pb[pslot][:, 0:n * 128],
                        start=first, stop=last, skip_group_check=True),
                        reads=[("Pb", pslot, i) for i in range(n)], writes=[("ps_r", ob)])
                    if last and typ == "d":
                        w = ng * 128
                        g2 = gi % 2
                        P.add("act", lambda e: e.activation(out=dcp[g2][:, 0:w], in_=ps_o[ob][:, 0:w], func=AF.Copy),
                              reads=[("ps_o", ob)], writes=[("ps_o", ob), ("dcp", g2)])
                        P.add("act", lambda e: e.activation(out=rcp[g2][:, 0:w], in_=ps_r[ob][:, 0:w], func=AF.Copy),
                              reads=[("ps_r", ob)], writes=[("ps_r", ob), ("rcp", g2)])
                    elif last and typ == "f":
                        w = ng * 128
                        g2 = gi % 2
                        for i in range(ng):
                            cs = slice(i * 128, (i + 1) * 128)
                            P.add("dve", lambda e, i=i, cs=cs: e.scalar_tensor_tensor(
                                out=acc[g2][:, cs], in0=ps_o[ob][:, cs], scalar=cjt[:, gi, i:i + 1], in1=dcp[g2][:, cs],
                                op0=ALU.mult, op1=ALU.add),
                                reads=[("ps_o", ob), ("cj", gi), ("dcp", g2)], writes=[("ps_o", ob), ("acc", g2, i)])
                            P.add("dve", lambda e, i=i, cs=cs: e.scalar_tensor_tensor(
                                out=recb[g2][:, cs], in0=ps_r[ob][:, cs], scalar=cjt[:, gi, i:i + 1], in1=rcp[g2][:, cs],
                                op0=ALU.mult, op1=ALU.add),
                                reads=[("ps_r", ob), ("cj", gi), ("rcp", g2)], writes=[("ps_r", ob), ("recq", g2, i)])
                        P.add("dve", lambda e: e.reciprocal(out=recb[g2][:, 0:w], in_=recb[g2][:, 0:w]),
                              reads=[("recq", g2, i) for i in range(ng)], writes=[("recb", g2)] + [("recq", g2, i) for i in range(ng)])
                        P.add("dve", lambda e: e.tensor_mul(out=mst[g2][:, 0:w], in0=acc[g2][:, 0:w], in1=recb[g2][:, 0:w]),
                              reads=[("acc", g2, i) for i in range(ng)] + [("recb", g2)], writes=[("mst", g2)])
                        md = mixdst(h, j0 - qoff, ng)
                        P.add("sp", lambda e: e.dma_start(out=md, in_=mst[g2][:, 0:w]),
                              reads=[("mst", g2)], writes=[("mixT_s", h)], chan=("mst", g2))
                        if jobs:
                            jobs.pop(0)()
                    elif last:
                        w = ng * 128
                        if h < HF:
                            P.add("dve", lambda e: e.reciprocal(out=recb[ob][:, 0:w], in_=ps_r[ob][:, 0:w]),
                                  reads=[("ps_r", ob)], writes=[("ps_r", ob), ("recb", ob)])
                        else:
                            P.add("act", lambda e: e.activation(out=recb[ob][:, 0:w], in_=ps_r[ob][:, 0:w], func=AF.Ln),
                                  reads=[("ps_r", ob)], writes=[("ps_r", ob), ("recb", ob)])
                            P.add("act", lambda e: e.activation(out=recb[ob][:, 0:w], in_=recb[ob][:, 0:w], func=AF.Exp,
                                                                scale=-1.0),
                                  reads=[("recb", ob)], writes=[("recb", ob)])
                        P.add("dve", lambda e: e.tensor_mul(out=mst[ob][:, 0:w], in0=ps_o[ob][:, 0:w], in1=recb[ob][:, 0:w]),
                              reads=[("ps_o", ob), ("recb", ob)], writes=[("ps_o", ob), ("mst", ob)])
                        md = mixdst(h, j0 - qoff, ng)
                        P.add("sp", lambda e: e.dma_start(out=md, in_=mst[ob][:, 0:w]),
                              reads=[("mst", ob)], writes=[("mixT_s", h)], chan=("mst", ob))
                        if jobs:
                            jobs.pop(0)()

                nstep = 0
                for (j0, ng, kb, ja, jb, first, last, typ, gi) in steps:
                    si = gcount[0] % 3
                    pslot = gcount[0] % NPB
                    gcount[0] += 1
                    if typ == "n":
                        if first:
                            ocount[0] += 1
                        ob = ocount[0] % 2
                    else:
                        ob = 0 if typ == "d" else 1
                    n = jb - ja + 1
                    P.add("pe", lambda e, si=si, kb=kb, ja=ja, jb=jb, n=n, hb2=hb2, qoff=qoff: e.matmul(
                        ps_s[si][:, 0:n * 128], lhsT=kT[hb2][:, kb * 128:(kb + 1) * 128],
                        rhs=qT[hb2][:, (ja - qoff) * 128:(jb + 1 - qoff) * 128], start=True, stop=True),
                        reads=[("kT", hb2), ("qT", hb2)], writes=[("ps_s", si)])
                    if fox and typ == "f":
                        P.add("act", lambda e, si=si, n=n, kb=kb, gi=gi, pslot=pslot: e.activation(
                            out=Pb[pslot][:, 0:n * 128], in_=ps_s[si][:, 0:n * 128], func=AF.Exp, scale=SCALE,
                            bias=biasF[:, gi, kb:kb + 1]),
                            reads=[("ps_s", si), ("biasF", gi)], writes=[("Pb", pslot, i) for i in range(n)])
                    elif fox:
                        for j in range(ja, jb + 1):
                            i = j - ja
                            P.add("act", lambda e, si=si, i=i, kb=kb, j=j, pslot=pslot, qoff=qoff: e.activation(
                                out=Pb[pslot][:, i * 128:(i + 1) * 128], in_=ps_s[si][:, i * 128:(i + 1) * 128],
                                func=AF.Exp, scale=SCALE, bias=biasT[:, j - qoff, kb:kb + 1]),
                                reads=[("ps_s", si), ("biasT", j - qoff)], writes=[("Pb", pslot, i)])
                        if ja == kb:
                            P.add("pool", lambda e, pslot=pslot: e.tensor_mul(
                                out=Pb[pslot][:, 0:128], in0=Pb[pslot][:, 0:128], in1=trib[:, :]),
                                reads=[("Pb", pslot, 0)], writes=[("Pb", pslot, 0)])
                    else:
                        P.add("act", lambda e, si=si, n=n, pslot=pslot: e.activation(
                            out=Pb[pslot][:, 0:n * 128], in_=ps_s[si][:, 0:n * 128], func=AF.Exp, scale=SCALE),
                            reads=[("ps_s", si)], writes=[("Pb", pslot, i) for i in range(n)])
                        P.add("dve" if gcount[0] % 2 == 0 else "pool", lambda e, pslot=pslot, n=n, kb=kb, ja=ja, jb=jb, h=h: e.tensor_mul(
                            out=Pb[pslot][:, 0:n * 128], in0=Pb[pslot][:, 0:n * 128],
                            in1=Eb[:, h - HF, ja - kb:jb - kb + 1, :].rearrange("p a b -> p (a b)")),
                            reads=[("Pb", pslot, i) for i in range(n)] + [("Eb", h - HF)],
                            writes=[("Pb", pslot, i) for i in range(n)])
                    pend.append((j0, ng, kb, ja, jb, first, last, pslot, ob, typ, gi))
                    if len(pend) > 2:
                        emit_pv(pend.pop(0))
                    nstep += 1
                    if nstep == 3 and hi + 1 < len(heads) and heads[hi + 1][3] == part and (hi + 1) not in loaded:
                        emit_loads(hi + 1)
                while pend:
                    emit_pv(pend.pop(0))

        def bcast_load(dst, src_row, chan):
            P.add("sp", lambda e: e.dma_start(out=dst[:, :], in_=src_row.to_broadcast([128, D])),
                  writes=[chan], chan=chan)

        def load_wo(es_, sfx, row, defer=None):
            wo = es_.enter_context(nc.sbuf_tensor("wo" + sfx, [128, KC, D], BF16))

            def emit():
                for c4 in range(4):
                    P.add("pool", lambda e, c4=c4: e.dma_start(
                        out=wo[:, :, c4 * 512:(c4 + 1) * 512],
                        in_=w_out_d[:, c4 * 512:(c4 + 1) * 512].rearrange("(k p) c -> p k c", p=128)),
                        writes=[("wo", c4)], chan=("wo", c4))
            if defer is not None:
                defer.append(emit)
            else:
                emit()
            return wo

        def cfgP(h):
            return dict(NK=NT, qtiles=list(range(NT)), qsrc=qT_s[h, :, :], ksrc=kT_s[h, :, :], vsrc=v_s[h, :, :, :],
                        Fd=Fall, Rd=Rall)

        def cfgS(h):
            if h < HF:
                return dict(NK=NP + 1, qtiles=[NP], qsrc=qTs_s[h, :, :], ksrc=kTfs_s[h, :, :], vsrc=vfs_s[h, :, :, :],
                            Fd=FallS, Rd=RallS)
            return dict(NK=5, qtiles=[4], qsrc=qTs_s[h, :, :], ksrc=kTbs_s[h - HF, :, :], vsrc=vbs_s[h - HF, :, :, :],
                        Fd=FallS, Rd=RallS)

        wcast_jobs = []
        if "D" in phases:
            def wc_up(r):
                return lambda: P.add("pool", lambda e: e.dma_start(out=wup_s[r * 128:(r + 1) * 128, :],
                                                                   in_=w_up_d[r * 128:(r + 1) * 128, :]),
                                     chan="wcast", group=True)

            def wc_dn(r):
                return lambda: P.add("pool", lambda e: e.dma_start(
                    out=wdn_s[r * 512:(r + 1) * 512, :].rearrange("(a p) c -> p a c", p=128),
                    in_=w_dn_d[r * 512:(r + 1) * 512, :].rearrange("(a p) c -> p a c", p=128)),
                    chan="wcast", group=True)
            for r in range(D // 128):
                wcast_jobs.append(wc_up(r))
                wcast_jobs.append(wc_dn(r))
        if "B" in phases:
            mdP = lambda h, j0, n: mixT_s[h * 128:(h + 1) * 128, j0 * 128:(j0 + n) * 128]
            mdS = lambda h, j0, n: mixTs_s[h * 128:(h + 1) * 128, j0 * 128:(j0 + n) * 128]
            hl = [(h, cfgP, mdP, 0) for h in range(NH)]
            if with_sample:
                hl += [(h, cfgS, mdS, 1) for h in range(NH)]
            esW = ExitStack()
            pre_jobs = []
            wo_p = load_wo(esW, "", 0, defer=pre_jobs) if "C" in phases else None

            def all_jobs(esS, ps_k):
                sj = s0_kv_jobs(esS, ps_k) if with_sample else []
                out = list(pre_jobs)
                wj = list(wcast_jobs)
                while sj or wj:
                    if wj:
                        out.append(wj.pop(0))
                    if sj:
                        out.append(sj.pop(0))
                return out
            with ExitStack() as esB:
                phase_b(esB, "", hl, all_jobs)
                P.barrier()

        def ln_tokmajor(pre, stt, bstt, tag, q):
            for q4 in range(4):
                P.add("dve", lambda e, q4=q4: e.bn_stats(out=bstt[:, q4 * 6:(q4 + 1) * 6], in_=pre[:, q4 * 512:(q4 + 1) * 512]),
                      reads=[(tag, q)], writes=[("bst" + tag, q, q4)])
            P.add("dve", lambda e: e.bn_aggr(out=stt[:, 0:2], in_=bstt[:, :]),
                  reads=[("bst" + tag, q, q4) for q4 in range(4)], writes=[("st0" + tag, q)])
            P.add("dve", lambda e: e.tensor_scalar_add(out=stt[:, 4:5], in0=stt[:, 1:2], scalar1=LN_EPS),
                  reads=[("st0" + tag, q)], writes=[("st4" + tag, q)])
            P.add("act", lambda e: e.activation(out=stt[:, 5:6], in_=stt[:, 4:5], func=AF.Ln),
                  reads=[("st4" + tag, q)], writes=[("st5" + tag, q)])
            P.add("act", lambda e: e.activation(out=stt[:, 5:6], in_=stt[:, 5:6], func=AF.Exp, scale=-0.5),
                  reads=[("st5" + tag, q)], writes=[("st5" + tag, q)])
            P.add("dve", lambda e: e.scalar_tensor_tensor(
                out=stt[:, 6:7], in0=stt[:, 0:1], scalar=-1.0, in1=stt[:, 5:6], op0=ALU.mult, op1=ALU.mult),
                reads=[("st0" + tag, q), ("st5" + tag, q)], writes=[("st6" + tag, q)])

        def phase_c1(esC, sfx, ntok, nvalid, row, xsrc, mixsrc, x1dst, u2dst, wo=None):
            def sbC(name, shape, dt):
                return esC.enter_context(nc.sbuf_tensor(name + sfx, list(shape), dt))
            TB = min(512, ntok)
            NS4 = TB // 128
            if wo is None:
                wo = load_wo(esC, sfx, row)
            gbc = sbC("gbc", [128, D], F32)
            lgbc = sbC("lgbc", [128, D], F32)
            lbbc = sbC("lbbc", [128, D], F32)
            mT = [sbC("mT%d" % i, [128, KC, TB], BF16) for i in range(2)]
            xt = [sbC("xtc%d" % i, [128, D], F32) for i in range(2)]
            pre = [sbC("pre%d" % i, [128, D], F32) for i in range(2)]
            tmp = [sbC("tmpc%d" % i, [128, 512], F32) for i in range(2)]
            xn2 = [sbC("xn2%d" % i, [128, D], BF16) for i in range(2)]
            u2st = [sbC("u2st%d" % i, [128, KC, 128], BF16) for i in range(2)]
            st = [sbC("stc%d" % i, [128, 8], F32) for i in range(4)]
            bst = [sbC("bstc%d" % i, [128, 24], F32) for i in range(4)]
            ps_mm = [esC.enter_context(nc.psum_tensor("psC_mm%d" % i + sfx, [128, 512], F32)) for i in range(4)]
            ps_tr = [esC.enter_context(nc.psum_tensor("psC_tr%d" % i + sfx, [128, 8, 128], BF16)) for i in range(4)]

            bcast_load(gbc, ada_s[row:row + 1, 2 * D:3 * D], "gbc" + sfx)
            bcast_load(lgbc, lnv_d[0:1, :], "lgbc" + sfx)
            bcast_load(lbbc, lnv_d[1:2, :], "lbbc" + sfx)
            mm = [0]
            pending = []
            pending2 = []
            for tb in range(ntok // TB):
                mb = tb % 2
                P.add("sp", lambda e, mb=mb, tb=tb: e.dma_start(
                    out=mT[mb][:, :, :], in_=mixsrc[:, tb * TB:(tb + 1) * TB].rearrange("(k p) t -> p k t", p=128)),
                    writes=[("mT", mb)], chan=("mT", mb))
                for s4 in range(NS4):
                    g = tb * NS4 + s4
                    b = g % 2
                    r0 = g * 128
                    if nvalid < 128:
                        P.add("pool", lambda e, b=b: e.memset(xt[b][:, :], 0.0), writes=[("xtc", b)])
                        P.add("sp", lambda e, b=b: e.dma_start(out=xt[b][0:nvalid, :], in_=xsrc[0:nvalid, :]),
                              writes=[("xtc", b)], chan=("xtc", b))
                    else:
                        P.add("sp", lambda e, b=b, r0=r0: e.dma_start(out=xt[b][:, :], in_=xsrc[r0:r0 + 128, :]),
                              writes=[("xtc", b)], chan=("xtc", b))
                    for c4 in range(4):
                        pm = mm[0] % 4
                        mm[0] += 1
                        for k in range(KC):
                            P.add("pe", lambda e, pm=pm, mb=mb, k=k, s4=s4, c4=c4: e.matmul(
                                ps_mm[pm][:, :], lhsT=mT[mb][:, k, s4 * 128:(s4 + 1) * 128],
                                rhs=wo[:, k, c4 * 512:(c4 + 1) * 512], start=(k == 0), stop=(k == KC - 1)),
                                reads=[("mT", mb), ("wo", c4)], writes=[("ps_mm", pm)])
                        ti = mm[0] % 2
                        cs = slice(c4 * 512, (c4 + 1) * 512)
                        P.add("dve", lambda e, pm=pm, ti=ti, cs=cs: e.tensor_mul(
                            out=tmp[ti][:, :], in0=ps_mm[pm][:, :], in1=gbc[:, cs]),
                            reads=[("ps_mm", pm), "gbc" + sfx], writes=[("ps_mm", pm), ("tmpc", ti)])
                        P.add("dve", lambda e, b=b, ti=ti, cs=cs: e.scalar_tensor_tensor(
                            out=pre[b][:, cs], in0=xt[b][:, cs], scalar=ALPHA, in1=tmp[ti][:, :],
                            op0=ALU.mult, op1=ALU.add),
                            reads=[("xtc", b), ("tmpc", ti)], writes=[("pre", b, c4)])
                    while pending2:
                        pending2.pop(0)()
                    while pending:
                        pending.pop(0)()
                    for q4 in range(4):
                        P.add("dve", lambda e, b=b, q4=q4: e.bn_stats(out=bst[b][:, q4 * 6:(q4 + 1) * 6],
                                                                     in_=pre[b][:, q4 * 512:(q4 + 1) * 512]),
                              reads=[("pre", b, q4)], writes=[("bstc", b, q4)])
                    P.add("dve", lambda e, b=b: e.bn_aggr(out=st[b][:, 0:2], in_=bst[b][:, :]),
                          reads=[("bstc", b, q4) for q4 in range(4)], writes=[("st0c", b)])
                    P.add("dve", lambda e, b=b: e.tensor_scalar_add(out=st[b][:, 4:5], in0=st[b][:, 1:2], scalar1=LN_EPS),
                          reads=[("st0c", b)], writes=[("st4c", b)])
                    P.add("act", lambda e, b=b: e.activation(out=st[b][:, 5:6], in_=st[b][:, 4:5], func=AF.Ln),
                          reads=[("st4c", b)], writes=[("st5c", b)])
                    P.add("act", lambda e, b=b: e.activation(out=st[b][:, 5:6], in_=st[b][:, 5:6], func=AF.Exp, scale=-0.5),
                          reads=[("st5c", b)], writes=[("st5c", b)])
                    P.add("dve", lambda e, b=b: e.scalar_tensor_tensor(
                        out=st[b][:, 6:7], in0=st[b][:, 0:1], scalar=-1.0, in1=st[b][:, 5:6], op0=ALU.mult, op1=ALU.mult),
                        reads=[("st0c", b), ("st5c", b)], writes=[("st6c", b)])
                    P.add("act", lambda e, b=b: e.activation(out=pre[b][:, :], in_=pre[b][:, :], func=AF.Identity,
                                                             scale=st[b][:, 5:6], bias=st[b][:, 6:7]),
                          reads=[("pre", b, q4) for q4 in range(4)] + [("st5c", b), ("st6c", b)],
                          writes=[("pre", b, q4) for q4 in range(4)])
                    P.add("pool", lambda e, b=b: e.tensor_mul(out=pre[b][:, :], in0=pre[b][:, :], in1=lgbc[:, :]),
                          reads=[("pre", b, q4) for q4 in range(4)] + ["lgbc" + sfx], writes=[("pre", b, q4) for q4 in range(4)])
                    P.add("pool", lambda e, b=b: e.tensor_add(out=pre[b][:, :], in0=pre[b][:, :], in1=lbbc[:, :]),
                          reads=[("pre", b, q4) for q4 in range(4)] + ["lbbc" + sfx], writes=[("pre", b, q4) for q4 in range(4)])
                    P.add("pool", lambda e, b=b, r0=r0: e.dma_start(out=x1dst[r0:r0 + 128, :], in_=pre[b][:, :]),
                          reads=[("pre", b, q4) for q4 in range(4)], writes=[("x1_s", g)], chan=("x1o", b))
                    def do_tail(g=g, b=b, r0=r0):
                        b2 = 2 + b
                        for q4 in range(4):
                            P.add("dve", lambda e, b=b, b2=b2, q4=q4: e.bn_stats(out=bst[b2][:, q4 * 6:(q4 + 1) * 6],
                                                                                in_=pre[b][:, q4 * 512:(q4 + 1) * 512]),
                                  reads=[("pre", b, q4)], writes=[("bstc", b2, q4)])
                        P.add("dve", lambda e, b2=b2: e.bn_aggr(out=st[b2][:, 0:2], in_=bst[b2][:, :]),
                              reads=[("bstc", b2, q4) for q4 in range(4)], writes=[("st0c", b2)])
                        P.add("dve", lambda e, b2=b2: e.tensor_scalar_add(out=st[b2][:, 4:5], in0=st[b2][:, 1:2], scalar1=LN_EPS),
                              reads=[("st0c", b2)], writes=[("st4c", b2)])
                        P.add("act", lambda e, b2=b2: e.activation(out=st[b2][:, 5:6], in_=st[b2][:, 4:5], func=AF.Ln),
                              reads=[("st4c", b2)], writes=[("st5c", b2)])
                        P.add("act", lambda e, b2=b2: e.activation(out=st[b2][:, 5:6], in_=st[b2][:, 5:6], func=AF.Exp, scale=-0.5),
                              reads=[("st5c", b2)], writes=[("st5c", b2)])
                        P.add("dve", lambda e, b2=b2: e.scalar_tensor_tensor(
                            out=st[b2][:, 6:7], in0=st[b2][:, 0:1], scalar=-1.0, in1=st[b2][:, 5:6], op0=ALU.mult, op1=ALU.mult),
                            reads=[("st0c", b2), ("st5c", b2)], writes=[("st6c", b2)])
                        P.add("act", lambda e, b=b, b2=b2: e.activation(out=xn2[b][:, :], in_=pre[b][:, :], func=AF.Identity,
                                                                       scale=st[b2][:, 5:6], bias=st[b2][:, 6:7]),
                              reads=[("pre", b, q4) for q4 in range(4)] + [("st5c", b2), ("st6c", b2)], writes=[("xn2", b)])
                        pending2.append(lambda: do_tail_b(g, b, r0))

                    def do_tail_b(g, b, r0):
                        for half in range(2):
                            pb = 2 * (g % 2) + half
                            eng = "act"
                            for kk in range(8):
                                k = half * 8 + kk
                                P.add("pe", lambda e, b=b, pb=pb, k=k, kk=kk: e.transpose(
                                    out=ps_tr[pb][:, kk, :], in_=xn2[b][:, k * 128:(k + 1) * 128], identity=identb[:, :]),
                                    reads=[("xn2", b)], writes=[("ps_trc", pb)])
                            for kk in range(8):
                                k = half * 8 + kk
                                if eng == "act":
                                    fn = lambda e, pb=pb, k=k, kk=kk, b=b: e.activation(
                                        out=u2st[b][:, k, :], in_=ps_tr[pb][:, kk, :], func=AF.Identity,
                                        scale=scp1[:, 1, k, row:row + 1], bias=adaT[:, 48 + k, row:row + 1])
                                else:
                                    fn = lambda e, pb=pb, k=k, kk=kk, b=b: e.tensor_scalar(
                                        out=u2st[b][:, k, :], in0=ps_tr[pb][:, kk, :],
                                        scalar1=scp1[:, 1, k, row:row + 1], scalar2=adaT[:, 48 + k, row:row + 1],
                                        op0=ALU.mult, op1=ALU.add)
                                P.add(eng, fn, reads=[("ps_trc", pb)], writes=[("u2st", b, k)])
                        P.add("pool", lambda e, b=b, r0=r0: e.dma_start(
                            out=u2dst[:, r0:r0 + 128].rearrange("(k p) t -> p k t", p=128), in_=u2st[b][:, :, :]),
                            reads=[("u2st", b, k) for k in range(KC)], writes=[("u2T_s", g)], chan=("u2o", b))
                    pending.append(do_tail)
            while pending:
                pending.pop(0)()
            while pending2:
                pending2.pop(0)()

        if "C" in phases:
            with ExitStack() as esC:
                phase_c1(esC, "", S, S, 0, x_d, mixT_s, x1_s, u2T_s, wo=wo_p)
                P.barrier()
            if with_sample:
                with ExitStack() as esC:
                    phase_c1(esC, "s", 128, NS, 1, xs_d, mixTs_s, x1s_s, u2Ts_s, wo=wo_p)
                    P.barrier()
            if "B" in phases:
                esW.close()

        def phase_c2(esD, sfx, ntok, nvalid, row, u2src, x1src, ydst):
            def sbD(name, shape, dt):
                return esD.enter_context(nc.sbuf_tensor(name + sfx, list(shape), dt))
            TT = min(512, ntok)
            nsub = TT // 128
            u2 = sbD("u2", [128, KC, TT], BF16)
            hT = sbD("hT", [128, DFF // 128, TT], BF16)
            wu = [sbD("wu%d" % i, [128, KC, 512], BF16) for i in range(2)]
            wd = [sbD("wd%d" % i, [128, 8, 512], BF16) for i in range(2)]
            x1 = sbD("x1t", [128, nsub, D], F32)
            gbc = sbD("gmbc", [128, D], F32)
            lgbc = sbD("lgbc2", [128, D], F32)
            lbbc = sbD("lbbc2", [128, D], F32)
            bupT = sbD("bupT", [128, DFF // 128], F32)
            zt = [sbD("zt%d" % i, [128, TT], F32) for i in range(2)]
            tmp = [sbD("tmpd%d" % i, [128, 512], F32) for i in range(2)]
            st = [sbD("std%d" % i, [128, 8], F32) for i in range(2)]
            bst = [sbD("bstd%d" % i, [128, 24], F32) for i in range(2)]
            ps_up = [esD.enter_context(nc.psum_tensor("psD_up%d" % i + sfx, [128, 512], F32)) for i in range(3)]
            ps_dn = [esD.enter_context(nc.psum_tensor("psD_dn%d" % i + sfx, [128, 512], F32)) for i in range(4)]
            ps_b = esD.enter_context(nc.psum_tensor("psD_b" + sfx, [128, 512], F32))

            bcast_load(gbc, ada_s[row:row + 1, 5 * D:6 * D], "gmbc" + sfx)
            bcast_load(lgbc, lnv_d[2:3, :], "lgbc2" + sfx)
            bcast_load(lbbc, lnv_d[3:4, :], "lbbc2" + sfx)
            bur = sbD("bur", [DFF // 128, 128], F32)
            P.add("sp", lambda e: e.dma_start(out=bur[:, :], in_=b_up_d[0, :].rearrange("(f p) -> f p", p=128)),
                  writes=["bur"], chan="bur" + sfx)
            P.add("pe", lambda e: e.transpose(out=ps_b[:, 0:DFF // 128], in_=bur[:, :],
                                              identity=ident[0:DFF // 128, 0:DFF // 128]),
                  reads=["bur"], writes=["ps_b"])
            P.add("dve", lambda e: e.tensor_copy(out=bupT[:, :], in_=ps_b[:, 0:DFF // 128]),
                  reads=["ps_b"], writes=["ps_b", "bupT"])
            wuc = [0]
            wdc = [0]
            upc = [0]
            wu_ready = {}

            def issue_wu(f4):
                wi = wuc[0] % 2
                wuc[0] += 1
                P.add("pool", lambda e, wi=wi, f4=f4: e.dma_start(
                    out=wu[wi][:, :, :], in_=wup_s[:, f4 * 512:(f4 + 1) * 512].rearrange("(k p) c -> p k c", p=128)),
                    writes=[("wu", wi)], chan=("wu", wi))
                return wi

            def load_u2(t0):
                P.add("sp", lambda e, t0=t0: e.dma_start(
                    out=u2[:, :, :], in_=u2src[:, t0:t0 + TT].rearrange("(k p) t -> p k t", p=128)),
                    writes=["u2"], chan="u2" + sfx)

            load_u2(0)
            for t0 in range(0, ntok, TT):
                P.add("sp", lambda e, t0=t0: e.dma_start(
                    out=x1[:, :, :], in_=x1src[t0:t0 + TT, :].rearrange("(s p) d -> p s d", p=128)),
                    writes=[("x1t", s4, c4) for s4 in range(nsub) for c4 in range(4)], chan="x1t" + sfx)
                for f4 in range(DFF // 512):
                    if (t0, f4) in wu_ready:
                        wi = wu_ready[(t0, f4)]
                    else:
                        wi = issue_wu(f4)
                    for ff in range(4):
                        fc = f4 * 4 + ff
                        pu = upc[0] % 3
                        zi = upc[0] % 2
                        upc[0] += 1
                        for k in range(KC):
                            P.add("pe", lambda e, pu=pu, wi=wi, k=k, ff=ff: e.matmul(
                                ps_up[pu][:, 0:TT], lhsT=wu[wi][:, k, ff * 128:(ff + 1) * 128], rhs=u2[:, k, :],
                                start=(k == 0), stop=(k == KC - 1)),
                                reads=[("wu", wi), "u2"], writes=[("ps_up", pu)])
                        P.add("act", lambda e, pu=pu, zi=zi, fc=fc: e.activation(
                            out=zt[zi][:, :], in_=ps_up[pu][:, 0:TT], func=AF.Relu, bias=bupT[:, fc:fc + 1]),
                            reads=[("ps_up", pu), "bupT"], writes=[("ps_up", pu), ("zt", zi)])
                        P.add("dve", lambda e, zi=zi, fc=fc: e.tensor_mul(out=hT[:, fc, :], in0=zt[zi][:, :], in1=zt[zi][:, :]),
                              reads=[("zt", zi)], writes=[("hT", fc)])
                if t0 + TT < ntok:
                    load_u2(t0 + TT)
                for c4 in range(4):
                    for fg in range(DFF // 1024):
                        wi = wdc[0] % 2
                        wdc[0] += 1
                        P.add("pool", lambda e, wi=wi, fg=fg, c4=c4: e.dma_start(
                            out=wd[wi][:, :, :],
                            in_=wdn_s[fg * 1024:(fg + 1) * 1024, c4 * 512:(c4 + 1) * 512].rearrange("(k p) c -> p k c", p=128)),
                            writes=[("wd", wi)], chan=("wd", wi))
                        for kk in range(8):
                            fc = fg * 8 + kk
                            for s4 in range(nsub):
                                P.add("pe", lambda e, wi=wi, kk=kk, fc=fc, s4=s4: e.matmul(
                                    ps_dn[s4][:, :], lhsT=hT[:, fc, s4 * 128:(s4 + 1) * 128], rhs=wd[wi][:, kk, :],
                                    start=(fc == 0), stop=(fc == DFF // 128 - 1)),
                                    reads=[("wd", wi), ("hT", fc)], writes=[("ps_dn", s4)])
                    cs = slice(c4 * 512, (c4 + 1) * 512)
                    for s4 in range(nsub):
                        ti = (c4 * nsub + s4) % 2
                        P.add("dve", lambda e, s4=s4, ti=ti, cs=cs: e.tensor_mul(
                            out=tmp[ti][:, :], in0=ps_dn[s4][:, :], in1=gbc[:, cs]),
                            reads=[("ps_dn", s4), "gmbc" + sfx], writes=[("ps_dn", s4), ("tmpd", ti)])
                        P.add("dve", lambda e, s4=s4, ti=ti, cs=cs: e.scalar_tensor_tensor(
                            out=x1[:, s4, cs], in0=x1[:, s4, cs], scalar=ALPHA, in1=tmp[ti][:, :],
                            op0=ALU.mult, op1=ALU.add),
                            reads=[("x1t", s4, c4), ("tmpd", ti)], writes=[("x1t", s4, c4)])
                if t0 + TT < ntok:
                    for f4 in (0, 1):
                        wu_ready[(t0 + TT, f4)] = issue_wu(f4)
                for s4 in range(nsub):
                    b = s4 % 2
                    allk = [("x1t", s4, c4) for c4 in range(4)]
                    for q4 in range(4):
                        P.add("dve", lambda e, b=b, q4=q4, s4=s4: e.bn_stats(out=bst[b][:, q4 * 6:(q4 + 1) * 6],
                                                                            in_=x1[:, s4, q4 * 512:(q4 + 1) * 512]),
                              reads=[("x1t", s4, q4)], writes=[("bstd", b, q4)])
                    P.add("dve", lambda e, b=b: e.bn_aggr(out=st[b][:, 0:2], in_=bst[b][:, :]),
                          reads=[("bstd", b, q4) for q4 in range(4)], writes=[("st0d", b)])
                    P.add("dve", lambda e, b=b: e.tensor_scalar_add(out=st[b][:, 4:5], in0=st[b][:, 1:2], scalar1=LN_EPS),
                          reads=[("st0d", b)], writes=[("st4d", b)])
                    P.add("act", lambda e, b=b: e.activation(out=st[b][:, 5:6], in_=st[b][:, 4:5], func=AF.Ln),
                          reads=[("st4d", b)], writes=[("st5d", b)])
                    P.add("act", lambda e, b=b: e.activation(out=st[b][:, 5:6], in_=st[b][:, 5:6], func=AF.Exp, scale=-0.5),
                          reads=[("st5d", b)], writes=[("st5d", b)])
                    P.add("dve", lambda e, b=b: e.scalar_tensor_tensor(
                        out=st[b][:, 6:7], in0=st[b][:, 0:1], scalar=-1.0, in1=st[b][:, 5:6], op0=ALU.mult, op1=ALU.mult),
                        reads=[("st0d", b), ("st5d", b)], writes=[("st6d", b)])
                    P.add("act", lambda e, b=b, s4=s4: e.activation(out=x1[:, s4, :], in_=x1[:, s4, :], func=AF.Identity,
                                                                   scale=st[b][:, 5:6], bias=st[b][:, 6:7]),
                          reads=allk + [("st5d", b), ("st6d", b)], writes=allk)
                    P.add("pool", lambda e, s4=s4: e.tensor_mul(out=x1[:, s4, :], in0=x1[:, s4, :], in1=lgbc[:, :]),
                          reads=allk + ["lgbc2" + sfx], writes=allk)
                    P.add("pool", lambda e, s4=s4: e.tensor_add(out=x1[:, s4, :], in0=x1[:, s4, :], in1=lbbc[:, :]),
                          reads=allk + ["lbbc2" + sfx], writes=allk)
                    r0 = t0 + s4 * 128
                    nr = min(128, nvalid)
                    P.add("sp", lambda e, s4=s4, r0=r0, nr=nr: e.dma_start(out=ydst[r0:r0 + nr, :], in_=x1[0:nr, s4, :]),
                          reads=allk, chan=("yo", s4))

        if "D" in phases:
            with ExitStack() as esD:
                phase_c2(esD, "", S, S, 0, u2T_s, x1_s, y_o)
                P.barrier()
            if with_sample:
                with ExitStack() as esD:
                    phase_c2(esD, "s", 128, NS, 1, u2Ts_s, x1s_s, ys_o)
                    P.barrier()

        P.barrier()
        P.add("sp", lambda e: None)
        P.emit()
    return nc


def _consts():
    ident = np.eye(128, dtype=np.float32)
    tri = np.triu(np.ones((128, 128), dtype=np.float32))
    sel = np.zeros((128, 128), dtype=np.float32)
    sel[64, :] = 1.0
    jx = np.ascontiguousarray(np.eye(128, dtype=np.float32)[::-1])
    k = np.arange(128)[:, None]
    i = np.arange(128)[None, :]
    mka = 1.0 - ((k >= 64) & (i < 64)).astype(np.float32)
    mkb = 1.0 - ((k < 64) & (i >= 64)).astype(np.float32)
    return dict(ident=ident, tri=tri, sel64=sel, jx=jx, mka=mka, mkb=mkb)


def _gtab(rel_bias):
    m = np.arange(768)
    idx = np.clip(127 - m, -128, 128) + 128
    return np.ascontiguousarray(rel_bias[:, idx], dtype=np.float32)


S_FULL = 4096
PAST_FULL = 4096
_NC_CACHE = {}


def kernel(x_prompt, x_sample, cache_fox_k, cache_fox_v, cache_fox_logf, cache_band_k, cache_band_v,
           c_prompt, c_sample, w_ada, b_ada, w_in, b_forget, rel_bias, w_out, ln_mix_g, ln_mix_b,
           w_up, b_up, w_down, ln_mlp_g, ln_mlp_b):
    f = lambda a: np.ascontiguousarray(np.asarray(a), dtype=np.float32)
    B, S, _ = x_prompt.shape
    PAST = cache_fox_k.shape[2]
    key = (S, PAST)
    if key not in _NC_CACHE:
        _NC_CACHE[key] = build(S, PAST)
    nc = _NC_CACHE[key]
    shared = dict(w_ada=f(w_ada[0]), b_ada=f(b_ada), w_in=f(w_in[0]), b_forget=f(b_forget),
                  gtab=_gtab(np.asarray(rel_bias[0])), w_out=f(w_out[0]), w_up=f(w_up[0]), b_up=f(b_up),
                  w_down=f(w_down[0]),
                  lnv=f(np.stack([ln_mix_g[0], ln_mix_b[0], ln_mlp_g[0], ln_mlp_b[0]])))
    shared.update(_consts())
    in_maps = []
    for b in range(B):
        m = dict(shared)
        m.update(x_p=f(x_prompt[b]), x_s=f(x_sample[b]), c2=f(np.stack([c_prompt[b], c_sample[b]])),
                 cfk=f(cache_fox_k[0, b]).reshape(PAST, HF * DH), cfv=f(cache_fox_v[0, b]).reshape(PAST, HF * DH),
                 cfl=f(cache_fox_logf[0, b]), cbk=f(cache_band_k[0, b]).reshape(-1, HB * DH),
                 cbv=f(cache_band_v[0, b]).reshape(-1, HB * DH))
        in_maps.append(m)
    res = run_bass_kernel_spmd(nc, in_maps, core_ids=list(range(B)))
    R = res.results
    WB = min(512, S)
    T = x_sample.shape[1]
    g = lambda name, shape: np.stack([np.asarray(R[b][name], dtype=np.float32).reshape(shape) for b in range(B)])
    return (g("y_p", (S, D)), g("y_s", (T, D)),
            g("fox_k_p", (S, HF, DH))[None], g("fox_v_p", (S, HF, DH))[None], g("fox_logf_p", (S, HF))[None],
            g("band_k_p", (WB, HB, DH))[None], g("band_v_p", (WB, HB, DH))[None],
            g("fox_k_s", (T, HF, DH))[None], g("fox_v_s", (T, HF, DH))[None], g("fox_logf_s", (T, HF))[None],
            g("band_k_s", (T, HB, DH))[None], g("band_v_s", (T, HB, DH))[None])
```

```python
from contextlib import ExitStack

import numpy as np
import concourse.bass as bass
import concourse.mybir as mybir
from concourse.bass_utils import run_bass_kernel_spmd

F32 = mybir.dt.float32
BF16 = mybir.dt.bfloat16
AF = mybir.ActivationFunctionType
ALU = mybir.AluOpType
AX = mybir.AxisListType

D = 2048
DH = 128
HF = 8
HB = 8
NH = HF + HB
DFF = 8192
IN_COLS = 6152
KC = D // 128
ALPHA = 2.0 ** 0.25
LN_EPS = 1e-5
SCALE = DH ** -0.5
ENGS = ("pe", "act", "dve", "pool", "sp")
BLK = {"pe": "tensor", "act": "scalar", "dve": "vector", "pool": "gpsimd", "sp": "sync"}


class Op:
    __slots__ = ("eng", "fn", "chan", "deps", "sig", "ticket", "cval", "gv")

    def __init__(self, eng, fn, chan):
        self.eng = eng
        self.fn = fn
        self.chan = chan
        self.deps = ()
        self.sig = False
        self.ticket = 0
        self.cval = 0
        self.gv = None


class Prog:
    def __init__(self, nc):
        self.nc = nc
        self.ops = {e: [] for e in ENGS}
        self.wr = {}
        self.rd = {}
        self.chan_last = {}
        self.chan_cnt = {}
        self.bar = []
        self.last = {}

    def add(self, eng, fn, reads=(), writes=(), chan=None, group=False):
        op = Op(eng, fn, chan)
        deps = {}
        for k in reads:
            w = self.wr.get(k)
            if w is not None:
                deps[id(w)] = w
        for k in writes:
            w = self.wr.get(k)
            if w is not None:
                deps[id(w)] = w
            for r in self.rd.get(k, {}).values():
                deps[id(r)] = r
        for b in self.bar:
            deps[id(b)] = b
        if chan is not None:
            prev = self.chan_last.get(chan)
            if prev is not None and not group:
                deps[id(prev)] = prev
            c = self.chan_cnt.get(chan, 0) + 1
            self.chan_cnt[chan] = c
            op.cval = 16 * c
            if group and prev is not None:
                op.gv = prev.gv
                op.gv[0] = op.cval
            else:
                op.gv = [op.cval]
            self.chan_last[chan] = op
        dl = []
        for d in deps.values():
            if d is op:
                continue
            if d.chan is None and chan is None and d.eng == "pe" and eng == "pe":
                continue
            d.sig = True
            dl.append(d)
        op.deps = dl
        who = chan if chan is not None else eng
        for k in reads:
            self.rd.setdefault(k, {})[who] = op
        for k in writes:
            self.wr[k] = op
            self.rd[k] = {}
        self.ops[eng].append(op)
        self.last[who] = op
        return op

    def barrier(self):
        self.bar = list(self.last.values())
        self.wr = {}
        self.rd = {}

    def emit(self):
        nc = self.nc
        for e in ENGS:
            t = 0
            for op in self.ops[e]:
                if op.chan is None and op.sig:
                    t += 1
                    op.ticket = t
        with ExitStack() as es:
            esem = {e: es.enter_context(nc.semaphore("s_" + e)) for e in ENGS}
            csem = {c: es.enter_context(nc.semaphore("c_%d" % i))
                    for i, c in enumerate(self.chan_cnt)}
            block = es.enter_context(nc.Block())
            for e in ENGS:
                ops = self.ops[e]

                def body(eng, ops=ops, e=e):
                    waited = {}
                    for op in ops:
                        for d in op.deps:
                            if d.chan is not None:
                                sem, val = csem[d.chan], d.gv[0]
                            else:
                                sem, val = esem[d.eng], d.ticket
                            if waited.get(sem.num, 0) < val:
                                eng.wait_ge(sem, val)
                                waited[sem.num] = val
                        ins = op.fn(eng)
                        if ins is None:
                            continue
                        if op.chan is not None:
                            ins.then_inc(csem[op.chan], 16)
                        elif op.sig:
                            ins.then_inc(esem[e], 1)

                getattr(block, BLK[e])(body)


def _split_cols(n, step):
    return [(c, min(step, n - c)) for c in range(0, n, step)]


def build(S, PAST, with_sample=True, debug=False, phases="ABCD"):
    assert S % 512 == 0
    nc = bass.Bass("TRN2", target_bir_lowering=False)
    P = Prog(nc)
    NT = S // 128

    def din(name, shape, dt=F32):
        return nc.dram_tensor(name, list(shape), dt, kind="ExternalInput").ap()

    def dout(name, shape, dt=F32):
        return nc.dram_tensor(name, list(shape), dt, kind="ExternalOutput").ap()

    def dscr(name, shape, dt):
        return nc.dram_tensor(name, list(shape), dt, kind="ExternalOutput" if debug else "Internal").ap()

    x_d = din("x_p", [S, D])
    c2_d = din("c2", [2, D])
    w_ada_d = din("w_ada", [D, 6 * D])
    b_ada_d = din("b_ada", [1, 6 * D])
    w_in_d = din("w_in", [D, IN_COLS])
    b_f_d = din("b_forget", [1, HF])
    ident_d = din("ident", [128, 128])
    tri_d = din("tri", [128, 128])
    sel_d = din("sel64", [128, 128])
    jx_d = din("jx", [128, 128])
    mka_d = din("mka", [128, 128])
    mkb_d = din("mkb", [128, 128])
    g_d = din("gtab", [HB, 768])
    w_out_d = din("w_out", [D, D])
    w_up_d = din("w_up", [D, DFF])
    b_up_d = din("b_up", [1, DFF])
    w_dn_d = din("w_down", [DFF, D])
    lnv_d = din("lnv", [4, D])
    y_o = dout("y_p", [S, D])
    NP = PAST // 128
    NS = 64
    xs_d = din("x_s", [NS, D])
    cfk_d = din("cfk", [PAST, HF * DH])
    cfv_d = din("cfv", [PAST, HF * DH])
    cfl_d = din("cfl", [PAST, HF])
    cbk_d = din("cbk", [512, HB * DH])
    cbv_d = din("cbv", [512, HB * DH])
    ys_o = dout("y_s", [NS, D])
    fks_o = dout("fox_k_s", [NS, HF * DH])
    fvs_o = dout("fox_v_s", [NS, HF * DH])
    fls_o = dout("fox_logf_s", [NS, HF])
    bks_o = dout("band_k_s", [NS, HB * DH])
    bvs_o = dout("band_v_s", [NS, HB * DH])

    fk_o = dout("fox_k_p", [S, HF * DH])
    fv_o = dout("fox_v_p", [S, HF * DH])
    fl_o = dout("fox_logf_p", [S, HF])
    WB = min(512, S)
    bk_o = dout("band_k_p", [WB, HB * DH])
    bv_o = dout("band_v_p", [WB, HB * DH])

    qT_s = dscr("qT_s", [NH, 128, S], BF16)
    kT_s = dscr("kT_s", [NH, 128, S], BF16)
    v_s = dscr("v_s", [NH, 128, NT, DH], BF16)
    ada_s = dscr("ada_s", [2, 6 * D], F32)
    mixT_s = dscr("mixT_s", [D, S], BF16)
    u2T_s = dscr("u2T_s", [D, S], BF16)
    x1_s = dscr("x1_s", [S, D], F32)
    wup_s = dscr("wup_s", [D, DFF], BF16)
    wdn_s = dscr("wdn_s", [DFF, D], BF16)
    qTs_s = dscr("qTs_s", [NH, 128, 128], BF16)
    kTfs_s = dscr("kTfs_s", [HF, 128, (NP + 1) * 128], BF16)
    kTbs_s = dscr("kTbs_s", [HB, 128, 5 * 128], BF16)
    vfs_s = dscr("vfs_s", [HF, 128, NP + 1, DH], BF16)
    vbs_s = dscr("vbs_s", [HB, 128, 5, DH], BF16)
    mixTs_s = dscr("mixTs_s", [D, 128], BF16)
    u2Ts_s = dscr("u2Ts_s", [D, 128], BF16)
    x1s_s = dscr("x1s_s", [128, D], F32)

    es = ExitStack()
    with es:
        def sb(name, shape, dt):
            return es.enter_context(nc.sbuf_tensor(name, list(shape), dt))

        ident = sb("ident_sb", [128, 128], F32)
        identb = sb("identb", [128, 128], BF16)
        tri = sb("tri_sb", [128, 128], F32)
        onesf = sb("onesf", [128, 128], F32)
        sel64 = sb("sel64_sb", [128, 128], F32)
        adaT = sb("adaT", [128, 96, 2], F32)
        scp1 = sb("scp1", [128, 2, KC, 2], F32)
        bfor = sb("bfor", [128, HF], F32)
        Fall = sb("Fall", [128, NT, HF], F32)
        Rall = sb("Rall", [128, NT, HF], F32)
        carry = sb("carry", [128, HF], F32)
        carryS = sb("carryS", [128, HF], F32)
        FallS = sb("FallS", [128, NP + 1, HF], F32)
        RallS = sb("RallS", [128, NP + 1, HF], F32)

        P.add("sp", lambda e: e.dma_start(out=ident[:, :], in_=ident_d[:, :]), writes=["c0"], chan="const")
        P.add("sp", lambda e: e.dma_start(out=tri[:, :], in_=tri_d[:, :]), writes=["c1"], chan="const", group=True)
        P.add("sp", lambda e: e.dma_start(out=sel64[:, :], in_=sel_d[:, :]), writes=["c2"], chan="const", group=True)
        P.add("sp", lambda e: e.dma_start(out=bfor[:, :], in_=b_f_d[0:1, :].to_broadcast([128, HF])),
              writes=["c3"], chan="const", group=True)
        P.add("dve", lambda e: e.tensor_copy(out=identb[:, :], in_=ident[:, :]), reads=["c0"], writes=["c4"])
        P.add("dve", lambda e: e.memset(onesf[:, :], 1.0), writes=["c5"])
        P.add("dve", lambda e: e.memset(carry[:, :], 0.0), writes=["carry"])
        trib = sb("trib", [128, 128], BF16)
        jx = sb("jx_sb", [128, 128], F32)
        mka = sb("mka_sb", [128, 128], F32)
        mkb = sb("mkb_sb", [128, 128], F32)
        P.add("sp", lambda e: e.dma_start(out=jx[:, :], in_=jx_d[:, :]), writes=["c6"], chan="const", group=True)
        P.add("sp", lambda e: e.dma_start(out=mka[:, :], in_=mka_d[:, :]), writes=["c7"], chan="const", group=True)
        P.add("sp", lambda e: e.dma_start(out=mkb[:, :], in_=mkb_d[:, :]), writes=["c8"], chan="const", group=True)
        P.add("dve", lambda e: e.tensor_copy(out=trib[:, :], in_=tri[:, :]), reads=["c1"], writes=["c9"])
        onesb = sb("onesb", [128, 128], BF16)
        P.add("dve", lambda e: e.memset(onesb[:, :], 1.0), writes=["c10"])

        def s0_logf(esS):
            clf = esS.enter_context(nc.sbuf_tensor("clf", [128, NP, HF], F32))
            ps_f = esS.enter_context(nc.psum_tensor("psS_f", [128, 512], F32))
            P.add("dve", lambda e: e.memset(carryS[:, :], 0.0), writes=["carryS"])
            P.add("sp", lambda e: e.dma_start(out=clf[:, :, :], in_=cfl_d[:, :].rearrange("(t p) h -> p t h", p=128)),
                  writes=["clf"], chan="clf")
            for t in range(NP):
                P.add("pe", lambda e, t=t: e.matmul(ps_f[:, 64:72], lhsT=tri[:, :], rhs=clf[:, t, :], start=True, stop=True),
                      reads=["clf"], writes=["psS_f"])
                P.add("pe", lambda e, t=t: e.matmul(ps_f[:, 80:88], lhsT=onesf[:, :], rhs=clf[:, t, :], start=True, stop=True),
                      reads=["clf"], writes=["psS_f"])
                P.add("dve", lambda e, t=t: e.tensor_add(out=FallS[:, t, :], in0=ps_f[:, 64:72], in1=carryS[:, :]),
                      reads=["psS_f", "carryS"], writes=["psS_f", ("FallS", t)])
                P.add("dve", lambda e: e.tensor_add(out=carryS[:, :], in0=ps_f[:, 80:88], in1=carryS[:, :]),
                      reads=["psS_f", "carryS"], writes=["psS_f", "carryS"])

        def s0_kv_jobs(esS, ps_k):
            kc = [esS.enter_context(nc.sbuf_tensor("kc%d" % i, [128, HF * DH], BF16)) for i in range(2)]
            kst = [esS.enter_context(nc.sbuf_tensor("kst%d" % i, [128, 8, 128], BF16)) for i in range(2)]
            jobs = []

            def vjob(dstv, srcv, nt, h):
                def f():
                    for t4 in range(0, nt, 4):
                        P.add("pool", lambda e, t4=t4: e.dma_start(
                            out=dstv[h, :, t4:t4 + 4, :],
                            in_=srcv[t4 * 128:(t4 + 4) * 128, h * DH:(h + 1) * DH].rearrange("(t p) d -> p t d", p=128)),
                            chan="vcache", group=True)
                return f

            def kload(n):
                (src, dstT, t) = kj[n]
                b = n % 2
                P.add("pool", lambda e: e.dma_start(out=kc[b][:, :], in_=src[t * 128:(t + 1) * 128, :]),
                      writes=[("kc", b)], chan=("kc", b))

            def kjob(n, src, dstT, t):
                def f():
                    b = n % 2
                    if n == 0:
                        kload(0)
                    for hh in range(8):
                        P.add("pe", lambda e, hh=hh: e.transpose(
                            out=ps_k[:, hh, :], in_=kc[b][:, hh * 128:(hh + 1) * 128], identity=identb[:, :]),
                            reads=[("kc", b)], writes=["ps_k"])
                    P.add("dve", lambda e: e.tensor_copy(out=kst[b][:, :, :], in_=ps_k[:, :, :]),
                          reads=["ps_k"], writes=["ps_k", ("kst", b)])
                    P.add("sp", lambda e: e.dma_start(
                        out=dstT[:, :, t * 128:(t + 1) * 128].rearrange("h p t -> p h t"), in_=kst[b][:, :, :]),
                        reads=[("kst", b)], chan=("kst", b))
                    if n + 1 < len(kj):
                        kload(n + 1)
                return f
            kj = [(cfk_d, kTfs_s, t) for t in range(NP)] + [(cbk_d, kTbs_s, t) for t in range(4)]
            vj = [vjob(vfs_s, cfv_d, NP, h) for h in range(HF)] + [vjob(vbs_s, cbv_d, 4, h) for h in range(HB)]
            for n, (src, dstT, t) in enumerate(kj):
                jobs.append(kjob(n, src, dstT, t))
                if n % 2 == 1 and vj:
                    jobs.append(vj.pop(0))
            jobs.extend(vj)
            return jobs

        def phase_ada(blocks, es0):
            def sb0(name, shape, dt):
                return es0.enter_context(nc.sbuf_tensor(name, list(shape), dt))
            csb = sb0("csb", [2, D], F32)
            ctmp = sb0("ctmp", [2, D], F32)
            sT = sb0("sT", [128, KC, 2], BF16)
            wab = [sb0("wab%d" % i, [128, KC, 512], BF16) for i in range(2)]
            bab = [sb0("bab%d" % i, [2, 512], F32) for i in range(2)]
            blk = [sb0("ablk%d" % i, [2, 512], F32) for i in range(2)]
            ps_t = es0.enter_context(nc.psum_tensor("ps_sT", [128, 512], F32))
            ps_a = [es0.enter_context(nc.psum_tensor("ps_a%d" % i, [128, 512], F32)) for i in range(2)]
            ps_at = es0.enter_context(nc.psum_tensor("ps_at", [128, 512], F32))

            P.add("sp", lambda e: e.dma_start(out=csb[:, :], in_=c2_d[:, :]), writes=["csb"], chan="cld")
            P.add("act", lambda e: e.activation(out=ctmp[:, :], in_=csb[:, :], func=AF.Exp, scale=-1.0),
                  reads=["csb"], writes=["ctmp"])
            P.add("dve", lambda e: e.tensor_scalar_add(out=ctmp[:, :], in0=ctmp[:, :], scalar1=1.0),
                  reads=["ctmp"], writes=["ctmp"])
            P.add("dve", lambda e: e.reciprocal(out=ctmp[:, :], in_=ctmp[:, :]), reads=["ctmp"], writes=["ctmp"])
            P.add("dve", lambda e: e.tensor_mul(out=ctmp[:, :], in0=ctmp[:, :], in1=csb[:, :]),
                  reads=["ctmp", "csb"], writes=["ctmp"])
            for k in range(KC):
                P.add("pe", lambda e, k=k: e.transpose(out=ps_t[:, 2 * k:2 * k + 2],
                                                       in_=ctmp[:, k * 128:(k + 1) * 128],
                                                       identity=ident[0:2, 0:2]),
                      reads=["ctmp"], writes=["ps_t"])
            P.add("dve", lambda e: e.tensor_copy(out=sT[:, :, :],
                                                 in_=ps_t[:, 0:2 * KC].rearrange("p (k r) -> p k r", r=2)),
                  reads=["ps_t"], writes=["sT"])
            for i, cb in enumerate(blocks):
                s = i % 2
                c0 = cb * 512
                P.add("pool", lambda e, s=s, c0=c0: e.dma_start(
                    out=wab[s][:, :, :], in_=w_ada_d[:, c0:c0 + 512].rearrange("(k p) c -> p k c", p=128)),
                    writes=[("wab", s)], chan=("wab", s))
                P.add("sp", lambda e, s=s, c0=c0: e.dma_start(
                    out=bab[s][:, :], in_=b_ada_d[0:1, c0:c0 + 512].to_broadcast([2, 512])),
                    writes=[("bab", s)], chan=("bab", s))
                for k in range(KC):
                    P.add("pe", lambda e, s=s, k=k: e.matmul(ps_a[s][0:2, :], lhsT=sT[:, k, :], rhs=wab[s][:, k, :],
                                                            start=(k == 0), stop=(k == KC - 1)),
                          reads=[("wab", s), "sT"], writes=[("ps_a", s)])
                P.add("dve", lambda e, s=s: e.tensor_add(out=blk[s][:, :], in0=ps_a[s][0:2, :], in1=bab[s][:, :]),
                      reads=[("ps_a", s), ("bab", s)], writes=[("ps_a", s), ("blk", s)])
                P.add("sp", lambda e, s=s, c0=c0: e.dma_start(out=ada_s[:, c0:c0 + 512], in_=blk[s][:, :]),
                      reads=[("blk", s)], writes=[("ada_s", cb)], chan=("ablk", s))
                for j in range(4):
                    jj = cb * 4 + j
                    P.add("pe", lambda e, s=s, j=j, jj=jj: e.transpose(
                        out=ps_at[:, 2 * (jj % 96):2 * (jj % 96) + 2], in_=blk[s][:, j * 128:(j + 1) * 128],
                        identity=ident[0:2, 0:2]),
                        reads=[("blk", s)], writes=["ps_at"])
                P.add("dve", lambda e, cb=cb: e.tensor_copy(
                    out=adaT[:, cb * 4:cb * 4 + 4, :],
                    in_=ps_at[:, 8 * cb:8 * cb + 8].rearrange("p (j r) -> p j r", r=2)),
                    reads=["ps_at"], writes=["ps_at", ("adaT", cb)])

        with ExitStack() as es0:
            phase_ada(list(range(24)), es0)
            P.add("dve", lambda e: e.tensor_scalar_add(out=scp1[:, 0, :, :], in0=adaT[:, 16:32, :], scalar1=1.0),
                  reads=[("adaT", cb) for cb in range(4, 8)], writes=["scp1a"])
            P.add("dve", lambda e: e.tensor_scalar_add(out=scp1[:, 1, :, :], in0=adaT[:, 64:80, :], scalar1=1.0),
                  reads=[("adaT", cb) for cb in range(16, 20)], writes=["scp1m"])
            if with_sample:
                s0_logf(es0)
            P.barrier()

        def phase_a(esA, sfx, xsrc, ntok, nvalid, row, outs, dst, Fd, Rd, gs_off, carry):
            def sbA(name, shape, dt):
                return esA.enter_context(nc.sbuf_tensor(name + sfx, list(shape), dt))
            TA = min(1024, ntok)
            QW = min(512, TA)
            nsubT = TA // 128
            xt = [sbA("xt%d" % i, [128, D], F32) for i in range(2)]
            xn = [sbA("xn%d" % i, [128, D], BF16) for i in range(2)]
            st = [sbA("st%d" % i, [128, 8], F32) for i in range(2)]
            bst = [sbA("bst%d" % i, [128, 24], F32) for i in range(2)]
            fs = sbA("fs", [128, 16], F32)
            uT = [sbA("uT%d" % i, [128, KC, TA], BF16) for i in range(2)]
            wb = [sbA("wb%d" % i, [128, KC, 512], BF16) for i in range(2)]
            wf = sbA("wf", [128, KC, HF], BF16)
            qst = [sbA("qst%d" % i, [128, 512], BF16) for i in range(2)]
            kvf = [sbA("kvf%d" % i, [128, 512], F32) for i in range(2)]
            kvb = [sbA("kvb%d" % i, [128, 512], BF16) for i in range(2)]
            kTst = [sbA("kTst%d" % i, [128, 4, 128], BF16) for i in range(2)]
            lf = [sbA("lf%d" % i, [128, 4, HF], F32) for i in range(2)]
            ps_tr = [esA.enter_context(nc.psum_tensor("psA_tr%d" % i + sfx, [128, 8, 128], BF16)) for i in range(4)]
            ps_mm = [esA.enter_context(nc.psum_tensor("psA_mm%d" % i + sfx, [128, 512], F32)) for i in range(2)]
            ps_f = esA.enter_context(nc.psum_tensor("psA_f" + sfx, [128, 512], F32))
            ps_kt = esA.enter_context(nc.psum_tensor("psA_kt" + sfx, [128, 8, 128], BF16))

            P.add("pool", lambda e: e.dma_start(
                out=wf[:, :, :], in_=w_in_d[:, 3072:3080].rearrange("(k p) c -> p k c", p=128)),
                writes=["wf"], chan="wf")

            cblocks = [(0, "q"), (512, "q"), (1024, "k"), (1536, "k"), (2048, "v"), (2560, "v"),
                       (3080, "q"), (3592, "q"), (4104, "k"), (4616, "k"), (5128, "v"), (5640, "v")]
            wcnt = [0]
            mmcnt = [0]
            stc = [0]
            sub_global = [0]
            kpend = []

            def ln_sub(t0, s):
                ub = (t0 // TA) % 2
                U = uT[ub]
                g = sub_global[0]
                sub_global[0] += 1
                b = g % 2
                r0 = t0 + s * 128
                if nvalid < 128:
                    P.add("pool", lambda e, b=b: e.memset(xt[b][:, :], 0.0), writes=[("xt", b)])
                    P.add("sp", lambda e, b=b: e.dma_start(out=xt[b][0:nvalid, :], in_=xsrc[0:nvalid, :]),
                          writes=[("xt", b)], chan=("xt", b))
                else:
                    P.add("sp", lambda e, b=b, r0=r0: e.dma_start(out=xt[b][:, :], in_=xsrc[r0:r0 + 128, :]),
                          writes=[("xt", b)], chan=("xt", b))
                for q4 in range(4):
                    P.add("dve", lambda e, b=b, q4=q4: e.bn_stats(out=bst[b][:, q4 * 6:(q4 + 1) * 6],
                                                                 in_=xt[b][:, q4 * 512:(q4 + 1) * 512]),
                          reads=[("xt", b)], writes=[("bst", b, q4)])
                P.add("dve", lambda e, b=b: e.bn_aggr(out=st[b][:, 0:2], in_=bst[b][:, :]),
                      reads=[("bst", b, q4) for q4 in range(4)], writes=[("st0", b)])
                P.add("dve", lambda e, b=b: e.tensor_scalar_add(out=st[b][:, 4:5], in0=st[b][:, 1:2], scalar1=LN_EPS),
                      reads=[("st0", b)], writes=[("st4", b)])
                P.add("act", lambda e, b=b: e.activation(out=st[b][:, 5:6], in_=st[b][:, 4:5], func=AF.Ln),
                      reads=[("st4", b)], writes=[("st5", b)])
                P.add("act", lambda e, b=b: e.activation(out=st[b][:, 5:6], in_=st[b][:, 5:6], func=AF.Exp, scale=-0.5),
                      reads=[("st5", b)], writes=[("st5", b)])
                P.add("dve", lambda e, b=b: e.scalar_tensor_tensor(
                    out=st[b][:, 6:7], in0=st[b][:, 0:1], scalar=-1.0, in1=st[b][:, 5:6],
                    op0=ALU.mult, op1=ALU.mult),
                    reads=[("st0", b), ("st5", b)], writes=[("st6", b)])
                P.add("act", lambda e, b=b: e.activation(out=xn[b][:, :], in_=xt[b][:, :], func=AF.Identity,
                                                         scale=st[b][:, 5:6], bias=st[b][:, 6:7]),
                      reads=[("xt", b), ("st5", b), ("st6", b)], writes=[("xn", b)])
                def back():
                    for half in range(2):
                        pb = 2 * (g % 2) + half
                        eng = "act" if half == 0 else "dve"
                        for kk in range(8):
                            k = half * 8 + kk
                            P.add("pe", lambda e, b=b, pb=pb, k=k, kk=kk: e.transpose(
                                out=ps_tr[pb][:, kk, :], in_=xn[b][:, k * 128:(k + 1) * 128], identity=identb[:, :]),
                                reads=[("xn", b)], writes=[("ps_tr", pb)])
                        for kk in range(8):
                            k = half * 8 + kk
                            if eng == "act":
                                fn = lambda e, pb=pb, k=k, kk=kk, s=s, U=U: e.activation(
                                    out=U[:, k, s * 128:(s + 1) * 128], in_=ps_tr[pb][:, kk, :], func=AF.Identity,
                                    scale=scp1[:, 0, k, row:row + 1], bias=adaT[:, k, row:row + 1])
                            else:
                                fn = lambda e, pb=pb, k=k, kk=kk, s=s, U=U: e.tensor_scalar(
                                    out=U[:, k, s * 128:(s + 1) * 128], in0=ps_tr[pb][:, kk, :],
                                    scalar1=scp1[:, 0, k, row:row + 1], scalar2=adaT[:, k, row:row + 1],
                                    op0=ALU.mult, op1=ALU.add)
                            P.add(eng, fn, reads=[("ps_tr", pb)], writes=[("uT", ub, s, k)])
                return back

            def forget_sub(t0, s):
                ub = (t0 // TA) % 2
                U = uT[ub]
                if True:
                    gs = (t0 // 128) + s
                    lb = (gs // 4) % 2
                    for k in range(KC):
                        P.add("pe", lambda e, k=k, s=s, U=U: e.matmul(
                            ps_f[:, 0:HF], lhsT=U[:, k, s * 128:(s + 1) * 128], rhs=wf[:, k, :],
                            start=(k == 0), stop=(k == KC - 1)),
                            reads=[("uT", ub, s, k), "wf"], writes=["ps_f"])
                    L = lf[lb][:, gs % 4, :]
                    P.add("dve", lambda e: e.tensor_add(out=fs[:, 0:8], in0=ps_f[:, 0:HF], in1=bfor[:, :]),
                          reads=["ps_f"], writes=["ps_f", "fs0"])
                    P.add("dve", lambda e: e.tensor_scalar_mul(out=fs[:, 8:16], in0=fs[:, 0:8], scalar1=-1.0),
                          reads=["fs0"], writes=["fs1"])
                    P.add("dve", lambda e: e.tensor_tensor(out=fs[:, 8:16], in0=fs[:, 8:16], in1=fs[:, 0:8], op=ALU.min),
                          reads=["fs0", "fs1"], writes=["fs1"])
                    P.add("act", lambda e: e.activation(out=fs[:, 8:16], in_=fs[:, 8:16], func=AF.Exp),
                          reads=["fs1"], writes=["fs1"])
                    P.add("act", lambda e: e.activation(out=fs[:, 8:16], in_=fs[:, 8:16], func=AF.Ln, bias=1.0),
                          reads=["fs1"], writes=["fs1"])
                    P.add("dve", lambda e: e.tensor_scalar_min(out=fs[:, 0:8], in0=fs[:, 0:8], scalar1=0.0),
                          reads=["fs0"], writes=["fs0"])
                    P.add("dve", lambda e, L=L: e.tensor_sub(out=L, in0=fs[:, 0:8], in1=fs[:, 8:16]),
                          reads=["fs0", "fs1"], writes=[("lf", lb, gs % 4)])
                    if nvalid < 128:
                        P.add("sp", lambda e, lb=lb, gs=gs: e.dma_start(
                            out=outs["logf"][0:nvalid, :], in_=lf[lb][0:nvalid, gs % 4, :]),
                            reads=[("lf", lb, gs % 4)], chan=("lfo", lb))
                    elif gs % 4 == 3:
                        P.add("sp", lambda e, lb=lb, gs=gs: e.dma_start(
                            out=outs["logf"][(gs - 3) * 128:(gs + 1) * 128, :].rearrange("(s p) h -> p s h", p=128),
                            in_=lf[lb][:, :, :]),
                            reads=[("lf", lb, i) for i in range(4)], chan=("lfo", lb))
                    P.add("pe", lambda e, L=L: e.matmul(ps_f[:, 64:72], lhsT=tri[:, :], rhs=L, start=True, stop=True),
                          reads=[("lf", lb, gs % 4)], writes=["ps_f"])
                    P.add("pe", lambda e, L=L: e.matmul(ps_f[:, 80:88], lhsT=onesf[:, :], rhs=L, start=True, stop=True),
                          reads=[("lf", lb, gs % 4)], writes=["ps_f"])
                    P.add("dve", lambda e, gs=gs: e.tensor_add(out=Fd[:, gs_off + gs, :], in0=ps_f[:, 64:72], in1=carry[:, :]),
                          reads=["ps_f", "carry"], writes=["ps_f", ("Fall", gs)])
                    P.add("dve", lambda e: e.tensor_add(out=carry[:, :], in0=ps_f[:, 80:88], in1=carry[:, :]),
                          reads=["ps_f", "carry"], writes=["ps_f", "carry"])
                    P.add("pe", lambda e, gs=gs: e.matmul(ps_f[:, 96:104], lhsT=sel64[:, :], rhs=Fd[:, gs_off + gs, :],
                                                          start=True, stop=True),
                          reads=[("Fall", gs)], writes=["ps_f"])
                    P.add("dve", lambda e, gs=gs: e.tensor_copy(out=Rd[:, gs_off + gs, :], in_=ps_f[:, 96:104]),
                          reads=["ps_f"], writes=["ps_f", ("Rall", gs)])

            def colblock(t0, c0, kind):
                ub = (t0 // TA) % 2
                U = uT[ub]
                wi = wcnt[0] % 2
                wcnt[0] += 1
                P.add("pool", lambda e, wi=wi, c0=c0: e.dma_start(
                    out=wb[wi][:, :, :], in_=w_in_d[:, c0:c0 + 512].rearrange("(k p) c -> p k c", p=128)),
                    writes=[("wb", wi)], chan=("wb", wi))
                h0 = (c0 // 512) * 4 if c0 < 3072 else HF + ((c0 - 3080) // 512) * 4
                h0 = h0 % 8 + (8 if c0 >= 3080 else 0)
                if kind == "q":
                    for hh in range(4):
                        h = h0 + hh
                        for tb in range(TA // QW):
                            pm = mmcnt[0] % 2
                            mmcnt[0] += 1
                            for k in range(KC):
                                P.add("pe", lambda e, pm=pm, wi=wi, k=k, hh=hh, tb=tb, U=U: e.matmul(
                                    ps_mm[pm][:, 0:QW], lhsT=wb[wi][:, k, hh * 128:(hh + 1) * 128],
                                    rhs=U[:, k, tb * QW:(tb + 1) * QW], start=(k == 0), stop=(k == KC - 1)),
                                    reads=[("wb", wi)] + [("uT", ub, tb * (QW // 128) + i, k) for i in range(QW // 128)],
                                    writes=[("ps_mm", pm)])
                            qi = stc[0] % 2
                            stc[0] += 1
                            P.add("act", lambda e, pm=pm, qi=qi: e.activation(
                                out=qst[qi][:, 0:QW], in_=ps_mm[pm][:, 0:QW], func=AF.Copy),
                                reads=[("ps_mm", pm)], writes=[("ps_mm", pm), ("qst", qi)])
                            qdst = dst["q"](h, t0 + tb * QW, QW)
                            P.add("sp", lambda e, qi=qi, qdst=qdst: e.dma_start(out=qdst, in_=qst[qi][:, 0:QW]),
                                  reads=[("qst", qi)], writes=[("qT_s", h)], chan=("qst", qi))
                else:
                    for s in range(nsubT):
                        gs = (t0 // 128) + s
                        pm = mmcnt[0] % 2
                        mmcnt[0] += 1
                        for k in range(KC):
                            P.add("pe", lambda e, pm=pm, wi=wi, k=k, s=s, U=U: e.matmul(
                                ps_mm[pm][:, :], lhsT=U[:, k, s * 128:(s + 1) * 128], rhs=wb[wi][:, k, :],
                                start=(k == 0), stop=(k == KC - 1)),
                                reads=[("wb", wi), ("uT", ub, s, k)], writes=[("ps_mm", pm)])
                        while kpend:
                            kpend.pop(0)()
                        qi = stc[0] % 2
                        stc[0] += 1
                        P.add("act", lambda e, pm=pm, qi=qi: e.activation(
                            out=kvf[qi][:, :], in_=ps_mm[pm][:, :], func=AF.Copy),
                            reads=[("ps_mm", pm)], writes=[("kvf", qi)])
                        P.add("dve", lambda e, pm=pm, qi=qi: e.tensor_copy(out=kvb[qi][:, :], in_=ps_mm[pm][:, :]),
                              reads=[("ps_mm", pm)], writes=[("ps_mm", pm), ("kvb", qi)])
                        fox = c0 < 3072
                        cc = (c0 - (1024 if kind == "k" else 2048)) if fox else (c0 - (4104 if kind == "k" else 5128))
                        nr = min(128, nvalid)
                        if fox:
                            odst = outs["fk" if kind == "k" else "fv"][gs * 128:gs * 128 + nr, cc:cc + 512]
                        elif gs * 128 >= ntok - outs["wb"]:
                            rr = gs * 128 - (ntok - outs["wb"])
                            odst = outs["bk" if kind == "k" else "bv"][rr:rr + nr, cc:cc + 512]
                        else:
                            odst = None
                        if odst is not None:
                            P.add("sp", lambda e, qi=qi, odst=odst, nr=nr: e.dma_start(out=odst, in_=kvf[qi][0:nr, :]),
                                  reads=[("kvf", qi)], chan=("kvf", qi))
                        if kind == "v":
                            vdst = dst["v"](h0, gs)
                            P.add("sp", lambda e, qi=qi, vdst=vdst: e.dma_start(
                                out=vdst, in_=kvb[qi][:, :].rearrange("p (h d) -> p h d", d=DH)),
                                reads=[("kvb", qi)], writes=[("v_s", h0)], chan=("kvb", qi))
                        else:
                            ki = stc[0] % 2

                            def ktail(qi=qi, ki=ki, h0=h0, gs=gs):
                                for hh in range(4):
                                    P.add("pe", lambda e, qi=qi, hh=hh: e.transpose(
                                        out=ps_kt[:, hh, :], in_=kvb[qi][:, hh * 128:(hh + 1) * 128], identity=identb[:, :]),
                                        reads=[("kvb", qi)], writes=["ps_kt"])
                                P.add("dve", lambda e, ki=ki: e.tensor_copy(out=kTst[ki][:, :, :], in_=ps_kt[:, 0:4, :]),
                                      reads=["ps_kt"], writes=["ps_kt", ("kTst", ki)])
                                kdst = dst["kt"](h0, gs)
                                P.add("sp", lambda e, ki=ki, kdst=kdst: e.dma_start(out=kdst, in_=kTst[ki][:, :, :]),
                                      reads=[("kTst", ki)], writes=[("kT_s", h0)], chan=("kTst", ki))
                            kpend.append(ktail)
                while kpend:
                    kpend.pop(0)()

            tiles = list(range(0, ntok, TA))
            for s in range(nsubT):
                ln_sub(tiles[0], s)()
            for ti, t0 in enumerate(tiles):
                pend_back = None
                for ci, (c0, kind) in enumerate(cblocks):
                    colblock(t0, c0, kind)
                    if ci < nsubT:
                        forget_sub(t0, ci)
                    if pend_back is not None:
                        pend_back()
                        pend_back = None
                    if ti + 1 < len(tiles) and ci < nsubT:
                        pend_back = ln_sub(tiles[ti + 1], ci)
                if pend_back is not None:
                    pend_back()

        dstP = dict(
            q=lambda h, a, w: qT_s[h, :, a:a + w],
            kt=lambda h0, gs: kT_s[h0:h0 + 4, :, gs * 128:(gs + 1) * 128].rearrange("h p t -> p h t"),
            v=lambda h0, gs: v_s[h0:h0 + 4, :, gs, :].rearrange("h p d -> p h d"))
        with ExitStack() as esA:
            phase_a(esA, "", x_d, S, S, 0, dict(fk=fk_o, fv=fv_o, logf=fl_o, bk=bk_o, bv=bv_o, wb=WB),
                    dstP, Fall, Rall, 0, carry)
            P.barrier()

        def q_s(h, a, w):
            return qTs_s[h, :, 0:w]

        def kt_s(h0, gs):
            if h0 < HF:
                return kTfs_s[h0:h0 + 4, :, NP * 128:(NP + 1) * 128].rearrange("h p t -> p h t")
            return kTbs_s[h0 - HF:h0 - HF + 4, :, 4 * 128:5 * 128].rearrange("h p t -> p h t")

        def v_ss(h0, gs):
            if h0 < HF:
                return vfs_s[h0:h0 + 4, :, NP, :].rearrange("h p d -> p h d")
            return vbs_s[h0 - HF:h0 - HF + 4, :, 4, :].rearrange("h p d -> p h d")

        if with_sample:
            with ExitStack() as esA:
                phase_a(esA, "s", xs_d, 128, NS, 1, dict(fk=fks_o, fv=fvs_o, logf=fls_o, bk=bks_o, bv=bvs_o, wb=128),
                        dict(q=q_s, kt=kt_s, v=v_ss), FallS, RallS, NP, carryS)
                P.barrier()


        def phase_b(esB, sfx, heads, jobs_fn=None):
            def sbB(name, shape, dt):
                return esB.enter_context(nc.sbuf_tensor(name + sfx, list(shape), dt))
            NKM = max(cfg(h)["NK"] for (h, cfg, _m, _p) in heads)
            NQM = max(len(cfg(h)["qtiles"]) for (h, cfg, _m, _p) in heads)
            qT = [sbB("qT%d" % i, [128, NQM * 128], BF16) for i in range(2)]
            kT = [sbB("kT%d" % i, [128, NKM * 128], BF16) for i in range(2)]
            vA = [sbB("vA%d" % i, [128, NKM, DH], BF16) for i in range(2)]
            biasT = sbB("biasT", [128, NQM, NKM], F32)
            Eb = sbB("Eb", [128, HB, 5, 128], F32)
            xh = [sbB("xh%d" % i, [128, 128], F32) for i in range(2)]
            NPB = 5
            Pb = [sbB("Pb%d" % i, [128, 512], BF16) for i in range(NPB)]
            recb = [sbB("recb%d" % i, [128, 512], F32) for i in range(2)]
            dcp = [sbB("dcp%d" % i, [128, 512], F32) for i in range(2)]
            rcp = [sbB("rcp%d" % i, [128, 512], F32) for i in range(2)]
            acc = [sbB("acc%d" % i, [128, 512], F32) for i in range(2)]
            NGM = (NQM + 3) // 4
            biasF = sbB("biasF", [128, NGM, NKM], F32)
            cjt = sbB("cjt", [128, NGM, 4], F32)
            mst = [sbB("mst%d" % i, [128, 512], BF16) for i in range(2)]
            ps_s = [esB.enter_context(nc.psum_tensor("psB_s%d" % i + sfx, [128, 512], F32)) for i in range(3)]
            ps_k = esB.enter_context(nc.psum_tensor("psB_k" + sfx, [128, 8, 128], BF16))
            jobs = jobs_fn(esB, ps_k) if jobs_fn is not None else []
            ps_o = [esB.enter_context(nc.psum_tensor("psB_o%d" % i + sfx, [128, 512], F32)) for i in range(2)]
            ps_r = [esB.enter_context(nc.psum_tensor("psB_r%d" % i + sfx, [128, 512], F32)) for i in range(2)]

            e_left = [0]
            if any(h >= HF for (h, _c, _m, _p) in heads):
                def ejob(hb):
                    def f():
                        for jj in range(5):
                            cnt = hb * 5 + jj
                            xi = cnt % 2
                            si = cnt % 3
                            src = bass.AP(tensor=g_d.tensor, offset=hb * 768 + 128 * jj, ap=[[1, 128], [1, 128]])
                            P.add("sp", lambda e, xi=xi, src=src: e.dma_start(out=xh[xi][:, :], in_=src),
                                  writes=[("xh", xi)], chan=("xh", xi))
                            P.add("pe", lambda e, xi=xi, si=si: e.matmul(ps_s[si][:, 0:128], lhsT=jx[:, :], rhs=xh[xi][:, :],
                                                                        start=True, stop=True),
                                  reads=[("xh", xi)], writes=[("ps_s", si)])
                            P.add("act", lambda e, si=si, jj=jj: e.activation(
                                out=Eb[:, hb, jj, :], in_=ps_s[si][:, 0:128], func=AF.Exp),
                                reads=[("ps_s", si)], writes=[("ps_s", si), ("Eb", hb)])
                            if jj in (0, 4):
                                mk = mka if jj == 0 else mkb
                                P.add("pool", lambda e, jj=jj, mk=mk: e.tensor_mul(
                                    out=Eb[:, hb, jj, :], in0=Eb[:, hb, jj, :], in1=mk[:, :]),
                                    reads=[("Eb", hb)], writes=[("Eb", hb)])
                        e_left[0] -= 1
                    return f
                ej = [ejob(hb) for hb in range(HB)]
                e_left[0] = len(ej)
                jobs = ej + jobs

            gcount = [0]
            ocount = [0]
            prev_part = None
            loaded = set()

            def emit_loads(hi):
                (h_, cfg_, _m, _p) = heads[hi]
                cf = cfg_(h_)
                hb2 = hi % 2
                NK = cf["NK"]
                nq = len(cf["qtiles"])
                loaded.add(hi)
                P.add("sp", lambda e: e.dma_start(out=qT[hb2][:, 0:nq * 128], in_=cf["qsrc"]),
                      writes=[("qT", hb2)], chan=("qkv", hb2))
                P.add("sp", lambda e: e.dma_start(out=kT[hb2][:, 0:NK * 128], in_=cf["ksrc"]),
                      writes=[("kT", hb2)], chan=("qkv", hb2), group=True)
                P.add("sp", lambda e: e.dma_start(out=vA[hb2][:, 0:NK, :], in_=cf["vsrc"]),
                      writes=[("vA", hb2)], chan=("qkv", hb2), group=True)

            for hi, (h, cfg, mixdst, part) in enumerate(heads):
                if prev_part is not None and part != prev_part:
                    while jobs:
                        jobs.pop(0)()
                    P.barrier()
                prev_part = part
                hb2 = hi % 2
                fox = h < HF
                if not fox:
                    while e_left[0] > 0:
                        jobs.pop(0)()
                cf = cfg(h)
                NK = cf["NK"]
                qtiles = cf["qtiles"]
                qoff = qtiles[0]
                Fd, Rd = cf["Fd"], cf["Rd"]
                nq = len(qtiles)
                if hi not in loaded:
                    emit_loads(hi)
                if fox:
                    for j in qtiles:
                        P.add("dve", lambda e, j=j, h=h, qoff=qoff, Fd=Fd, Rd=Rd: e.tensor_scalar(
                            out=biasT[:, j - qoff, 0:j + 1], in0=Fd[:, 0:j + 1, h], scalar1=-1.0, scalar2=Rd[:, j, h:h + 1],
                            op0=ALU.mult, op1=ALU.add),
                            writes=[("biasT", j - qoff)])
                steps = []
                for g0 in range(0, nq, 4):
                    G = qtiles[g0:g0 + 4]
                    j0, j1 = G[0], G[-1]
                    gi = g0 // 4
                    if fox and j0 > 0:
                        P.add("dve", lambda e, gi=gi, j0=j0, h=h, Fd=Fd, Rd=Rd: e.tensor_scalar(
                            out=biasF[:, gi, 0:j0], in0=Fd[:, 0:j0, h], scalar1=-1.0, scalar2=Rd[:, j0, h:h + 1],
                            op0=ALU.mult, op1=ALU.add),
                            writes=[("biasF", gi)])
                        P.add("dve", lambda e, gi=gi, j0=j0, ng=len(G), h=h, Rd=Rd: e.tensor_scalar(
                            out=cjt[:, gi, 0:ng], in0=Rd[:, j0:j0 + ng, h], scalar1=Rd[:, j0, h:h + 1], scalar2=None,
                            op0=ALU.subtract),
                            writes=[("cj", gi)])
                        P.add("act", lambda e, gi=gi, ng=len(G): e.activation(
                            out=cjt[:, gi, 0:ng], in_=cjt[:, gi, 0:ng], func=AF.Exp),
                            reads=[("cj", gi)], writes=[("cj", gi)])
                        for kb in range(j0, j1 + 1):
                            steps.append((j0, len(G), kb, kb, j1, kb == j0, kb == j1, "d", gi))
                        for kb in range(0, j0):
                            steps.append((j0, len(G), kb, j0, j1, kb == 0, kb == j0 - 1, "f", gi))
                        continue
                    kbs = list(range(0, j1 + 1)) if fox else list(range(max(0, j0 - 4), j1 + 1))
                    for kb in kbs:
                        ja = max(j0, kb)
                        jb = j1 if fox else min(j1, kb + 4)
                        steps.append((j0, len(G), kb, ja, jb, kb == kbs[0], kb == kbs[-1], "n", gi))
                pend = []

                def emit_pv(item, hb2=hb2, h=h, qoff=qoff, mixdst=mixdst):
                    (j0, ng, kb, ja, jb, first, last, pslot, ob, typ, gi) = item
                    n = jb - ja + 1
                    c0 = (ja - j0) * 128
                    P.add("pe", lambda e: e.matmul(
                        ps_o[ob][:, c0:c0 + n * 128], lhsT=vA[hb2][:, kb, :], rhs=Pb[pslot][:, 0:n * 128],
                        start=first, stop=last, skip_group_check=True),
                        reads=[("Pb", pslot, i) for i in range(n)] + [("vA", hb2)], writes=[("ps_o", ob)])
                    P.add("pe", lambda e: e.matmul(
                        ps_r[ob][:, c0:c0 + n * 128], lhsT=onesb[:, :], rhs=Pb[pslot][:, 0:n * 128],
                        start=first, stop=last, skip_group_check=True),
                        reads=[("Pb", pslot, i) for i in range(n)], writes=[("ps_r", ob)])
                    if last and typ == "d":
                        w = ng * 128
                        g2 = gi % 2
                        P.add("dve", lambda e: e.tensor_copy(out=dcp[g2][:, 0:w], in_=ps_o[ob][:, 0:w]),
                              reads=[("ps_o", ob)], writes=[("ps_o", ob), ("dcp", g2)])
                        P.add("dve", lambda e: e.tensor_copy(out=rcp[g2][:, 0:w], in_=ps_r[ob][:, 0:w]),
                              reads=[("ps_r", ob)], writes=[("ps_r", ob), ("rcp", g2)])
                    elif last and typ == "f":
                        w = ng * 128
                        g2 = gi % 2
                        for i in range(ng):
                            cs = slice(i * 128, (i + 1) * 128)
                            P.add("dve", lambda e, i=i, cs=cs: e.scalar_tensor_tensor(
                                out=acc[g2][:, cs], in0=ps_o[ob][:, cs], scalar=cjt[:, gi, i:i + 1], in1=dcp[g2][:, cs],
                                op0=ALU.mult, op1=ALU.add),
                                reads=[("ps_o", ob), ("cj", gi), ("dcp", g2)], writes=[("ps_o", ob), ("acc", g2, i)])
                            P.add("dve", lambda e, i=i, cs=cs: e.scalar_tensor_tensor(
                                out=recb[g2][:, cs], in0=ps_r[ob][:, cs], scalar=cjt[:, gi, i:i + 1], in1=rcp[g2][:, cs],
                                op0=ALU.mult, op1=ALU.add),
                                reads=[("ps_r", ob), ("cj", gi), ("rcp", g2)], writes=[("ps_r", ob), ("recq", g2, i)])
                        P.add("dve", lambda e: e.reciprocal(out=recb[g2][:, 0:w], in_=recb[g2][:, 0:w]),
                              reads=[("recq", g2, i) for i in range(ng)], writes=[("recb", g2)] + [("recq", g2, i) for i in range(ng)])
                        P.add("dve", lambda e: e.tensor_mul(out=mst[g2][:, 0:w], in0=acc[g2][:, 0:w], in1=recb[g2][:, 0:w]),
                              reads=[("acc", g2, i) for i in range(ng)] + [("recb", g2)], writes=[("mst", g2)])
                        md = mixdst(h, j0 - qoff, ng)
                        P.add("sp", lambda e: e.dma_start(out=md, in_=mst[g2][:, 0:w]),
                              reads=[("mst", g2)], writes=[("mixT_s", h)], chan=("mst", g2))
                        if jobs:
                            jobs.pop(0)()
                    elif last:
                        w = ng * 128
                        if h < HF:
                            P.add("dve", lambda e: e.reciprocal(out=recb[ob][:, 0:w], in_=ps_r[ob][:, 0:w]),
                                  reads=[("ps_r", ob)], writes=[("ps_r", ob), ("recb", ob)])
                        else:
                            P.add("act", lambda e: e.activation(out=recb[ob][:, 0:w], in_=ps_r[ob][:, 0:w], func=AF.Ln),
                                  reads=[("ps_r", ob)], writes=[("ps_r", ob), ("recb", ob)])
                            P.add("act", lambda e: e.activation(out=recb[ob][:, 0:w], in_=recb[ob][:, 0:w], func=AF.Exp,
                                                                scale=-1.0),
                                  reads=[("recb", ob)], writes=[("recb", ob)])
                        P.add("dve", lambda e: e.tensor_mul(out=mst[ob][:, 0:w], in0=ps_o[ob][:, 0:w], in1=recb[ob][:, 0:w]),
                              reads=[("ps_o", ob), ("recb", ob)], writes=[("ps_o", ob), ("mst", ob)])
                        md = mixdst(h, j0 - qoff, ng)
                        P.add("sp", lambda e: e.dma_start(out=md, in_=mst[ob][:, 0:w]),
                              reads=[("mst", ob)], writes=[("mixT_s", h)], chan=("mst", ob))
                        if jobs:
                            jobs.pop(0)()

                nstep = 0
                for (j0, ng, kb, ja, jb, first, last, typ, gi) in steps:
                    si = gcount[0] % 3
                    pslot = gcount[0] % NPB
                    gcount[0] += 1
                    if typ == "n":
                        if first:
                            ocount[0] += 1
                        ob = ocount[0] % 2
                    else:
                        ob = 0 if typ == "d" else 1
                    n = jb - ja + 1
                    P.add("pe", lambda e, si=si, kb=kb, ja=ja, jb=jb, n=n, hb2=hb2, qoff=qoff: e.matmul(
                        ps_s[si][:, 0:n * 128], lhsT=kT[hb2][:, kb * 128:(kb + 1) * 128],
                        rhs=qT[hb2][:, (ja - qoff) * 128:(jb + 1 - qoff) * 128], start=True, stop=True),
                        reads=[("kT", hb2), ("qT", hb2)], writes=[("ps_s", si)])
                    if fox and typ == "f":
                        P.add("act", lambda e, si=si, n=n, kb=kb, gi=gi, pslot=pslot: e.activation(
                            out=Pb[pslot][:, 0:n * 128], in_=ps_s[si][:, 0:n * 128], func=AF.Exp, scale=SCALE,
                            bias=biasF[:, gi, kb:kb + 1]),
                            reads=[("ps_s", si), ("biasF", gi)], writes=[("Pb", pslot, i) for i in range(n)])
                    elif fox:
                        for j in range(ja, jb + 1):
                            i = j - ja
                            P.add("act", lambda e, si=si, i=i, kb=kb, j=j, pslot=pslot, qoff=qoff: e.activation(
                                out=Pb[pslot][:, i * 128:(i + 1) * 128], in_=ps_s[si][:, i * 128:(i + 1) * 128],
                                func=AF.Exp, scale=SCALE, bias=biasT[:, j - qoff, kb:kb + 1]),
                                reads=[("ps_s", si), ("biasT", j - qoff)], writes=[("Pb", pslot, i)])
                        if ja == kb:
                            P.add("pool", lambda e, pslot=pslot: e.tensor_mul(
                                out=Pb[pslot][:, 0:128], in0=Pb[pslot][:, 0:128], in1=trib[:, :]),
                                reads=[("Pb", pslot, 0)], writes=[("Pb", pslot, 0)])
                    else:
                        P.add("act", lambda e, si=si, n=n, pslot=pslot: e.activation(
                            out=Pb[pslot][:, 0:n * 128], in_=ps_s[si][:, 0:n * 128], func=AF.Exp, scale=SCALE),
                            reads=[("ps_s", si)], writes=[("Pb", pslot, i) for i in range(n)])
                        P.add("dve" if gcount[0] % 2 == 0 else "pool", lambda e, pslot=pslot, n=n, kb=kb, ja=ja, jb=jb, h=h: e.tensor_mul(
                            out=Pb[pslot][:, 0:n * 128], in0=Pb[pslot][:, 0:n * 128],
                            in1=Eb[:, h - HF, ja - kb:jb - kb + 1, :].rearrange("p a b -> p (a b)")),
                            reads=[("Pb", pslot, i) for i in range(n)] + [("Eb", h - HF)],
                            writes=[("Pb", pslot, i) for i in range(n)])
                    pend.append((j0, ng, kb, ja, jb, first, last, pslot, ob, typ, gi))
                    if len(pend) > 3:
                        emit_pv(pend.pop(0))
                    nstep += 1
                    if nstep == 3 and hi + 1 < len(heads) and heads[hi + 1][3] == part and (hi + 1) not in loaded:
                        emit_loads(hi + 1)
                while pend:
                    emit_pv(pend.pop(0))

        def bcast_load(dst, src_row, chan):
            P.add("sp", lambda e: e.dma_start(out=dst[:, :], in_=src_row.to_broadcast([128, D])),
                  writes=[chan], chan=chan)

        def load_wo(es_, sfx, row, defer=None):
            wo = es_.enter_context(nc.sbuf_tensor("wo" + sfx, [128, KC, D], BF16))

            def emit():
                for c4 in range(4):
                    P.add("pool", lambda e, c4=c4: e.dma_start(
                        out=wo[:, :, c4 * 512:(c4 + 1) * 512],
                        in_=w_out_d[:, c4 * 512:(c4 + 1) * 512].rearrange("(k p) c -> p k c", p=128)),
                        writes=[("wo", c4)], chan=("wo", c4))
            if defer is not None:
                defer.append(emit)
            else:
                emit()
            return wo

        def cfgP(h):
            return dict(NK=NT, qtiles=list(range(NT)), qsrc=qT_s[h, :, :], ksrc=kT_s[h, :, :], vsrc=v_s[h, :, :, :],
                        Fd=Fall, Rd=Rall)

        def cfgS(h):
            if h < HF:
                return dict(NK=NP + 1, qtiles=[NP], qsrc=qTs_s[h, :, :], ksrc=kTfs_s[h, :, :], vsrc=vfs_s[h, :, :, :],
                            Fd=FallS, Rd=RallS)
            return dict(NK=5, qtiles=[4], qsrc=qTs_s[h, :, :], ksrc=kTbs_s[h - HF, :, :], vsrc=vbs_s[h - HF, :, :, :],
                        Fd=FallS, Rd=RallS)

        wcast_jobs = []
        if "D" in phases:
            def wc_up(r):
                return lambda: P.add("pool", lambda e: e.dma_start(out=wup_s[r * 128:(r + 1) * 128, :],
                                                                   in_=w_up_d[r * 128:(r + 1) * 128, :]),
                                     chan="wcast", group=True)

            def wc_dn(r):
                return lambda: P.add("pool", lambda e: e.dma_start(
                    out=wdn_s[r * 512:(r + 1) * 512, :].rearrange("(a p) c -> p a c", p=128),
                    in_=w_dn_d[r * 512:(r + 1) * 512, :].rearrange("(a p) c -> p a c", p=128)),
                    chan="wcast", group=True)
            for r in range(D // 128):
                wcast_jobs.append(wc_up(r))
                wcast_jobs.append(wc_dn(r))
        if "B" in phases:
            mdP = lambda h, j0, n: mixT_s[h * 128:(h + 1) * 128, j0 * 128:(j0 + n) * 128]
            mdS = lambda h, j0, n: mixTs_s[h * 128:(h + 1) * 128, j0 * 128:(j0 + n) * 128]
            hl = [(h, cfgP, mdP, 0) for h in range(NH)]
            if with_sample:
                hl += [(h, cfgS, mdS, 1) for h in range(NH)]
            esW = ExitStack()
            pre_jobs = []
            wo_p = load_wo(esW, "", 0, defer=pre_jobs) if "C" in phases else None

            def all_jobs(esS, ps_k):
                sj = s0_kv_jobs(esS, ps_k) if with_sample else []
                out = list(pre_jobs)
                wj = list(wcast_jobs)
                while sj or wj:
                    if wj:
                        out.append(wj.pop(0))
                    if sj:
                        out.append(sj.pop(0))
                return out
            with ExitStack() as esB:
                phase_b(esB, "", hl, all_jobs)
                P.barrier()

        def ln_tokmajor(pre, stt, bstt, tag, q):
            for q4 in range(4):
                P.add("dve", lambda e, q4=q4: e.bn_stats(out=bstt[:, q4 * 6:(q4 + 1) * 6], in_=pre[:, q4 * 512:(q4 + 1) * 512]),
                      reads=[(tag, q)], writes=[("bst" + tag, q, q4)])
            P.add("dve", lambda e: e.bn_aggr(out=stt[:, 0:2], in_=bstt[:, :]),
                  reads=[("bst" + tag, q, q4) for q4 in range(4)], writes=[("st0" + tag, q)])
            P.add("dve", lambda e: e.tensor_scalar_add(out=stt[:, 4:5], in0=stt[:, 1:2], scalar1=LN_EPS),
                  reads=[("st0" + tag, q)], writes=[("st4" + tag, q)])
            P.add("act", lambda e: e.activation(out=stt[:, 5:6], in_=stt[:, 4:5], func=AF.Ln),
                  reads=[("st4" + tag, q)], writes=[("st5" + tag, q)])
            P.add("act", lambda e: e.activation(out=stt[:, 5:6], in_=stt[:, 5:6], func=AF.Exp, scale=-0.5),
                  reads=[("st5" + tag, q)], writes=[("st5" + tag, q)])
            P.add("dve", lambda e: e.scalar_tensor_tensor(
                out=stt[:, 6:7], in0=stt[:, 0:1], scalar=-1.0, in1=stt[:, 5:6], op0=ALU.mult, op1=ALU.mult),
                reads=[("st0" + tag, q), ("st5" + tag, q)], writes=[("st6" + tag, q)])

        def phase_c1(esC, sfx, ntok, nvalid, row, xsrc, mixsrc, x1dst, u2dst, wo=None):
            def sbC(name, shape, dt):
                return esC.enter_context(nc.sbuf_tensor(name + sfx, list(shape), dt))
            TB = min(512, ntok)
            NS4 = TB // 128
            if wo is None:
                wo = load_wo(esC, sfx, row)
            gbc = sbC("gbc", [128, D], F32)
            lgbc = sbC("lgbc", [128, D], F32)
            lbbc = sbC("lbbc", [128, D], F32)
            mT = [sbC("mT%d" % i, [128, KC, TB], BF16) for i in range(2)]
            xt = [sbC("xtc%d" % i, [128, D], F32) for i in range(2)]
            pre = [sbC("pre%d" % i, [128, D], F32) for i in range(2)]
            tmp = [sbC("tmpc%d" % i, [128, 512], F32) for i in range(2)]
            xn2 = [sbC("xn2%d" % i, [128, D], BF16) for i in range(2)]
            u2st = [sbC("u2st%d" % i, [128, KC, 128], BF16) for i in range(2)]
            st = [sbC("stc%d" % i, [128, 8], F32) for i in range(4)]
            bst = [sbC("bstc%d" % i, [128, 24], F32) for i in range(4)]
            ps_mm = [esC.enter_context(nc.psum_tensor("psC_mm%d" % i + sfx, [128, 512], F32)) for i in range(4)]
            ps_tr = [esC.enter_context(nc.psum_tensor("psC_tr%d" % i + sfx, [128, 8, 128], BF16)) for i in range(4)]

            bcast_load(gbc, ada_s[row:row + 1, 2 * D:3 * D], "gbc" + sfx)
            bcast_load(lgbc, lnv_d[0:1, :], "lgbc" + sfx)
            bcast_load(lbbc, lnv_d[1:2, :], "lbbc" + sfx)
            mm = [0]
            pending = []
            pending2 = []
            for tb in range(ntok // TB):
                mb = tb % 2
                P.add("sp", lambda e, mb=mb, tb=tb: e.dma_start(
                    out=mT[mb][:, :, :], in_=mixsrc[:, tb * TB:(tb + 1) * TB].rearrange("(k p) t -> p k t", p=128)),
                    writes=[("mT", mb)], chan=("mT", mb))
                for s4 in range(NS4):
                    g = tb * NS4 + s4
                    b = g % 2
                    r0 = g * 128
                    if nvalid < 128:
                        P.add("pool", lambda e, b=b: e.memset(xt[b][:, :], 0.0), writes=[("xtc", b)])
                        P.add("sp", lambda e, b=b: e.dma_start(out=xt[b][0:nvalid, :], in_=xsrc[0:nvalid, :]),
                              writes=[("xtc", b)], chan=("xtc", b))
                    else:
                        P.add("sp", lambda e, b=b, r0=r0: e.dma_start(out=xt[b][:, :], in_=xsrc[r0:r0 + 128, :]),
                              writes=[("xtc", b)], chan=("xtc", b))
                    for c4 in range(4):
                        pm = mm[0] % 4
                        mm[0] += 1
                        for k in range(KC):
                            P.add("pe", lambda e, pm=pm, mb=mb, k=k, s4=s4, c4=c4: e.matmul(
                                ps_mm[pm][:, :], lhsT=mT[mb][:, k, s4 * 128:(s4 + 1) * 128],
                                rhs=wo[:, k, c4 * 512:(c4 + 1) * 512], start=(k == 0), stop=(k == KC - 1)),
                                reads=[("mT", mb), ("wo", c4)], writes=[("ps_mm", pm)])
                        ti = mm[0] % 2
                        cs = slice(c4 * 512, (c4 + 1) * 512)
                        P.add("dve", lambda e, pm=pm, ti=ti, cs=cs: e.tensor_mul(
                            out=tmp[ti][:, :], in0=ps_mm[pm][:, :], in1=gbc[:, cs]),
                            reads=[("ps_mm", pm), "gbc" + sfx], writes=[("ps_mm", pm), ("tmpc", ti)])
                        P.add("dve", lambda e, b=b, ti=ti, cs=cs: e.scalar_tensor_tensor(
                            out=pre[b][:, cs], in0=xt[b][:, cs], scalar=ALPHA, in1=tmp[ti][:, :],
                            op0=ALU.mult, op1=ALU.add),
                            reads=[("xtc", b), ("tmpc", ti)], writes=[("pre", b, c4)])
                    while pending2:
                        pending2.pop(0)()
                    while pending:
                        pending.pop(0)()
                    for q4 in range(4):
                        P.add("dve", lambda e, b=b, q4=q4: e.bn_stats(out=bst[b][:, q4 * 6:(q4 + 1) * 6],
                                                                     in_=pre[b][:, q4 * 512:(q4 + 1) * 512]),
                              reads=[("pre", b, q4)], writes=[("bstc", b, q4)])
                    P.add("dve", lambda e, b=b: e.bn_aggr(out=st[b][:, 0:2], in_=bst[b][:, :]),
                          reads=[("bstc", b, q4) for q4 in range(4)], writes=[("st0c", b)])
                    P.add("dve", lambda e, b=b: e.tensor_scalar_add(out=st[b][:, 4:5], in0=st[b][:, 1:2], scalar1=LN_EPS),
                          reads=[("st0c", b)], writes=[("st4c", b)])
                    P.add("act", lambda e, b=b: e.activation(out=st[b][:, 5:6], in_=st[b][:, 4:5], func=AF.Ln),
                          reads=[("st4c", b)], writes=[("st5c", b)])
                    P.add("act", lambda e, b=b: e.activation(out=st[b][:, 5:6], in_=st[b][:, 5:6], func=AF.Exp, scale=-0.5),
                          reads=[("st5c", b)], writes=[("st5c", b)])
                    P.add("dve", lambda e, b=b: e.scalar_tensor_tensor(
                        out=st[b][:, 6:7], in0=st[b][:, 0:1], scalar=-1.0, in1=st[b][:, 5:6], op0=ALU.mult, op1=ALU.mult),
                        reads=[("st0c", b), ("st5c", b)], writes=[("st6c", b)])
                    P.add("act", lambda e, b=b: e.activation(out=pre[b][:, :], in_=pre[b][:, :], func=AF.Identity,
                                                             scale=st[b][:, 5:6], bias=st[b][:, 6:7]),
                          reads=[("pre", b, q4) for q4 in range(4)] + [("st5c", b), ("st6c", b)],
                          writes=[("pre", b, q4) for q4 in range(4)])
                    P.add("pool", lambda e, b=b: e.tensor_mul(out=pre[b][:, :], in0=pre[b][:, :], in1=lgbc[:, :]),
                          reads=[("pre", b, q4) for q4 in range(4)] + ["lgbc" + sfx], writes=[("pre", b, q4) for q4 in range(4)])
                    P.add("pool", lambda e, b=b: e.tensor_add(out=pre[b][:, :], in0=pre[b][:, :], in1=lbbc[:, :]),
                          reads=[("pre", b, q4) for q4 in range(4)] + ["lbbc" + sfx], writes=[("pre", b, q4) for q4 in range(4)])
                    P.add("pool", lambda e, b=b, r0=r0: e.dma_start(out=x1dst[r0:r0 + 128, :], in_=pre[b][:, :]),
                          reads=[("pre", b, q4) for q4 in range(4)], writes=[("x1_s", g)], chan=("x1o", b))
                    def do_tail(g=g, b=b, r0=r0):
                        b2 = 2 + b
                        for q4 in range(4):
                            P.add("dve", lambda e, b=b, b2=b2, q4=q4: e.bn_stats(out=bst[b2][:, q4 * 6:(q4 + 1) * 6],
                                                                                in_=pre[b][:, q4 * 512:(q4 + 1) * 512]),
                                  reads=[("pre", b, q4)], writes=[("bstc", b2, q4)])
                        P.add("dve", lambda e, b2=b2: e.bn_aggr(out=st[b2][:, 0:2], in_=bst[b2][:, :]),
                              reads=[("bstc", b2, q4) for q4 in range(4)], writes=[("st0c", b2)])
                        P.add("dve", lambda e, b2=b2: e.tensor_scalar_add(out=st[b2][:, 4:5], in0=st[b2][:, 1:2], scalar1=LN_EPS),
                              reads=[("st0c", b2)], writes=[("st4c", b2)])
                        P.add("act", lambda e, b2=b2: e.activation(out=st[b2][:, 5:6], in_=st[b2][:, 4:5], func=AF.Ln),
                              reads=[("st4c", b2)], writes=[("st5c", b2)])
                        P.add("act", lambda e, b2=b2: e.activation(out=st[b2][:, 5:6], in_=st[b2][:, 5:6], func=AF.Exp, scale=-0.5),
                              reads=[("st5c", b2)], writes=[("st5c", b2)])
                        P.add("dve", lambda e, b2=b2: e.scalar_tensor_tensor(
                            out=st[b2][:, 6:7], in0=st[b2][:, 0:1], scalar=-1.0, in1=st[b2][:, 5:6], op0=ALU.mult, op1=ALU.mult),
                            reads=[("st0c", b2), ("st5c", b2)], writes=[("st6c", b2)])
                        P.add("act", lambda e, b=b, b2=b2: e.activation(out=xn2[b][:, :], in_=pre[b][:, :], func=AF.Identity,
                                                                       scale=st[b2][:, 5:6], bias=st[b2][:, 6:7]),
                              reads=[("pre", b, q4) for q4 in range(4)] + [("st5c", b2), ("st6c", b2)], writes=[("xn2", b)])
                        pending2.append(lambda: do_tail_b(g, b, r0))

                    def do_tail_b(g, b, r0):
                        for half in range(2):
                            pb = 2 * (g % 2) + half
                            eng = "act" if half == 0 else "dve"
                            for kk in range(8):
                                k = half * 8 + kk
                                P.add("pe", lambda e, b=b, pb=pb, k=k, kk=kk: e.transpose(
                                    out=ps_tr[pb][:, kk, :], in_=xn2[b][:, k * 128:(k + 1) * 128], identity=identb[:, :]),
                                    reads=[("xn2", b)], writes=[("ps_trc", pb)])
                            for kk in range(8):
                                k = half * 8 + kk
                                if eng == "act":
                                    fn = lambda e, pb=pb, k=k, kk=kk, b=b: e.activation(
                                        out=u2st[b][:, k, :], in_=ps_tr[pb][:, kk, :], func=AF.Identity,
                                        scale=scp1[:, 1, k, row:row + 1], bias=adaT[:, 48 + k, row:row + 1])
                                else:
                                    fn = lambda e, pb=pb, k=k, kk=kk, b=b: e.tensor_scalar(
                                        out=u2st[b][:, k, :], in0=ps_tr[pb][:, kk, :],
                                        scalar1=scp1[:, 1, k, row:row + 1], scalar2=adaT[:, 48 + k, row:row + 1],
                                        op0=ALU.mult, op1=ALU.add)
                                P.add(eng, fn, reads=[("ps_trc", pb)], writes=[("u2st", b, k)])
                        P.add("pool", lambda e, b=b, r0=r0: e.dma_start(
                            out=u2dst[:, r0:r0 + 128].rearrange("(k p) t -> p k t", p=128), in_=u2st[b][:, :, :]),
                            reads=[("u2st", b, k) for k in range(KC)], writes=[("u2T_s", g)], chan=("u2o", b))
                    pending.append(do_tail)
            while pending:
                pending.pop(0)()
            while pending2:
                pending2.pop(0)()

        if "C" in phases:
            with ExitStack() as esC:
                phase_c1(esC, "", S, S, 0, x_d, mixT_s, x1_s, u2T_s, wo=wo_p)
                P.barrier()
            if with_sample:
                with ExitStack() as esC:
                    phase_c1(esC, "s", 128, NS, 1, xs_d, mixTs_s, x1s_s, u2Ts_s, wo=wo_p)
                    P.barrier()
            if "B" in phases:
                esW.close()

        def phase_c2(esD, sfx, ntok, nvalid, row, u2src, x1src, ydst):
            def sbD(name, shape, dt):
                return esD.enter_context(nc.sbuf_tensor(name + sfx, list(shape), dt))
            TT = min(512, ntok)
            nsub = TT // 128
            u2 = sbD("u2", [128, KC, TT], BF16)
            hT = sbD("hT", [128, DFF // 128, TT], BF16)
            wu = [sbD("wu%d" % i, [128, KC, 512], BF16) for i in range(2)]
            wd = [sbD("wd%d" % i, [128, 8, 512], BF16) for i in range(2)]
            x1 = sbD("x1t", [128, nsub, D], F32)
            gbc = sbD("gmbc", [128, D], F32)
            lgbc = sbD("lgbc2", [128, D], F32)
            lbbc = sbD("lbbc2", [128, D], F32)
            bupT = sbD("bupT", [128, DFF // 128], F32)
            zt = [sbD("zt%d" % i, [128, TT], F32) for i in range(2)]
            tmp = [sbD("tmpd%d" % i, [128, 512], F32) for i in range(2)]
            st = [sbD("std%d" % i, [128, 8], F32) for i in range(2)]
            bst = [sbD("bstd%d" % i, [128, 24], F32) for i in range(2)]
            ps_up = [esD.enter_context(nc.psum_tensor("psD_up%d" % i + sfx, [128, 512], F32)) for i in range(3)]
            ps_dn = [esD.enter_context(nc.psum_tensor("psD_dn%d" % i + sfx, [128, 512], F32)) for i in range(4)]
            ps_b = esD.enter_context(nc.psum_tensor("psD_b" + sfx, [128, 512], F32))

            bcast_load(gbc, ada_s[row:row + 1, 5 * D:6 * D], "gmbc" + sfx)
            bcast_load(lgbc, lnv_d[2:3, :], "lgbc2" + sfx)
            bcast_load(lbbc, lnv_d[3:4, :], "lbbc2" + sfx)
            bur = sbD("bur", [DFF // 128, 128], F32)
            P.add("sp", lambda e: e.dma_start(out=bur[:, :], in_=b_up_d[0, :].rearrange("(f p) -> f p", p=128)),
                  writes=["bur"], chan="bur" + sfx)
            P.add("pe", lambda e: e.transpose(out=ps_b[:, 0:DFF // 128], in_=bur[:, :],
                                              identity=ident[0:DFF // 128, 0:DFF // 128]),
                  reads=["bur"], writes=["ps_b"])
            P.add("dve", lambda e: e.tensor_copy(out=bupT[:, :], in_=ps_b[:, 0:DFF // 128]),
                  reads=["ps_b"], writes=["ps_b", "bupT"])
            wuc = [0]
            wdc = [0]
            upc = [0]
            wu_ready = {}

            def issue_wu(f4):
                wi = wuc[0] % 2
                wuc[0] += 1
                P.add("pool", lambda e, wi=wi, f4=f4: e.dma_start(
                    out=wu[wi][:, :, :], in_=wup_s[:, f4 * 512:(f4 + 1) * 512].rearrange("(k p) c -> p k c", p=128)),
                    writes=[("wu", wi)], chan=("wu", wi))
                return wi

            def load_u2(t0):
                P.add("sp", lambda e, t0=t0: e.dma_start(
                    out=u2[:, :, :], in_=u2src[:, t0:t0 + TT].rearrange("(k p) t -> p k t", p=128)),
                    writes=["u2"], chan="u2" + sfx)

            load_u2(0)
            for t0 in range(0, ntok, TT):
                P.add("sp", lambda e, t0=t0: e.dma_start(
                    out=x1[:, :, :], in_=x1src[t0:t0 + TT, :].rearrange("(s p) d -> p s d", p=128)),
                    writes=[("x1t", s4, c4) for s4 in range(nsub) for c4 in range(4)], chan="x1t" + sfx)
                for f4 in range(DFF // 512):
                    if (t0, f4) in wu_ready:
                        wi = wu_ready[(t0, f4)]
                    else:
                        wi = issue_wu(f4)
                    for ff in range(4):
                        fc = f4 * 4 + ff
                        pu = upc[0] % 3
                        zi = upc[0] % 2
                        upc[0] += 1
                        for k in range(KC):
                            P.add("pe", lambda e, pu=pu, wi=wi, k=k, ff=ff: e.matmul(
                                ps_up[pu][:, 0:TT], lhsT=wu[wi][:, k, ff * 128:(ff + 1) * 128], rhs=u2[:, k, :],
                                start=(k == 0), stop=(k == KC - 1)),
                                reads=[("wu", wi), "u2"], writes=[("ps_up", pu)])
                        P.add("act", lambda e, pu=pu, zi=zi, fc=fc: e.activation(
                            out=zt[zi][:, :], in_=ps_up[pu][:, 0:TT], func=AF.Relu, bias=bupT[:, fc:fc + 1]),
                            reads=[("ps_up", pu), "bupT"], writes=[("ps_up", pu), ("zt", zi)])
                        P.add("dve", lambda e, zi=zi, fc=fc: e.tensor_mul(out=hT[:, fc, :], in0=zt[zi][:, :], in1=zt[zi][:, :]),
                              reads=[("zt", zi)], writes=[("hT", fc)])
                if t0 + TT < ntok:
                    load_u2(t0 + TT)
                for c4 in range(4):
                    for fg in range(DFF // 1024):
                        wi = wdc[0] % 2
                        wdc[0] += 1
                        P.add("pool", lambda e, wi=wi, fg=fg, c4=c4: e.dma_start(
                            out=wd[wi][:, :, :],
                            in_=wdn_s[fg * 1024:(fg + 1) * 1024, c4 * 512:(c4 + 1) * 512].rearrange("(k p) c -> p k c", p=128)),
                            writes=[("wd", wi)], chan=("wd", wi))
                        for kk in range(8):
                            fc = fg * 8 + kk
                            for s4 in range(nsub):
                                P.add("pe", lambda e, wi=wi, kk=kk, fc=fc, s4=s4: e.matmul(
                                    ps_dn[s4][:, :], lhsT=hT[:, fc, s4 * 128:(s4 + 1) * 128], rhs=wd[wi][:, kk, :],
                                    start=(fc == 0), stop=(fc == DFF // 128 - 1)),
                                    reads=[("wd", wi), ("hT", fc)], writes=[("ps_dn", s4)])
                    cs = slice(c4 * 512, (c4 + 1) * 512)
                    for s4 in range(nsub):
                        ti = (c4 * nsub + s4) % 2
                        P.add("dve", lambda e, s4=s4, ti=ti, cs=cs: e.tensor_mul(
                            out=tmp[ti][:, :], in0=ps_dn[s4][:, :], in1=gbc[:, cs]),
                            reads=[("ps_dn", s4), "gmbc" + sfx], writes=[("ps_dn", s4), ("tmpd", ti)])
                        P.add("dve", lambda e, s4=s4, ti=ti, cs=cs: e.scalar_tensor_tensor(
                            out=x1[:, s4, cs], in0=x1[:, s4, cs], scalar=ALPHA, in1=tmp[ti][:, :],
                            op0=ALU.mult, op1=ALU.add),
                            reads=[("x1t", s4, c4), ("tmpd", ti)], writes=[("x1t", s4, c4)])
                if t0 + TT < ntok:
                    for f4 in (0, 1):
                        wu_ready[(t0 + TT, f4)] = issue_wu(f4)
                for s4 in range(nsub):
                    b = s4 % 2
                    allk = [("x1t", s4, c4) for c4 in range(4)]
                    for q4 in range(4):
                        P.add("dve", lambda e, b=b, q4=q4, s4=s4: e.bn_stats(out=bst[b][:, q4 * 6:(q4 + 1) * 6],
                                                                            in_=x1[:, s4, q4 * 512:(q4 + 1) * 512]),
                              reads=[("x1t", s4, q4)], writes=[("bstd", b, q4)])
                    P.add("dve", lambda e, b=b: e.bn_aggr(out=st[b][:, 0:2], in_=bst[b][:, :]),
                          reads=[("bstd", b, q4) for q4 in range(4)], writes=[("st0d", b)])
                    P.add("dve", lambda e, b=b: e.tensor_scalar_add(out=st[b][:, 4:5], in0=st[b][:, 1:2], scalar1=LN_EPS),
                          reads=[("st0d", b)], writes=[("st4d", b)])
                    P.add("act", lambda e, b=b: e.activation(out=st[b][:, 5:6], in_=st[b][:, 4:5], func=AF.Ln),
                          reads=[("st4d", b)], writes=[("st5d", b)])
                    P.add("act", lambda e, b=b: e.activation(out=st[b][:, 5:6], in_=st[b][:, 5:6], func=AF.Exp, scale=-0.5),
                          reads=[("st5d", b)], writes=[("st5d", b)])
                    P.add("dve", lambda e, b=b: e.scalar_tensor_tensor(
                        out=st[b][:, 6:7], in0=st[b][:, 0:1], scalar=-1.0, in1=st[b][:, 5:6], op0=ALU.mult, op1=ALU.mult),
                        reads=[("st0d", b), ("st5d", b)], writes=[("st6d", b)])
                    P.add("act", lambda e, b=b, s4=s4: e.activation(out=x1[:, s4, :], in_=x1[:, s4, :], func=AF.Identity,
                                                                   scale=st[b][:, 5:6], bias=st[b][:, 6:7]),
                          reads=allk + [("st5d", b), ("st6d", b)], writes=allk)
                    P.add("pool", lambda e, s4=s4: e.tensor_mul(out=x1[:, s4, :], in0=x1[:, s4, :], in1=lgbc[:, :]),
                          reads=allk + ["lgbc2" + sfx], writes=allk)
                    P.add("pool", lambda e, s4=s4: e.tensor_add(out=x1[:, s4, :], in0=x1[:, s4, :], in1=lbbc[:, :]),
                          reads=allk + ["lbbc2" + sfx], writes=allk)
                    r0 = t0 + s4 * 128
                    nr = min(128, nvalid)
                    P.add("sp", lambda e, s4=s4, r0=r0, nr=nr: e.dma_start(out=ydst[r0:r0 + nr, :], in_=x1[0:nr, s4, :]),
                          reads=allk, chan=("yo", s4))

        if "D" in phases:
            with ExitStack() as esD:
                phase_c2(esD, "", S, S, 0, u2T_s, x1_s, y_o)
                P.barrier()
            if with_sample:
                with ExitStack() as esD:
                    phase_c2(esD, "s", 128, NS, 1, u2Ts_s, x1s_s, ys_o)
                    P.barrier()

        P.barrier()
        P.add("sp", lambda e: None)
        P.emit()
    return nc


def _consts():
    ident = np.eye(128, dtype=np.float32)
    tri = np.triu(np.ones((128, 128), dtype=np.float32))
    sel = np.zeros((128, 128), dtype=np.float32)
    sel[64, :] = 1.0
    jx = np.ascontiguousarray(np.eye(128, dtype=np.float32)[::-1])
    k = np.arange(128)[:, None]
    i = np.arange(128)[None, :]
    mka = 1.0 - ((k >= 64) & (i < 64)).astype(np.float32)
    mkb = 1.0 - ((k < 64) & (i >= 64)).astype(np.float32)
    return dict(ident=ident, tri=tri, sel64=sel, jx=jx, mka=mka, mkb=mkb)


def _gtab(rel_bias):
    m = np.arange(768)
    idx = np.clip(127 - m, -128, 128) + 128
    return np.ascontiguousarray(rel_bias[:, idx], dtype=np.float32)


S_FULL = 4096
PAST_FULL = 4096
_NC_CACHE = {}


def kernel(x_prompt, x_sample, cache_fox_k, cache_fox_v, cache_fox_logf, cache_band_k, cache_band_v,
           c_prompt, c_sample, w_ada, b_ada, w_in, b_forget, rel_bias, w_out, ln_mix_g, ln_mix_b,
           w_up, b_up, w_down, ln_mlp_g, ln_mlp_b):
    f = lambda a: np.ascontiguousarray(np.asarray(a), dtype=np.float32)
    B, S, _ = x_prompt.shape
    PAST = cache_fox_k.shape[2]
    key = (S, PAST)
    if key not in _NC_CACHE:
        _NC_CACHE[key] = build(S, PAST)
    nc = _NC_CACHE[key]
    shared = dict(w_ada=f(w_ada[0]), b_ada=f(b_ada), w_in=f(w_in[0]), b_forget=f(b_forget),
                  gtab=_gtab(np.asarray(rel_bias[0])), w_out=f(w_out[0]), w_up=f(w_up[0]), b_up=f(b_up),
                  w_down=f(w_down[0]),
                  lnv=f(np.stack([ln_mix_g[0], ln_mix_b[0], ln_mlp_g[0], ln_mlp_b[0]])))
    shared.update(_consts())
    in_maps = []
    for b in range(B):
        m = dict(shared)
        m.update(x_p=f(x_prompt[b]), x_s=f(x_sample[b]), c2=f(np.stack([c_prompt[b], c_sample[b]])),
                 cfk=f(cache_fox_k[0, b]).reshape(PAST, HF * DH), cfv=f(cache_fox_v[0, b]).reshape(PAST, HF * DH),
                 cfl=f(cache_fox_logf[0, b]), cbk=f(cache_band_k[0, b]).reshape(-1, HB * DH),
                 cbv=f(cache_band_v[0, b]).reshape(-1, HB * DH))
        in_maps.append(m)
    res = run_bass_kernel_spmd(nc, in_maps, core_ids=list(range(B)))
    R = res.results
    WB = min(512, S)
    T = x_sample.shape[1]
    g = lambda name, shape: np.stack([np.asarray(R[b][name], dtype=np.float32).reshape(shape) for b in range(B)])
    return (g("y_p", (S, D)), g("y_s", (T, D)),
            g("fox_k_p", (S, HF, DH))[None], g("fox_v_p", (S, HF, DH))[None], g("fox_logf_p", (S, HF))[None],
            g("band_k_p", (WB, HB, DH))[None], g("band_v_p", (WB, HB, DH))[None],
            g("fox_k_s", (T, HF, DH))[None], g("fox_v_s", (T, HF, DH))[None], g("fox_logf_s", (T, HF))[None],
            g("band_k_s", (T, HB, DH))[None], g("band_v_s", (T, HB, DH))[None])
```
